# Optimizing a Trainium2 kernel written in Bass

```python
import jax, jax.numpy as jnp
from jax import lax
import numpy as np

D_MODEL = 4096
BATCH = 8
SEQ = 2048
DEPTH = 2

N_EVEN = (DEPTH + 1) // 2
N_ODD = DEPTH // 2

MIX_WIDTH = D_MODEL
RWKV_WIDTH = MIX_WIDTH // 2
RWKV_HEAD = 64
RWKV_HEADS = RWKV_WIDTH // RWKV_HEAD
DECAY_LORA = 128
ICLR_LORA = 128
SB_WIDTH = MIX_WIDTH - RWKV_WIDTH
SB_HEAD = 128
SB_HEADS = SB_WIDTH // SB_HEAD
SB_BLOCK = 128
SGU_WIDTH = MIX_WIDTH
SGU_CHUNK = 128
SGU_GROUPS = 16
SGU_GROUP_DIM = SGU_WIDTH // SGU_GROUPS

RMS_EPS = 1e-6
GN_EPS = 64e-5
LN_EPS = 1e-5
L2_EPS = 1e-12

RWKV_SHIFT_COLS = 3 * RWKV_WIDTH + DECAY_LORA + ICLR_LORA
EVEN_IN_COLS = RWKV_SHIFT_COLS + RWKV_WIDTH + 4 * SB_WIDTH
ODD_IN_COLS = 3 * SGU_WIDTH

kernel_name = 'hybrid_rwkv7_stickbreak_chunksgu'


def rmsnorm(x, g):
    xf = x.astype(jnp.float32)
    y = xf * lax.rsqrt(jnp.mean(xf * xf, axis=-1, keepdims=True) + RMS_EPS)
    return (y * g.astype(jnp.float32)).astype(x.dtype)


def token_shift(p):
    return jnp.pad(p[:, :-1], ((0, 0), (1, 0), (0, 0)))


def rwkv7_step(state, inp):
    r_t, w_t, k_t, v_t, kk_t, a_t = inp
    sa = jnp.einsum('bhvk,bhk->bhv', state, -kk_t)
    state = (state * w_t[:, :, None, :]
             + sa[..., None] * (kk_t * a_t)[:, :, None, :]
             + v_t[..., None] * k_t[:, :, None, :])
    y = jnp.einsum('bhvk,bhk->bhv', state, r_t)
    return state, y


def rwkv7_mix(p, w_dec_up, w0, a_up, a0, k_k, k_a, r_k, gn_g, gn_b):
    f32 = jnp.float32
    B, S, _ = p.shape
    r, k, v, w_lo, a_lo = jnp.split(
        p.astype(f32),
        [RWKV_WIDTH, 2 * RWKV_WIDTH, 3 * RWKV_WIDTH, 3 * RWKV_WIDTH + DECAY_LORA], axis=-1)
    w_log = -jax.nn.softplus(-(w0.astype(f32) + jnp.tanh(w_lo) @ w_dec_up.astype(f32))) - 0.5
    decay = jnp.exp(-jnp.exp(w_log))
    a = jax.nn.sigmoid(a0.astype(f32) + a_lo @ a_up.astype(f32))

    def heads(t):
        return t.reshape(B, S, RWKV_HEADS, RWKV_HEAD)

    kk = heads(k * k_k.astype(f32))
    kk = kk / jnp.maximum(jnp.sqrt(jnp.sum(kk * kk, axis=-1, keepdims=True)), L2_EPS)
    k = k * (1.0 + (a - 1.0) * k_a.astype(f32))
    r_h, k_h, v_h, w_h, a_h = heads(r), heads(k), heads(v), heads(decay), heads(a)

    xs = tuple(jnp.moveaxis(t, 1, 0) for t in (r_h, w_h, k_h, v_h, kk, a_h))
    state0 = jnp.zeros((B, RWKV_HEADS, RWKV_HEAD, RWKV_HEAD), f32)
    _, ys = lax.scan(rwkv7_step, state0, xs)
    y = jnp.moveaxis(ys, 0, 1)

    mu = jnp.mean(y, axis=-1, keepdims=True)
    var = jnp.mean(jnp.square(y - mu), axis=-1, keepdims=True)
    y = ((y - mu) * lax.rsqrt(var + GN_EPS)).reshape(B, S, RWKV_WIDTH)
    y = y * gn_g.astype(f32) + gn_b.astype(f32)
    bonus = jnp.sum(r_h * k_h * r_k.astype(f32).reshape(RWKV_HEADS, RWKV_HEAD),
                    axis=-1, keepdims=True) * v_h
    return y + bonus.reshape(B, S, RWKV_WIDTH)


def stick_breaking_attention(q, k, v):
    B, S, _ = q.shape

    def heads(t):
        return jnp.transpose(t.reshape(B, S, SB_HEADS, SB_HEAD), (0, 2, 1, 3))

    qh, kh, vh = heads(q), heads(k), heads(v)
    scale = 1.0 / np.sqrt(SB_HEAD)
    outs = []
    for blk in range(S // SB_BLOCK):
        q0 = blk * SB_BLOCK
        k_end = q0 + SB_BLOCK
        z = jnp.einsum('bhtd,bhsd->bhts', qh[:, :, q0:k_end], kh[:, :, :k_end]).astype(jnp.float32) * scale
        t_pos = q0 + jnp.arange(SB_BLOCK)[:, None]
        s_pos = jnp.arange(k_end)[None, :]
        causal = s_pos < t_pos
        log_keep = jnp.where(causal, jax.nn.log_sigmoid(-z), 0.0)
        later = lax.cumsum(log_keep, axis=3, reverse=True) - log_keep
        att = jnp.where(causal, jnp.exp(jax.nn.log_sigmoid(z) + later), 0.0)
        outs.append(jnp.einsum('bhts,bhsd->bhtd', att.astype(vh.dtype), vh[:, :, :k_end]))
    o = jnp.concatenate(outs, axis=2)
    return jnp.transpose(o, (0, 2, 1, 3)).reshape(B, S, SB_WIDTH)


def even_layer(h, w_in, shift_mu, w_dec_up, w0, a_up, a0, k_k, k_a, r_k, gn_g, gn_b, w_out):
    p = h @ w_in
    s1 = RWKV_SHIFT_COLS
    s2 = s1 + RWKV_WIDTH
    s3 = s2 + SB_WIDTH
    s4 = s3 + SB_WIDTH
    s5 = s4 + SB_WIDTH
    p_rwkv, g_rwkv, q, k, v, g_sb = jnp.split(p, [s1, s2, s3, s4, s5], axis=-1)
    p_rwkv = p_rwkv + shift_mu * (token_shift(p_rwkv) - p_rwkv)
    y_a = rwkv7_mix(p_rwkv, w_dec_up, w0, a_up, a0, k_k, k_a, r_k, gn_g, gn_b)
    y_a = y_a * jax.nn.silu(g_rwkv.astype(jnp.float32))
    y_b = stick_breaking_attention(q, k, v).astype(jnp.float32) * jax.nn.silu(g_sb.astype(jnp.float32))
    y = jnp.concatenate([y_a, y_b], axis=-1).astype(h.dtype)
    return (y @ w_out).astype(h.dtype)


def odd_layer(h, w_in, ln_g, ln_b, w_s, b_s, w_out):
    B, S, _ = h.shape
    u, v, g = jnp.split(h @ w_in, 3, axis=-1)
    u = jax.nn.gelu(u.astype(jnp.float32), approximate=False)
    v = jax.nn.gelu(v.astype(jnp.float32), approximate=False)
    mu = jnp.mean(v, axis=-1, keepdims=True)
    var = jnp.mean(jnp.square(v - mu), axis=-1, keepdims=True)
    v = (v - mu) * lax.rsqrt(var + LN_EPS) * ln_g.astype(jnp.float32) + ln_b.astype(jnp.float32)
    vc = v.reshape(B, S // SGU_CHUNK, SGU_CHUNK, SGU_GROUPS, SGU_GROUP_DIM)
    causal = jnp.tril(jnp.ones((SGU_CHUNK, SGU_CHUNK), dtype=bool))
    ws = jnp.where(causal[None], w_s.astype(jnp.float32), 0.0)
    mixed = jnp.einsum('gts,bcsgd->bctgd', ws, vc) + b_s.astype(jnp.float32).T[:, :, None]
    mixed = mixed.reshape(B, S, SGU_WIDTH)
    y = (u * mixed * jax.nn.silu(g.astype(jnp.float32))).astype(h.dtype)
    return (y @ w_out).astype(h.dtype)


def setup_inputs(seed: int = 0) -> dict:
    key = jax.random.key(seed)
    ks = jax.random.split(key, 24)
    f32 = jnp.float32
    W = RWKV_WIDTH

    def nrm(k, shape, s):
        return jax.random.normal(k, shape, f32) * s

    return {
        'x': jax.random.normal(ks[0], (BATCH, SEQ, D_MODEL), f32),
        'norm_g': 1.0 + nrm(ks[1], (DEPTH, D_MODEL), 0.02),
        'final_norm_g': 1.0 + nrm(ks[2], (D_MODEL,), 0.02),
        'e_w_in': nrm(ks[3], (N_EVEN, D_MODEL, EVEN_IN_COLS), D_MODEL ** -0.5),
        'e_shift_mu': jax.random.uniform(ks[4], (N_EVEN, RWKV_SHIFT_COLS), f32),
        'e_w_decay_up': nrm(ks[5], (N_EVEN, DECAY_LORA, W), 0.5 * DECAY_LORA ** -0.5),
        'e_w0': jax.random.uniform(ks[6], (N_EVEN, W), f32, -6.0, -1.0),
        'e_a_up': nrm(ks[7], (N_EVEN, ICLR_LORA, W), 0.5 * ICLR_LORA ** -0.5),
        'e_a0': nrm(ks[8], (N_EVEN, W), 0.1),
        'e_k_k': 0.85 + nrm(ks[9], (N_EVEN, W), 0.05),
        'e_k_a': 1.0 + nrm(ks[10], (N_EVEN, W), 0.05),
        'e_r_k': nrm(ks[11], (N_EVEN, W), 0.1),
        'e_gn_g': 1.0 + nrm(ks[12], (N_EVEN, W), 0.02),
        'e_gn_b': nrm(ks[13], (N_EVEN, W), 0.02),
        'e_w_out': nrm(ks[14], (N_EVEN, MIX_WIDTH, D_MODEL), MIX_WIDTH ** -0.5),
        'o_w_in': nrm(ks[15], (N_ODD, D_MODEL, ODD_IN_COLS), D_MODEL ** -0.5),
        'o_ln_g': 1.0 + nrm(ks[16], (N_ODD, SGU_WIDTH), 0.02),
        'o_ln_b': nrm(ks[17], (N_ODD, SGU_WIDTH), 0.02),
        'o_w_s': nrm(ks[18], (N_ODD, SGU_GROUPS, SGU_CHUNK, SGU_CHUNK), 0.5 * SGU_CHUNK ** -0.5),
        'o_b_s': 1.0 + nrm(ks[19], (N_ODD, SGU_GROUPS, SGU_CHUNK), 0.1),
        'o_w_out': nrm(ks[20], (N_ODD, SGU_WIDTH, D_MODEL), SGU_WIDTH ** -0.5),
    }


def reference(x, norm_g, final_norm_g, e_w_in, e_shift_mu, e_w_decay_up, e_w0, e_a_up, e_a0,
              e_k_k, e_k_a, e_r_k, e_gn_g, e_gn_b, e_w_out, o_w_in, o_ln_g, o_ln_b, o_w_s,
              o_b_s, o_w_out):
    for layer in range(DEPTH):
        h = rmsnorm(x, norm_g[layer])
        i = layer // 2
        if layer % 2 == 0:
            x = x + even_layer(h, e_w_in[i], e_shift_mu[i], e_w_decay_up[i], e_w0[i], e_a_up[i],
                               e_a0[i], e_k_k[i], e_k_a[i], e_r_k[i], e_gn_g[i], e_gn_b[i], e_w_out[i])
        else:
            x = x + odd_layer(h, o_w_in[i], o_ln_g[i], o_ln_b[i], o_w_s[i], o_b_s[i], o_w_out[i])
    return rmsnorm(x, final_norm_g)
```

```python
import numpy as np
from contextlib import ExitStack
import concourse.bass as bass
import concourse.mybir as mybir
from concourse.bass_utils import run_bass_kernel_spmd

F32 = mybir.dt.float32
BF16 = mybir.dt.bfloat16
AF = mybir.ActivationFunctionType
ALU = mybir.AluOpType
AX = mybir.AxisListType

D = 4096
T = 2048
NB = 8
RW = 2048
SBW = 2048
EC = 16640
OC = 12288
RMS_EPS = 1e-6
GN_EPS = 64e-5
LN_EPS = 1e-5
M0, M1, M2, M3, IDO, BDO, INDO, MBIG, RMASK = 0, 128, 256, 384, 512, 640, 768, 772, 1668
NCONST = 1668 + 2048
NSLOT = 4
XC = 1


class Sched:
    ENG = ("pe", "act", "dve", "pool", "sp")

    def __init__(self, nc, es, n_dma=12):
        self.nc = nc
        self.semobj = {}
        self.cnt = {}
        for e in self.ENG:
            self.semobj["s_" + e] = es.enter_context(nc.semaphore("s_" + e))
            self.cnt[e] = 0
        self.dq = {}
        for q in ("sp", "pool"):
            names = []
            for i in range(n_dma):
                nm = f"d_{q}{i}"
                self.semobj[nm] = es.enter_context(nc.semaphore(nm))
                names.append(nm)
            self.dq[q] = {"names": names, "val": [0] * n_dma, "next": 0}
        self.ops = {e: [] for e in self.ENG}
        self.wm = {e: {} for e in self.ENG}
        self.lastw = {}
        self.readers = {}

    def _deps(self, reads, writes, pwrites):
        deps = []
        for k in reads:
            deps.extend(self.lastw.get(k, {}).items())
        for k in writes:
            deps.extend(self.lastw.get(k, {}).items())
            deps.extend(self.readers.get(k, {}).items())
        for k in pwrites:
            deps.extend(self.readers.get(k, {}).items())
        return deps

    def _waits(self, eng, deps):
        need = {}
        for (s, v) in deps:
            if eng == "pe" and s == "s_pe":
                continue
            if self.wm[eng].get(s, 0) >= v:
                continue
            if need.get(s, 0) < v:
                need[s] = v
        for s, v in need.items():
            self.wm[eng][s] = v
        return list(need.items())

    def _commit(self, tok, reads, writes, pwrites):
        s, v = tok
        for k in writes:
            self.lastw[k] = {s: v}
            self.readers[k] = {}
        for k in pwrites:
            d = self.lastw.setdefault(k, {})
            d[s] = max(d.get(s, 0), v)
        for k in reads:
            d = self.readers.setdefault(k, {})
            d[s] = max(d.get(s, 0), v)

    def op(self, eng, fn, reads=(), writes=(), pwrites=()):
        reads, writes, pwrites = tuple(reads), tuple(writes), tuple(pwrites)
        waits = self._waits(eng, self._deps(reads, writes, pwrites))
        self.cnt[eng] += 1
        tok = ("s_" + eng, self.cnt[eng])
        self.ops[eng].append((waits, fn, "s_" + eng, 1))
        self._commit(tok, reads, writes, pwrites)
        return tok

    def dma(self, q, fn, reads=(), writes=(), pwrites=()):
        reads, writes, pwrites = tuple(reads), tuple(writes), tuple(pwrites)
        d = self.dq[q]
        i = d["next"]
        d["next"] = (i + 1) % len(d["names"])
        deps = self._deps(reads, writes, pwrites)
        if d["val"][i] > 0:
            deps.append((d["names"][i], d["val"][i]))
        waits = self._waits(q, deps)
        d["val"][i] += 16
        tok = (d["names"][i], d["val"][i])
        self.ops[q].append((waits, fn, d["names"][i], 16))
        self._commit(tok, reads, writes, pwrites)
        return tok

    def barrier(self):
        toks = self.all_tokens()
        for e in self.ENG:
            self.final_wait(e, toks)

    def final_wait(self, eng, toks):
        waits = self._waits(eng, list(toks))
        self.ops[eng].append((waits, None, None, 0))

    def all_tokens(self):
        toks = []
        for e in self.ENG:
            if self.cnt[e]:
                toks.append(("s_" + e, self.cnt[e]))
        for q in self.dq.values():
            for nm, v in zip(q["names"], q["val"]):
                if v:
                    toks.append((nm, v))
        return toks

    def emit(self):
        nc = self.nc
        engmap = {}

        def run(name, eng):
            for (waits, fn, sem, inc) in self.ops[name]:
                for (s, v) in waits:
                    eng.wait_ge(self.semobj[s], v)
                if fn is None:
                    continue
                ins = fn(eng)
                ins.then_inc(self.semobj[sem], inc)

        with nc.Block() as block:
            @block.tensor
            def _(e):
                run("pe", e)

            @block.scalar
            def _(e):
                run("act", e)

            @block.vector
            def _(e):
                run("dve", e)

            @block.gpsimd
            def _(e):
                run("pool", e)

            @block.sync
            def _(e):
                run("sp", e)


class Ctx:
    pass


def _rot(lst, state, name):
    i = state.get(name, 0)
    state[name] = (i + 1) % len(lst)
    return i


def norm_tokens(C, srcT, gcol, t0, ntok, hT=None, hkey=None, dstT=None):
    S = C.S
    nc = C.nc
    src_v = srcT.rearrange("(c p) t -> p c t", p=128)
    dst_v = dstT.rearrange("(c p) t -> p c t", p=128) if dstT is not None else None
    for tt in range(ntok // 512):
        tok = slice(t0 + tt * 512, t0 + (tt + 1) * 512)
        ps = C.psum[_rot(C.psum, C.rs, "psn")]
        pskey = ("ps", C.psum.index(ps))
        for c4 in range(32 // XC):
            xi = _rot(C.xst, C.rs, "xst")
            xb = C.xst[xi]
            S.dma("sp", lambda e, xb=xb, c4=c4, tok=tok: e.dma_start(out=xb[:, :, :], in_=src_v[:, c4 * XC:(c4 + 1) * XC, tok]),
                  reads=[("dram", srcT.name)], writes=[("xst", xi)])
            for cc in range(XC):
                c = c4 * XC + cc
                qi = _rot(C.sqb, C.rs, "sqb")
                qb = C.sqb[qi]
                S.op("act", lambda e, qb=qb, xb=xb, cc=cc: e.activation(out=qb[:, :], in_=xb[:, cc, :], func=AF.Square),
                     reads=[("xst", xi)], writes=[("sqb", qi)])
                S.op("pe", lambda e, ps=ps, qb=qb, c=c: e.matmul(ps[:, :], lhsT=C.ones_f[:, :], rhs=qb[:, :], start=(c == 0), stop=(c == 31)),
                     reads=[("sqb", qi)], writes=[pskey])
        S.op("act", lambda e, ps=ps: e.activation(out=C.rstd[:, :], in_=ps[:, :], func=AF.Sqrt, bias=C.eps_rms[:, 0:1], scale=1.0 / D),
             reads=[pskey], writes=["rstd"])
        S.op("dve", lambda e: e.reciprocal(out=C.rstd[:, :], in_=C.rstd[:, :]), reads=["rstd"], writes=["rstd"])
        for c4 in range(32 // XC):
            xi = _rot(C.xst, C.rs, "xst")
            xb = C.xst[xi]
            S.dma("sp", lambda e, xb=xb, c4=c4, tok=tok: e.dma_start(out=xb[:, :, :], in_=src_v[:, c4 * XC:(c4 + 1) * XC, tok]),
                  reads=[("dram", srcT.name)], writes=[("xst", xi)])
            if hT is not None:
                for cc in range(XC):
                    c = c4 * XC + cc
                    S.op("dve", lambda e, xb=xb, cc=cc, c=c, tt=tt: e.scalar_tensor_tensor(
                        out=hT[:, c, tt * 512:(tt + 1) * 512], in0=xb[:, cc, :], scalar=gcol[:, c:c + 1],
                        in1=C.rstd[:, :], op0=ALU.mult, op1=ALU.mult),
                        reads=[("xst", xi), "rstd"], pwrites=[hkey])
            else:
                for cc in range(XC):
                    c = c4 * XC + cc
                    S.op("dve", lambda e, xb=xb, cc=cc, c=c: e.scalar_tensor_tensor(
                        out=xb[:, cc, :], in0=xb[:, cc, :], scalar=gcol[:, c:c + 1],
                        in1=C.rstd[:, :], op0=ALU.mult, op1=ALU.mult),
                        reads=[("xst", xi), "rstd"], writes=[("xst", xi)])
                S.dma("sp", lambda e, xb=xb, c4=c4, tok=tok: e.dma_start(out=dst_v[:, c4 * XC:(c4 + 1) * XC, tok], in_=xb[:, :, :]),
                      reads=[("xst", xi)], pwrites=[("dram", dstT.name)])


def project_gen(C, actT, akey, w, groups, t0):
    S = C.S
    wv = w.rearrange("(kc p) n -> p kc n", p=128)
    for g in groups:
        dst = g["dst"]
        dkey = ("dram", dst.name)
        for ct in range(0, g["n"], 512):
            ncol = min(512, g["n"] - ct)
            col0 = g["c0"] + ct
            wi = _rot(C.wbuf, C.rs, "wbuf")
            wb = C.wbuf[wi]
            for hk in range(2):
                S.dma("pool", lambda e, wb=wb, col0=col0, ncol=ncol, hk=hk: e.dma_start(
                    out=wb[:, hk * 16:(hk + 1) * 16, 0:ncol], in_=wv[:, hk * 16:(hk + 1) * 16, col0:col0 + ncol]),
                    reads=[("dram", w.name)], writes=[("wbuf", wi, hk)])
            wkeys = [("wbuf", wi, 0), ("wbuf", wi, 1)]
            if g["mode"] == "F":
                for cc in range(ncol // 128):
                    st_list = C.stF if g["dt"] == F32 else C.stB
                    sname = "stF" if g["dt"] == F32 else "stB"
                    si = _rot(st_list, C.rs, sname)
                    st = st_list[si]
                    skey = (sname, si)
                    row0 = ct + cc * 128
                    if g.get("resid") is not None:
                        res = g["resid"]
                        S.dma("sp", lambda e, st=st, res=res, row0=row0: e.dma_start(
                            out=st[:, :], in_=res[row0:row0 + 128, t0:t0 + 1024]),
                            reads=[("dram", res.name)], writes=[skey])
                    for tt in range(2):
                        pi = C.psm_banks[_rot(C.psm_banks, C.rs, "psm%d" % len(C.psm_banks))]
                        ps = C.psum[pi]

                        def mm(e, ps=ps, wb=wb, cc=cc, tt=tt):
                            ins = None
                            for kc in range(32):
                                ins = e.matmul(ps[:, :], lhsT=wb[:, kc, cc * 128:(cc + 1) * 128],
                                               rhs=actT[:, kc, tt * 512:(tt + 1) * 512],
                                               start=(kc == 0), stop=(kc == 31))
                            return ins
                        S.op("pe", mm, reads=wkeys + [akey], writes=[("ps", pi)])
                        yield
                        if g.get("resid") is not None:
                            S.op("dve", lambda e, ps=ps, st=st, tt=tt: e.tensor_tensor(
                                out=st[:, tt * 512:(tt + 1) * 512], in0=ps[:, :], in1=st[:, tt * 512:(tt + 1) * 512], op=ALU.add),
                                reads=[("ps", pi), skey], pwrites=[skey])
                        else:
                            ev = "act" if _rot([0, 1], C.rs, "evsel") == 0 else "dve"
                            if ev == "act":
                                S.op("act", lambda e, ps=ps, st=st, tt=tt: e.copy(out=st[:, tt * 512:(tt + 1) * 512], in_=ps[:, :]),
                                     reads=[("ps", pi)], pwrites=[skey])
                            else:
                                S.op("dve", lambda e, ps=ps, st=st, tt=tt: e.tensor_copy(out=st[:, tt * 512:(tt + 1) * 512], in_=ps[:, :]),
                                     reads=[("ps", pi)], pwrites=[skey])
                    S.dma("sp", lambda e, st=st, dst=dst, row0=row0: e.dma_start(
                        out=dst[row0:row0 + 128, t0:t0 + 1024], in_=st[:, :]),
                        reads=[skey], pwrites=[dkey])
            else:
                for tb in range(8):
                    st_list = C.stF if g["dt"] == F32 else C.stB
                    sname = "stF" if g["dt"] == F32 else "stB"
                    si = _rot(st_list, C.rs, sname)
                    st = st_list[si]
                    skey = (sname, si)
                    pi = C.psm_banks[_rot(C.psm_banks, C.rs, "psm%d" % len(C.psm_banks))]
                    ps = C.psum[pi]

                    def mm(e, ps=ps, wb=wb, tb=tb, ncol=ncol):
                        ins = None
                        for kc in range(32):
                            ins = e.matmul(ps[:, 0:ncol], lhsT=actT[:, kc, tb * 128:(tb + 1) * 128],
                                           rhs=wb[:, kc, 0:ncol], start=(kc == 0), stop=(kc == 31))
                        return ins
                    S.op("pe", mm, reads=wkeys + [akey], writes=[("ps", pi)])
                    yield
                    ev = "act" if _rot([0, 1], C.rs, "evsel") == 0 else "dve"
                    if ev == "act":
                        S.op("act", lambda e, ps=ps, st=st, ncol=ncol: e.copy(out=st[:, 0:ncol], in_=ps[:, 0:ncol]),
                             reads=[("ps", pi)], writes=[skey])
                    else:
                        S.op("dve", lambda e, ps=ps, st=st, ncol=ncol: e.tensor_copy(out=st[:, 0:ncol], in_=ps[:, 0:ncol]),
                             reads=[("ps", pi)], writes=[skey])
                    ro = g.get("row_off", 0)
                    S.dma("sp", lambda e, st=st, dst=dst, tb=tb, ct=ct, ncol=ncol, ro=ro: e.dma_start(
                        out=dst[ro + t0 + tb * 128:ro + t0 + (tb + 1) * 128, ct:ct + ncol], in_=st[:, 0:ncol]),
                        reads=[skey], pwrites=[dkey])


def project(C, actT, akey, w, groups, t0):
    for _ in project_gen(C, actT, akey, w, groups, t0):
        pass


def load_actT(C, srcT, t0):
    S = C.S
    v = srcT.rearrange("(c p) t -> p c t", p=128)
    for c8 in range(4):
        S.dma("sp", lambda e, c8=c8: e.dma_start(out=C.hT[:, c8 * 8:(c8 + 1) * 8, :], in_=v[:, c8 * 8:(c8 + 1) * 8, t0:t0 + 1024]),
              reads=[("dram", srcT.name)], pwrites=["hT"])


def phase_rwkv(C):
    S, nc, I, R = C.S, C.nc, C.I, C.R
    rs_ = C.rs
    NEG_E = -float(np.exp(-0.5))
    with ExitStack() as es:
        def sb(name, shape, dt=F32):
            return es.enter_context(nc.sbuf_tensor("rk_" + name, list(shape), dt))
        vec = sb("vec", [128, 10, 16])
        wdu = sb("wdu", [128, RW])
        aup = sb("aup", [128, RW])
        bd = sb("bd", [128, 128])
        ind = sb("ind", [128, 2])
        identf = sb("identf", [128, 128])
        identb = sb("identb", [128, 128], BF16)
        rmask = sb("rmask", [128, T])
        mk4 = sb("mk4", [128, 512])
        m04 = sb("m04", [128, 4, 128])
        twlo = sb("twlo", [128, T])
        alo = sb("alo", [128, T])
        gneps = sb("gneps", [128, 1])
        G = [sb(f"G{i}", [128, T]) for i in range(8)]
        bt = sb("bt", [128, T], BF16)
        kt = sb("kt", [128, T], BF16)
        QT = sb("QT", [128, 16, 2, 128], BF16)
        Btok = sb("Btok", [128, 16, 128], BF16)
        Ktok = sb("Ktok", [128, 16, 128], BF16)
        Vb = sb("Vb", [128, 16, 128], BF16)
        ATall = sb("ATall", [128, 32, 512], BF16)
        MTall = sb("MTall", [128, 32, 128], BF16)
        Xg = [[sb(f"Xg{s}{i}", [128, 4, 128], BF16) for i in range(2)] for s in range(NSLOT)]
        XTg = [[sb(f"XTg{s}{i}", [128, 4, 128], BF16) for i in range(2)] for s in range(NSLOT)]
        gl = sb("gl", [128, 16])
        bon = sb("bon", [128, 32])
        st1 = sb("st1", [128, 32])
        st2 = sb("st2", [128, 32])
        muv = sb("muv", [128, 128])
        gng = sb("gng", [128, 128])
        gnb = sb("gnb", [128, 128])
        Hf = sb("Hf", [128, 128])
        Hb = sb("Hb", [128, 128], BF16)
        th = sb("th", [128, 128])
        RHSb = [sb(f"RHSb{i}", [128, 128], BF16) for i in range(2)]
        Ub = [sb(f"Ub{i}", [128, 128], BF16) for i in range(2)]

        def ld(dst, src, key):
            S.dma("sp", lambda e: e.dma_start(out=dst, in_=src), writes=[key])
        ld(vec[:, :, :], I.e_vec.rearrange("a p j -> p a j"), "vec")
        ld(wdu[:, :], I.e_wdu[:, :], "wdu")
        ld(aup[:, :], I.e_aup[:, :], "aup")
        ld(bd[:, :], I.consts[:, BDO:BDO + 128], "bd")
        ld(ind[:, :], I.consts[:, INDO:INDO + 2], "ind")
        ld(identf[:, :], I.consts[:, IDO:IDO + 128], "identf")
        ld(rmask[:, :], I.consts[:, RMASK:RMASK + T], "rmask")
        for q in range(4):
            S.dma("sp", lambda e, q=q: e.dma_start(out=m04[:, q, :], in_=I.consts[:, M0:M0 + 128]), pwrites=["m04"])
            mo = M1 if q % 2 == 0 else M2
            S.dma("sp", lambda e, q=q, mo=mo: e.dma_start(out=mk4[:, q * 128:(q + 1) * 128], in_=I.consts[:, mo:mo + 128]), pwrites=["mk4"])
        S.op("dve", lambda e: e.tensor_copy(out=identb[:, :], in_=identf[:, :]), reads=["identf"], writes=["identb"])
        S.op("pool", lambda e: e.memset(G[7][0:1, :], 0.0), writes=["G7"])
        S.op("pool", lambda e: e.memset(gneps[:, :], GN_EPS), writes=["gneps"])
        S.op("pool", lambda e: e.memset(th[:, :], 0.0), writes=["th"])
        S.op("dve", lambda e: e.tensor_copy(out=C.psum[7][:, 0:128], in_=th[:, :]), reads=["th"], writes=[("ps", 7)])
        S.dma("sp", lambda e: e.dma_start(out=R.vtok[0:1, :], in_=G[7][0:1, :]), reads=["G7"], pwrites=[("dram", R.vtok.name)])

        def lerpF(src, skey, mu_col, tmp, tkey):
            S.op("dve", lambda e: e.tensor_tensor(out=tmp[:, 1:T], in0=src[:, 0:T - 1], in1=src[:, 1:T], op=ALU.subtract),
                 reads=[skey], pwrites=[tkey])
            S.op("dve", lambda e: e.tensor_scalar(out=tmp[:, 0:1], in0=src[:, 0:1], scalar1=-1.0, scalar2=None, op0=ALU.mult),
                 reads=[skey], pwrites=[tkey])
            S.op("dve", lambda e: e.scalar_tensor_tensor(out=src[:, :], in0=tmp[:, :], scalar=mu_col, in1=src[:, :],
                                                         op0=ALU.mult, op1=ALU.add),
                 reads=[skey, tkey, "vec"], writes=[skey])

        ld(twlo[:, :], R.loT[0:128, :], "twlo")
        ld(alo[:, :], R.loT[128:256, :], "alo")
        lerpF(twlo, "twlo", vec[:, 7, 0:1], G[2], "G2")
        lerpF(alo, "alo", vec[:, 8, 0:1], G[2], "G2")
        S.op("act", lambda e: e.activation(out=twlo[:, :], in_=twlo[:, :], func=AF.Tanh), reads=["twlo"], writes=["twlo"])

        def psr():
            i = 2 + _rot(list(range(5)), rs_, "rkps")
            return i, C.psum[i]

        def g3(t):
            return t[:, :].rearrange("p (c f) -> p c f", c=16)

        def g32(t):
            return t[:, :].rearrange("p (c f) -> p c f", c=32)

        def pair(j):
            jc = slice(j * 128, (j + 1) * 128)
            r_, k_, tmp, lw, a_, kkn, c_, ex = G
            S.dma("sp", lambda e: e.dma_start(out=r_[:, :], in_=R.rT[jc, :]), reads=[("dram", R.rT.name)], writes=["G0"])
            S.dma("sp", lambda e: e.dma_start(out=k_[:, :], in_=R.kT[jc, :]), reads=[("dram", R.kT.name)], writes=["G1"])
            lerpF(r_, "G0", vec[:, 5, j:j + 1], tmp, "G2")
            lerpF(k_, "G1", vec[:, 6, j:j + 1], tmp, "G2")
            for tc in range(4):
                ts_ = slice(tc * 512, (tc + 1) * 512)
                pi, ps = psr()
                S.op("pe", lambda e, ps=ps, ts_=ts_: e.matmul(ps[:, :], lhsT=wdu[:, jc], rhs=twlo[:, ts_], start=True, stop=True),
                     reads=["wdu", "twlo"], writes=[("ps", pi)])
                S.op("act", lambda e, ps=ps, ts_=ts_: e.activation(out=lw[:, ts_], in_=ps[:, :], func=AF.Sigmoid, bias=vec[:, 0, j:j + 1], scale=1.0),
                     reads=[("ps", pi), "vec"], pwrites=["G3"])
                pi, ps = psr()
                S.op("pe", lambda e, ps=ps, ts_=ts_: e.matmul(ps[:, :], lhsT=aup[:, jc], rhs=alo[:, ts_], start=True, stop=True),
                     reads=["aup", "alo"], writes=[("ps", pi)])
                S.op("act", lambda e, ps=ps, ts_=ts_: e.activation(out=a_[:, ts_], in_=ps[:, :], func=AF.Sigmoid, bias=vec[:, 1, j:j + 1], scale=1.0),
                     reads=[("ps", pi), "vec"], pwrites=["G4"])
            S.op("act", lambda e: e.activation(out=tmp[:, :], in_=k_[:, :], func=AF.Identity, scale=vec[:, 2, j:j + 1]),
                 reads=["G1", "vec"], writes=["G2"])
            S.op("act", lambda e: e.activation(out=ex[:, :], in_=tmp[:, :], func=AF.Square), reads=["G2"], writes=["G7"])
            for tc in range(4):
                ts_ = slice(tc * 512, (tc + 1) * 512)
                pi, ps = psr()
                S.op("pe", lambda e, ps=ps, ts_=ts_: e.matmul(ps[:, :], lhsT=bd[:, :], rhs=ex[:, ts_], start=True, stop=True),
                     reads=["bd", "G7"], writes=[("ps", pi)])
                S.op("act", lambda e, ps=ps, ts_=ts_: e.activation(out=c_[:, ts_], in_=ps[:, :], func=AF.Sqrt),
                     reads=[("ps", pi)], pwrites=["G6"])
            S.op("dve", lambda e: e.tensor_scalar(out=c_[:, :], in0=c_[:, :], scalar1=1e-12, scalar2=None, op0=ALU.max),
                 reads=["G6"], writes=["G6"])
            S.op("dve", lambda e: e.reciprocal(out=c_[:, :], in_=c_[:, :]), reads=["G6"], writes=["G6"])
            S.op("dve", lambda e: e.tensor_tensor(out=kkn[:, :], in0=tmp[:, :], in1=c_[:, :], op=ALU.mult),
                 reads=["G2", "G6"], writes=["G5"])
            S.op("dve", lambda e: e.tensor_scalar(out=tmp[:, :], in0=a_[:, :], scalar1=-1.0, scalar2=vec[:, 3, j:j + 1], op0=ALU.add, op1=ALU.mult),
                 reads=["G4", "vec"], writes=["G2"])
            S.op("dve", lambda e: e.scalar_tensor_tensor(out=k_[:, :], in0=tmp[:, :], scalar=1.0, in1=k_[:, :], op0=ALU.add, op1=ALU.mult),
                 reads=["G2", "G1"], writes=["G1"])
            S.op("dve", lambda e: e.scalar_tensor_tensor(out=tmp[:, :], in0=r_[:, :], scalar=vec[:, 4, j:j + 1], in1=k_[:, :], op0=ALU.mult, op1=ALU.mult),
                 reads=["G0", "G1", "vec"], writes=["G2"])
            pib, psb = psr()

            def mmb(e, psb=psb):
                ins = None
                for c in range(16):
                    ins = e.matmul(psb[:, 2 * c:2 * c + 2], lhsT=tmp[:, c * 128:(c + 1) * 128], rhs=ind[:, :], start=True, stop=True)
                return ins
            S.op("pe", mmb, reads=["G2", "ind"], writes=[("ps", pib)])
            S.op("act", lambda e, psb=psb: e.copy(out=bon[:, :], in_=psb[:, 0:32]), reads=[("ps", pib)], writes=["bon"])
            S.op("dve", lambda e: e.tensor_tensor(out=a_[:, :], in0=kkn[:, :], in1=a_[:, :], op=ALU.mult), reads=["G5", "G4"], writes=["G4"])
            S.op("dve", lambda e: e.tensor_tensor_scan(out=c_[:, :], data0=rmask[:, :], data1=lw[:, :], initial=0.0, op0=ALU.mult, op1=ALU.add),
                 reads=["rmask", "G3"], writes=["G6"])
            S.op("dve", lambda e: e.tensor_tensor(out=lw[:, :], in0=c_[:, :], in1=lw[:, :], op=ALU.subtract), reads=["G6", "G3"], writes=["G3"])
            S.op("act", lambda e: e.activation(out=ex[:, :], in_=c_[:, :], func=AF.Exp, scale=NEG_E), reads=["G6"], writes=["G7"])
            S.op("dve", lambda e: e.tensor_tensor(out=QT[:, :, 1, :], in0=g3(r_), in1=g3(ex), op=ALU.mult), reads=["G0", "G7"], pwrites=["QT"])
            S.op("dve", lambda e: e.tensor_copy(out=gl[:, :], in_=g3(ex)[:, :, 127]), reads=["G7"], writes=["gl"])
            S.op("act", lambda e: e.activation(out=ex[:, :], in_=lw[:, :], func=AF.Exp, scale=NEG_E), reads=["G3"], writes=["G7"])
            S.op("dve", lambda e: e.scalar_tensor_tensor(out=QT[:, :, 0, :], in0=g3(kkn), scalar=-1.0, in1=g3(ex), op0=ALU.mult, op1=ALU.mult),
                 reads=["G5", "G7"], pwrites=["QT"])
            S.op("act", lambda e: e.activation(out=ex[:, :], in_=c_[:, :], func=AF.Exp, scale=-NEG_E), reads=["G6"], writes=["G7"])
            S.op("dve", lambda e: e.tensor_tensor(out=bt[:, :], in0=a_[:, :], in1=ex[:, :], op=ALU.mult), reads=["G4", "G7"], writes=["bt"])
            S.op("dve", lambda e: e.tensor_tensor(out=kt[:, :], in0=k_[:, :], in1=ex[:, :], op=ALU.mult), reads=["G1", "G7"], writes=["kt"])
            if C.rk_stop == 1:
                return
            for (srcT_, skey, dstk, dkey) in ((bt, "bt", Btok, "Btok"), (kt, "kt", Ktok, "Ktok")):
                for hf in range(2):
                    pi, ps = psr()
                    psv = ps[:, :].bitcast(BF16)

                    def tr(e, psv=psv, srcT_=srcT_, hf=hf):
                        ins = None
                        for cc in range(8):
                            c = hf * 8 + cc
                            ins = e.transpose(out=psv[:, cc * 128:(cc + 1) * 128], in_=srcT_[:, c * 128:(c + 1) * 128], identity=identb[:, :])
                        return ins
                    S.op("pe", tr, reads=[skey, "identb"], writes=[("ps", pi)])
                    S.op("act", lambda e, psv=psv, dstk=dstk, hf=hf: e.copy(
                        out=dstk[:, hf * 8:(hf + 1) * 8, :].rearrange("p c f -> p (c f)"), in_=psv[:, :]),
                        reads=[("ps", pi)], pwrites=[dkey])
            vcur, vprev, sg = G[0], G[1], G[2]
            S.dma("sp", lambda e: e.dma_start(out=g3(vcur), in_=R.vtok[1:T + 1, jc].rearrange("(c p) f -> p c f", p=128)),
                  reads=[("dram", R.vtok.name)], writes=["G0"])
            S.dma("sp", lambda e: e.dma_start(out=g3(vprev), in_=R.vtok[0:T, jc].rearrange("(c p) f -> p c f", p=128)),
                  reads=[("dram", R.vtok.name)], writes=["G1"])
            S.dma("sp", lambda e: e.dma_start(out=g3(sg), in_=R.gtok[:, jc].rearrange("(c p) f -> p c f", p=128)),
                  reads=[("dram", R.gtok.name)], writes=["G2"])
            S.dma("sp", lambda e: e.dma_start(out=muv[:, :], in_=I.e_mu_v[j:j + 1, 0:128].partition_broadcast(128)), writes=["muv"])
            S.dma("sp", lambda e: e.dma_start(out=gng[:, :], in_=I.e_gn[0, j:j + 1, 0:128].partition_broadcast(128)), writes=["gng"])
            S.dma("sp", lambda e: e.dma_start(out=gnb[:, :], in_=I.e_gn[1, j:j + 1, 0:128].partition_broadcast(128)), writes=["gnb"])
            bc16 = lambda t: t[:, :].unsqueeze(1).broadcast_to([128, 16, 128])
            S.op("pool", lambda e: e.tensor_tensor(out=vprev[:, :], in0=vprev[:, :], in1=vcur[:, :], op=ALU.subtract), reads=["G0", "G1"], writes=["G1"])
            S.op("pool", lambda e: e.tensor_tensor(out=g3(vprev), in0=g3(vprev), in1=bc16(muv), op=ALU.mult), reads=["G1", "muv"], writes=["G1"])
            S.op("pool", lambda e: e.tensor_tensor(out=vcur[:, :], in0=vcur[:, :], in1=vprev[:, :], op=ALU.add), reads=["G0", "G1"], writes=["G0"])
            S.op("pool", lambda e: e.tensor_copy(out=Vb[:, :, :], in_=g3(vcur)), reads=["G0"], writes=["Vb"])
            S.op("act", lambda e: e.activation(out=sg[:, :], in_=sg[:, :], func=AF.Silu), reads=["G2"], writes=["G2"])

            if C.rk_stop == 2:
                return
            def group(cg, slot):
                p0 = cg * 4
                X, XT = Xg[slot], XTg[slot]
                psBs = [psr(), psr()]
                for q in range(4):
                    hp = q // 2
                    c = cg * 2 + q % 2
                    rows = slice(64 * hp, 64 * hp + 64)
                    cs_ = slice(c * 128, (c + 1) * 128)
                    pia = hp
                    psA = C.psum[pia]
                    pib, psB = psBs[hp]
                    cl = q % 2

                    def mma(e, psA=psA, rows=rows, cs_=cs_, c=c):
                        qv = QT[rows, c, :, :].rearrange("p a t -> p (a t)")
                        e.matmul(psA[:, 0:256], lhsT=bt[rows, cs_], rhs=qv, start=True, stop=True)
                        return e.matmul(psA[:, 256:512], lhsT=kt[rows, cs_], rhs=qv, start=True, stop=True)
                    S.op("pe", mma, reads=["bt", "kt", "QT"], writes=[("ps", pia)])
                    S.op("dve", lambda e, psA=psA, q=q: e.tensor_tensor(out=ATall[:, p0 + q, :], in0=psA[:, :], in1=mk4[:, :], op=ALU.mult),
                         reads=[("ps", pia), "mk4"], pwrites=[("ATall", cg)])
                    S.op("pe", lambda e, psB=psB, rows=rows, cs_=cs_, c=c, cl=cl: e.matmul(
                        psB[:, cl * 128:(cl + 1) * 128], lhsT=QT[rows, c, 0, :], rhs=bt[rows, cs_], start=True, stop=True),
                        reads=["QT", "bt"], pwrites=[("ps", pib)])
                for hp in range(2):
                    pib, psB = psBs[hp]
                    S.op("dve", lambda e, psB=psB, hp=hp: e.tensor_tensor(
                        out=X[0][:, 2 * hp:2 * hp + 2, :].rearrange("p q f -> p (q f)"), in0=psB[:, 0:256],
                        in1=m04[:, 0:2, :].rearrange("p q f -> p (q f)"), op=ALU.mult),
                        reads=[("ps", pib), "m04"], pwrites=[("Xg", slot, 0)])
                S.op("dve", lambda e: e.tensor_tensor(out=MTall[:, p0:p0 + 4, :], in0=ATall[:, p0:p0 + 4, 0:128],
                                                      in1=identb[:, :].unsqueeze(1).broadcast_to([128, 4, 128]), op=ALU.add),
                     reads=[("ATall", cg), "identb"], writes=[("MTall", cg)])
                yield
                xi = 0
                xtb = None
                xtkey = ("ATall", cg)

                def xt_ap(level_buf, q):
                    if level_buf is None:
                        return ATall[:, p0 + q, 0:128]
                    return XT[level_buf][:, q, :]
                for lvl in range(1, 7):
                    nxi = 1 - xi
                    piX, psX = psr()

                    def mmx(e, psX=psX, xi=xi, xtb=xtb):
                        ins = None
                        for q in range(4):
                            ins = e.matmul(psX[:, q * 128:(q + 1) * 128], lhsT=xt_ap(xtb, q), rhs=X[xi][:, q, :], start=True, stop=True)
                        return ins
                    S.op("pe", mmx, reads=[xtkey, ("Xg", slot, xi)], writes=[("ps", piX)])
                    if lvl < 6:
                        piT, psT = psr()
                        nxt = 0 if xtb is None else 1 - xtb

                        def mmt(e, psT=psT, xi=xi, xtb=xtb):
                            ins = None
                            for q in range(4):
                                ins = e.matmul(psT[:, q * 128:(q + 1) * 128], lhsT=X[xi][:, q, :], rhs=xt_ap(xtb, q), start=True, stop=True)
                            return ins
                        S.op("pe", mmt, reads=[xtkey, ("Xg", slot, xi)], writes=[("ps", piT)])
                    S.op("act", lambda e, psX=psX, nxi=nxi: e.copy(out=X[nxi][:, :, :].rearrange("p q f -> p (q f)"), in_=psX[:, :]),
                         reads=[("ps", piX)], writes=[("Xg", slot, nxi)])
                    if lvl < 6:
                        S.op("act", lambda e, psT=psT, nxt=nxt: e.copy(out=XT[nxt][:, :, :].rearrange("p q f -> p (q f)"), in_=psT[:, :]),
                             reads=[("ps", piT)], writes=[("XTg", slot, nxt)])
                        xtb = nxt
                        xtkey = ("XTg", slot, nxt)
                    xi = nxi
                    piD, psD = psr()

                    def mmd(e, psD=psD, xi=xi):
                        ins = None
                        for q in range(4):
                            ins = e.matmul(psD[:, q * 128:(q + 1) * 128], lhsT=X[xi][:, q, :], rhs=MTall[:, p0 + q, :], start=True, stop=True)
                        return ins
                    S.op("pe", mmd, reads=[("Xg", slot, xi), ("MTall", cg)], writes=[("ps", piD)])
                    S.op("dve", lambda e, psD=psD: e.tensor_tensor(out=MTall[:, p0:p0 + 4, :].rearrange("p q f -> p (q f)"),
                                                                   in0=MTall[:, p0:p0 + 4, :].rearrange("p q f -> p (q f)"), in1=psD[:, :], op=ALU.add),
                         reads=[("MTall", cg), ("ps", piD)], writes=[("MTall", cg)])
                    yield

            if C.rk_stop == 3:
                return
            for gp_ in range(8 // NSLOT):
                gens = [group(NSLOT * gp_ + s_, s_) for s_ in range(NSLOT)]
                for _step in range(7):
                    for g_ in gens:
                        next(g_)
            if C.rk_stop == 4:
                return
            S.op("pool", lambda e: e.memset(Hf[:, :], 0.0), writes=["Hf"])
            S.op("pool", lambda e: e.memset(Hb[:, :], 0.0), writes=["Hb"])
            Yall = G[1]
            psH = C.psum[7]
            for c in range(16):
                cg = c // 2
                bi = c % 2
                ix = [(c // 2) * 4 + hp * 2 + (c % 2) for hp in range(2)]
                piR, psR = psr()

                def mmr(e, psR=psR, c=c, ix=ix):
                    ins = e.matmul(psR[:, 0:128], lhsT=QT[:, c, 0, :], rhs=Hb[:, :], start=True, stop=False)
                    for hp in range(2):
                        vs = slice(64 * hp, 64 * hp + 64)
                        ins = e.matmul(psR[:, vs], lhsT=ATall[:, ix[hp], 256:384], rhs=Vb[:, c, vs], start=False, stop=(hp == 1))
                    return ins
                S.op("pe", mmr, reads=["QT", "Hb", ("ATall", cg), "Vb"], writes=[("ps", piR)])
                S.op("act", lambda e, psR=psR, bi=bi: e.copy(out=RHSb[bi][:, :], in_=psR[:, 0:128]), reads=[("ps", piR)], writes=[("RHSb", bi)])
                piU, psU = psr()

                def mmu(e, psU=psU, c=c, bi=bi, ix=ix):
                    ins = None
                    for hp in range(2):
                        vs = slice(64 * hp, 64 * hp + 64)
                        ins = e.matmul(psU[:, vs], lhsT=MTall[:, ix[hp], :], rhs=RHSb[bi][:, vs], start=True, stop=True)
                    return ins
                S.op("pe", mmu, reads=[("MTall", cg), ("RHSb", bi)], writes=[("ps", piU)])
                S.op("dve", lambda e, psU=psU, bi=bi: e.tensor_copy(out=Ub[bi][:, :], in_=psU[:, 0:128]), reads=[("ps", piU)], writes=[("Ub", bi)])

                def mmh(e, c=c, bi=bi):
                    ins = None
                    for hp in range(2):
                        vs = slice(64 * hp, 64 * hp + 64)
                        e.matmul(psH[vs, vs], lhsT=Btok[:, c, vs], rhs=Ub[bi][:, vs], start=True, stop=False)
                        ins = e.matmul(psH[vs, vs], lhsT=Ktok[:, c, vs], rhs=Vb[:, c, vs], start=False, stop=True)
                    return ins
                S.op("pe", mmh, reads=["Btok", "Ktok", "Vb", ("Ub", bi)], writes=[("ps", 7)])
                piY, psY = psr()

                def mmy(e, psY=psY, c=c, bi=bi, ix=ix):
                    ins = e.matmul(psY[:, 0:128], lhsT=QT[:, c, 1, :], rhs=Hb[:, :], start=True, stop=False)
                    for hp in range(2):
                        vs = slice(64 * hp, 64 * hp + 64)
                        e.matmul(psY[:, vs], lhsT=ATall[:, ix[hp], 128:256], rhs=Ub[bi][:, vs], start=False, stop=False)
                        ins = e.matmul(psY[:, vs], lhsT=ATall[:, ix[hp], 384:512], rhs=Vb[:, c, vs], start=False, stop=(hp == 1))
                    return ins
                S.op("pe", mmy, reads=["QT", "Hb", ("ATall", cg), "Vb", ("Ub", bi)], writes=[("ps", piY)])
                S.op("act", lambda e, psY=psY, c=c: e.copy(out=Yall[:, c * 128:(c + 1) * 128], in_=psY[:, 0:128]),
                     reads=[("ps", piY)], pwrites=["G1"])
                S.op("dve", lambda e: e.tensor_tensor(out=th[:, :], in0=psH[:, 0:128], in1=Hf[:, :], op=ALU.add),
                     reads=[("ps", 7), "Hf"], writes=["th"])
                S.op("dve", lambda e, c=c: e.tensor_scalar(out=Hb[:, :], in0=th[:, :], scalar1=gl[:, c:c + 1], scalar2=None, op0=ALU.mult),
                     reads=["th", "gl"], writes=["Hb"])
                S.op("act", lambda e, c=c: e.activation(out=Hf[:, :], in_=th[:, :], func=AF.Identity, scale=gl[:, c:c + 1]),
                     reads=["th", "gl"], writes=["Hf"])

            if C.rk_stop == 5:
                return
            Y3 = g32(Yall)
            sq3 = g32(G[3])
            bc64 = lambda t: t[:, :].unsqueeze(2).broadcast_to([128, 32, 64])
            S.op("dve", lambda e: e.tensor_reduce(out=st1[:, :], in_=Y3, axis=AX.X, op=ALU.add), reads=["G1", "G1"], writes=["st1"])
            S.op("dve", lambda e: e.tensor_scalar(out=st1[:, :], in0=st1[:, :], scalar1=1.0 / 64, scalar2=None, op0=ALU.mult), reads=["st1"], writes=["st1"])
            S.op("dve", lambda e: e.tensor_tensor(out=Y3, in0=Y3, in1=bc64(st1), op=ALU.subtract), reads=["G1", "st1"], writes=["G1"])
            S.op("act", lambda e: e.activation(out=G[3][:, :], in_=Yall[:, :], func=AF.Square), reads=["G1"], writes=["G3"])
            S.op("dve", lambda e: e.tensor_reduce(out=st2[:, :], in_=sq3, axis=AX.X, op=ALU.add), reads=["G3"], writes=["st2"])
            S.op("act", lambda e: e.activation(out=st2[:, :], in_=st2[:, :], func=AF.Sqrt, bias=gneps[:, 0:1], scale=1.0 / 64),
                 reads=["st2", "gneps"], writes=["st2"])
            S.op("dve", lambda e: e.reciprocal(out=st2[:, :], in_=st2[:, :]), reads=["st2"], writes=["st2"])
            S.op("dve", lambda e: e.tensor_tensor(out=Y3, in0=Y3, in1=bc64(st2), op=ALU.mult), reads=["G1", "st2"], writes=["G1"])
            S.op("dve", lambda e: e.tensor_tensor(out=g3(Yall), in0=g3(Yall), in1=bc16(gng), op=ALU.mult), reads=["G1", "gng"], writes=["G1"])
            S.op("dve", lambda e: e.tensor_tensor(out=g3(Yall), in0=g3(Yall), in1=bc16(gnb), op=ALU.add), reads=["G1", "gnb"], writes=["G1"])
            S.op("dve", lambda e: e.tensor_tensor(out=sq3, in0=g32(G[0]), in1=bc64(bon), op=ALU.mult), reads=["G0", "bon"], writes=["G3"])
            S.op("dve", lambda e: e.tensor_tensor(out=Yall[:, :], in0=Yall[:, :], in1=G[3][:, :], op=ALU.add), reads=["G1", "G3"], writes=["G1"])
            Ytok = kt[:, :].rearrange("p (c f) -> p c f", c=16)
            yTsb = bt
            S.op("dve", lambda e: e.tensor_tensor(out=Ytok, in0=g3(Yall), in1=g3(G[2]), op=ALU.mult), reads=["G1", "G2"], writes=["kt"])
            for hf in range(2):
                pi, ps = psr()
                psv = ps[:, :].bitcast(BF16)

                def tr(e, psv=psv, hf=hf):
                    ins = None
                    for cc in range(8):
                        c = hf * 8 + cc
                        ins = e.transpose(out=psv[:, cc * 128:(cc + 1) * 128], in_=Ytok[:, c, :], identity=identb[:, :])
                    return ins
                S.op("pe", tr, reads=["kt", "identb"], writes=[("ps", pi)])
                S.op("act", lambda e, psv=psv, hf=hf: e.copy(out=yTsb[:, hf * 1024:(hf + 1) * 1024], in_=psv[:, :]),
                     reads=[("ps", pi)], pwrites=["bt"])
            S.dma("sp", lambda e: e.dma_start(out=R.yT[jc, :], in_=yTsb[:, :]), reads=["bt"], pwrites=[("dram", R.yT.name)])

        for j_ in range(C.rk_pairs):
            pair(j_)


def phase_sb_gen(C):
    S, nc, I, R = C.S, C.nc, C.I, C.R
    scale = 1.0 / float(np.sqrt(128.0))
    with ExitStack() as es:
        def sb(name, shape, dt=F32):
            return es.enter_context(nc.sbuf_tensor(name, list(shape), dt))
        qT = [sb(f"sbq{i}", [128, T], BF16) for i in range(2)]
        kT = [sb(f"sbk{i}", [128, T], BF16) for i in range(2)]
        kTs = [sb(f"sbks{i}", [128, T], BF16) for i in range(1)]
        vh = [sb(f"sbv{i}", [128, 16, 128], BF16) for i in range(2)]
        gch = [sb(f"sbg{i}", [128, 512]) for i in range(1)]
        egc = [sb(f"sbeg{i}", [128, 512]) for i in range(1)]
        mtmp = sb("sb_mtmp", [128, 128])
        Lst = sb("sb_L", [128, 128], BF16)
        onb = sb("sb_ones", [128, 128], BF16)
        mbig = sb("sb_mbig", [128, 896], BF16)
        mbigf = e_sb = None
        onec = sb("sb_onec", [128, 1])
        e_sb = [sb(f"sb_e{i}", [128, 512]) for i in range(1)]
        sp_sb = [sb(f"sb_sp{i}", [128, 512]) for i in range(2)]
        arg_sb = [sb(f"sb_arg{i}", [128, 512]) for i in range(2)]
        spb = [sb(f"sb_spb{i}", [128, 512], BF16) for i in range(2)]
        att = [sb(f"sb_att{i}", [128, 512], BF16) for i in range(2)]
        acc = [sb(f"sb_acc{i}", [128, 512]) for i in range(1)]
        accb = [sb(f"sb_accb{i}", [128, 512], BF16) for i in range(2)]
        ost = [sb(f"sb_ost{i}", [128, 512], BF16) for i in range(2)]

        S.dma("sp", lambda e: e.dma_start(out=mtmp[:, :], in_=I.consts[:, M0:M0 + 128]), writes=["sb_mtmp"])
        S.dma("pool", lambda e: e.dma_start(out=mbig[:, :], in_=I.consts[:, MBIG:MBIG + 896]), writes=["sb_mbig"])
        S.op("dve", lambda e: e.tensor_scalar(out=Lst[:, :], in0=mtmp[:, :], scalar1=-1.0, scalar2=None, op0=ALU.mult), reads=["sb_mtmp"], writes=["sb_L"])
        S.op("pool", lambda e: e.memset(onb[:, :], -1.0), writes=["sb_ones"])
        S.op("pool", lambda e: e.memset(onec[:, :], 1.0), writes=["sb_onec"])
        vv = R.vstok.rearrange("(kb p) c -> p kb c", p=128)
        PZ, PL, PO = (0, 1), (2, 3), (6, 7)
        cnt = {"item": 0, "qcg": 0, "accb": 0}

        def f_z(it):
            kb, qc, hi, bz = it["kb"], it["qc"], it["hi"], it["bz"]
            pz = PZ[it["zb"]]
            S.op("pe", lambda e: e.matmul(C.psum[pz][:, :], lhsT=kT[hi][:, kb * 128:(kb + 1) * 128],
                                          rhs=qT[hi][:, qc * 512:(qc + 1) * 512], start=True, stop=True),
                 reads=[("sbq", hi), ("sbk", hi)], writes=[("ps", pz)])

        def f_sp(it):
            kb, qc, bz, be = it["kb"], it["qc"], it["bz"], it["be"]
            pz = PZ[it["zb"]]
            S.op("act", lambda e: e.activation(out=e_sb[be][:, :], in_=C.psum[pz][:, :], func=AF.Exp, scale=scale),
                 reads=[("ps", pz)], writes=[("sb_e", be)])
            S.op("act", lambda e: e.activation(out=sp_sb[bz][:, :], in_=e_sb[be][:, :], func=AF.Ln, bias=onec[:, 0:1], scale=1.0),
                 reads=[("sb_e", be), "sb_onec"], writes=[("sb_sp", bz)])
            off = kb - 4 * qc
            if off >= 0:
                m0 = 384 - off * 128
                S.op("dve", lambda e: e.tensor_tensor(out=sp_sb[bz][:, :], in0=sp_sb[bz][:, :], in1=mbig[:, m0:m0 + 512], op=ALU.mult),
                     reads=[("sb_sp", bz), "sb_mbig"], writes=[("sb_sp", bz)])
            S.op("act", lambda e: e.copy(out=spb[bz][:, :], in_=sp_sb[bz][:, :]),
                 reads=[("sb_sp", bz)], writes=[("sb_spb", bz)])

        def f_later(it):
            kb, qc, first, bz, bl, ai = it["kb"], it["qc"], it["first"], it["bz"], it["bl"], it["ai"]
            pl = PL[bl]
            hi = it["hi"]
            if kb == first and qc == 0:
                S.op("act", lambda e: e.activation(out=kTs[0][:, :], in_=kT[hi][:, :], func=AF.Copy, scale=scale),
                     reads=[("sbk", hi)], writes=[("sbks", 0)])
            if kb == first:
                def mm(e):
                    e.matmul(C.psum[pl][:, :], lhsT=Lst[:, :], rhs=spb[bz][:, :], start=True, stop=False)
                    return e.matmul(C.psum[pl][:, :], lhsT=kTs[0][:, kb * 128:(kb + 1) * 128], rhs=qT[hi][:, qc * 512:(qc + 1) * 512],
                                    start=False, stop=True)
                S.op("pe", mm, reads=["sb_L", ("sb_spb", bz), ("sbks", 0), ("sbq", hi)], writes=[("ps", pl)])
            else:
                abi = it["abi"]

                def mm(e):
                    e.matmul(C.psum[pl][:, :], lhsT=Lst[:, :], rhs=spb[bz][:, :], start=True, stop=False)
                    e.matmul(C.psum[pl][:, :], lhsT=onb[:, :], rhs=accb[abi][:, :], start=False, stop=False)
                    return e.matmul(C.psum[pl][:, :], lhsT=kTs[0][:, kb * 128:(kb + 1) * 128], rhs=qT[hi][:, qc * 512:(qc + 1) * 512],
                                    start=False, stop=True)
                S.op("pe", mm, reads=["sb_L", "sb_ones", ("sb_spb", bz), ("sb_accb", abi), ("sbks", 0), ("sbq", hi)], writes=[("ps", pl)])
            S.op("dve", lambda e: e.tensor_tensor(out=arg_sb[bl][:, :], in0=C.psum[pl][:, :], in1=sp_sb[bz][:, :], op=ALU.subtract),
                 reads=[("sb_sp", bz), ("ps", pl)], writes=[("sb_arg", bl)])
            if kb > 0:
                if kb == first:
                    S.op("pool", lambda e: e.tensor_copy(out=acc[ai][:, :], in_=sp_sb[bz][:, :]),
                         reads=[("sb_sp", bz)], writes=[("sb_acc", ai)])
                else:
                    S.op("pool", lambda e: e.tensor_tensor(out=acc[ai][:, :], in0=acc[ai][:, :], in1=sp_sb[bz][:, :], op=ALU.add),
                         reads=[("sb_sp", bz), ("sb_acc", ai)], writes=[("sb_acc", ai)])
                nb = it["nabi"]
                S.op("dve", lambda e: e.tensor_copy(out=accb[nb][:, :], in_=acc[ai][:, :]),
                     reads=[("sb_acc", ai)], writes=[("sb_accb", nb)])

        def f_att(it):
            h, kb, qc, first, hi, bl, oi = it["h"], it["kb"], it["qc"], it["first"], it["hi"], it["bl"], it["oi"]
            po = PO[oi]
            S.op("act", lambda e: e.activation(out=att[bl][:, :], in_=arg_sb[bl][:, :], func=AF.Exp),
                 reads=[("sb_arg", bl)], writes=[("sb_att", bl)])
            off = kb - 4 * qc
            if off >= 0:
                m0 = 384 - off * 128
                S.op("pool", lambda e: e.tensor_tensor(out=att[bl][:, :], in0=att[bl][:, :], in1=mbig[:, m0:m0 + 512], op=ALU.mult),
                     reads=[("sb_att", bl), "sb_mbig"], writes=[("sb_att", bl)])
            S.op("pe", lambda e: e.matmul(C.psum[po][:, :], lhsT=vh[hi][:, kb, :], rhs=att[bl][:, :],
                                          start=(kb == first), stop=(kb == 0)),
                 reads=[("sbv", hi), ("sb_att", bl)], writes=[("ps", po)])
            if kb == 0:
                gi = it["gi"]
                S.op("dve", lambda e: e.tensor_scalar(out=egc[gi][:, :], in0=egc[gi][:, :], scalar1=1.0, scalar2=None, op0=ALU.add),
                     reads=[("sbeg", gi)], writes=[("sbeg", gi)])
                S.op("dve", lambda e: e.reciprocal(out=egc[gi][:, :], in_=egc[gi][:, :]), reads=[("sbeg", gi)], writes=[("sbeg", gi)])
                S.op("dve", lambda e: e.tensor_tensor(out=gch[gi][:, :], in0=C.psum[po][:, :], in1=gch[gi][:, :], op=ALU.mult),
                     reads=[("ps", po), ("sbg", gi)], writes=[("sbg", gi)])
                S.op("dve", lambda e: e.tensor_tensor(out=ost[oi][:, :], in0=gch[gi][:, :], in1=egc[gi][:, :], op=ALU.mult),
                     reads=[("sbg", gi), ("sbeg", gi)], writes=[("sb_ost", oi)])
                S.dma("sp", lambda e: e.dma_start(out=R.yT[RW + h * 128:RW + (h + 1) * 128, qc * 512:(qc + 1) * 512], in_=ost[oi][:, :]),
                      reads=[("sb_ost", oi)], pwrites=[("dram", R.yT.name)])

        items = []
        n = 0
        for h in range(C.sb_heads):
            hi = h % 2
            for qc in range(4):
                first = 4 * qc + 3
                g = h * 4 + qc
                for kb in range(first, -1, -1):
                    items.append(dict(h=h, hi=hi, qc=qc, kb=kb, first=first, bz=n % 2, zb=n % 2, be=0, bl=n % 2, ai=0, oi=g % 2, gi=0,
                                      abi=(n - 1) % 2, nabi=n % 2))
                    n += 1

        def load_head(h):
            hi = h % 2
            S.dma("sp", lambda e: e.dma_start(out=qT[hi][:, :], in_=R.qT[h * 128:(h + 1) * 128, :]),
                  reads=[("dram", R.qT.name)], writes=[("sbq", hi)])
            S.dma("sp", lambda e: e.dma_start(out=kT[hi][:, :], in_=R.ksT[h * 128:(h + 1) * 128, :]),
                  reads=[("dram", R.ksT.name)], writes=[("sbk", hi)])
            S.dma("sp", lambda e: e.dma_start(out=vh[hi][:, :, :], in_=vv[:, :, h * 128:(h + 1) * 128]),
                  reads=[("dram", R.vstok.name)], writes=[("sbv", hi)])

        def load_gate(it):
            h, qc, gi = it["h"], it["qc"], it["gi"]
            S.dma("sp", lambda e: e.dma_start(out=gch[gi][:, :], in_=R.gsT[h * 128:(h + 1) * 128, qc * 512:(qc + 1) * 512]),
                  reads=[("dram", R.gsT.name)], writes=[("sbg", gi)])
            S.op("act", lambda e: e.activation(out=egc[gi][:, :], in_=gch[gi][:, :], func=AF.Exp, scale=-1.0),
                 reads=[("sbg", gi)], writes=[("sbeg", gi)])

        NI = len(items)
        loaded = set()
        for s in range(NI + 3):
            if s < NI:
                hh = items[s]["h"]
                for h2 in (hh, hh + 1):
                    if h2 < C.sb_heads and h2 not in loaded and (h2 == hh or items[s]["qc"] >= 2):
                        load_head(h2)
                        loaded.add(h2)
                if s == 0:
                    load_gate(items[0])
                f_z(items[s])
            if 0 <= s - 1 < NI:
                f_sp(items[s - 1])
            if 0 <= s - 2 < NI:
                f_later(items[s - 2])
            if 0 <= s - 3 < NI:
                f_att(items[s - 3])
                if items[s - 3]["kb"] == 0 and s - 2 < NI:
                    load_gate(items[s - 2])
            yield


def phase_sb(C):
    for _ in phase_sb_gen(C):
        pass


def phase_sgu(C):
    S, nc, I, R = C.S, C.nc, C.I, C.R
    with ExitStack() as es:
        def sb(name, shape, dt=F32):
            return es.enter_context(nc.sbuf_tensor(name, list(shape), dt))
        lng = sb("lng", [128, D])
        lnb = sb("lnb", [128, D])
        wsf = sb("wsf", [128, 16, 128])
        wsb = sb("wsb", [128, 16, 128], BF16)
        msk = sb("sg_msk", [128, 128])
        bsb = sb("bsb", [128, 16, 128])
        vb = [sb(f"vb{i}", [128, D]) for i in range(2)]
        vnb = [sb(f"vnb{i}", [128, D], BF16) for i in range(2)]
        stats = sb("sg_stats", [128, 8, 6])
        mv = sb("sg_mv", [128, 2])
        rs = sb("sg_rs", [128, 1])
        epsc = sb("sg_eps", [128, 1])
        ub = [sb(f"ub{i}", [128, 4, 128]) for i in range(8)]
        gb = [sb(f"gb{i}", [128, 4, 128]) for i in range(8)]
        mb = [sb(f"mb{i}", [128, 4, 128]) for i in range(2)]
        yb = [sb(f"yb{i}", [128, 4, 128], BF16) for i in range(2)]

        S.dma("sp", lambda e: e.dma_start(out=lng[:, :], in_=I.o_ln[0:1, :].partition_broadcast(128)), writes=["lng"])
        S.dma("sp", lambda e: e.dma_start(out=lnb[:, :], in_=I.o_ln[1:2, :].partition_broadcast(128)), writes=["lnb"])
        S.dma("sp", lambda e: e.dma_start(out=bsb[:, :, :].rearrange("p g t -> p (g t)"), in_=I.o_bs[0:1, :].partition_broadcast(128)), writes=["bsb"])
        S.dma("sp", lambda e: e.dma_start(out=wsf[:, :, :], in_=I.o_wsT[:, :, :]), writes=["wsf"])
        S.dma("sp", lambda e: e.dma_start(out=msk[:, :], in_=I.consts[:, M2:M2 + 128]), writes=["sg_msk"])
        S.op("pool", lambda e: e.memset(epsc[:, :], LN_EPS), writes=["sg_eps"])
        for g in range(16):
            S.op("dve", lambda e, g=g: e.tensor_tensor(out=wsb[:, g, :], in0=wsf[:, g, :], in1=msk[:, :], op=ALU.mult),
                 reads=["wsf", "sg_msk"], pwrites=["wsb"])

        uv = R.uT.rearrange("(fc p) t -> p fc t", p=128)
        gv = R.ggT.rearrange("(fc p) t -> p fc t", p=128)
        yv = R.y2T.rearrange("(fc p) t -> p fc t", p=128)
        rs_ = C.rs
        for c in range(16):
            tok = slice(c * 128, (c + 1) * 128)
            vi = c % 2
            v_, vn_ = vb[vi], vnb[vi]
            for q in range(4):
                S.dma("sp", lambda e, v_=v_, q=q, tok=tok: e.dma_start(out=v_[:, q * 1024:(q + 1) * 1024], in_=R.vtk[tok, q * 1024:(q + 1) * 1024]),
                      reads=[("dram", R.vtk.name)], pwrites=[("vb", vi)])
            S.op("act", lambda e, v_=v_: e.activation(out=v_[:, :], in_=v_[:, :], func=AF.Gelu),
                 reads=[("vb", vi)], writes=[("vb", vi)])
            for q in range(8):
                S.op("dve", lambda e, v_=v_, q=q: e.bn_stats(out=stats[:, q, :], in_=v_[:, q * 512:(q + 1) * 512]),
                     reads=[("vb", vi)], pwrites=["sg_stats"])
            S.op("dve", lambda e: e.bn_aggr(out=mv[:, :], in_=stats[:, :, :].rearrange("p a b -> p (a b)")),
                 reads=["sg_stats"], writes=["sg_mv"])
            S.op("act", lambda e: e.activation(out=rs[:, :], in_=mv[:, 1:2], func=AF.Sqrt, bias=epsc[:, 0:1], scale=1.0),
                 reads=["sg_mv", "sg_eps"], writes=["sg_rs"])
            S.op("dve", lambda e: e.reciprocal(out=rs[:, :], in_=rs[:, :]), reads=["sg_rs"], writes=["sg_rs"])
            S.op("dve", lambda e, v_=v_: e.tensor_scalar(out=v_[:, :], in0=v_[:, :], scalar1=mv[:, 0:1], scalar2=rs[:, 0:1],
                                                        op0=ALU.subtract, op1=ALU.mult),
                 reads=[("vb", vi), "sg_mv", "sg_rs"], writes=[("vb", vi)])
            S.op("pool", lambda e, v_=v_: e.tensor_tensor(out=v_[:, :], in0=v_[:, :], in1=lng[:, :], op=ALU.mult),
                 reads=[("vb", vi), "lng"], writes=[("vb", vi)])
            S.op("dve", lambda e, v_=v_, vn_=vn_: e.tensor_tensor(out=vn_[:, :], in0=v_[:, :], in1=lnb[:, :], op=ALU.add),
                 reads=[("vb", vi), "lnb"], writes=[("vnb", vi)])
            for f4 in range(8):
                u_, g_ = ub[f4], gb[f4]
                S.dma("sp", lambda e, u_=u_, f4=f4, tok=tok: e.dma_start(out=u_[:, :, :], in_=uv[:, f4 * 4:(f4 + 1) * 4, tok]),
                      reads=[("dram", R.uT.name)], writes=[("sgu", f4)])
                S.dma("sp", lambda e, g_=g_, f4=f4, tok=tok: e.dma_start(out=g_[:, :, :], in_=gv[:, f4 * 4:(f4 + 1) * 4, tok]),
                      reads=[("dram", R.ggT.name)], writes=[("sgg", f4)])
                S.op("act", lambda e, u_=u_: e.activation(out=u_[:, :, :], in_=u_[:, :, :], func=AF.Gelu),
                     reads=[("sgu", f4)], writes=[("sgu", f4)])
            for f4 in range(8):
                g_ = gb[f4]
                S.op("act", lambda e, g_=g_: e.activation(out=g_[:, :, :], in_=g_[:, :, :], func=AF.Silu),
                     reads=[("sgg", f4)], writes=[("sgg", f4)])
            for f4 in range(8):
                u_, g_ = ub[f4], gb[f4]
                bi = f4 % 2
                m_, y_ = mb[bi], yb[bi]
                pi = _rot(C.psum, rs_, "psm")
                ps = C.psum[pi]

                def mm(e, ps=ps, vn_=vn_, f4=f4):
                    ins = None
                    for k in range(4):
                        fc = f4 * 4 + k
                        ins = e.matmul(ps[:, k * 128:(k + 1) * 128], lhsT=vn_[:, fc * 128:(fc + 1) * 128],
                                       rhs=wsb[:, fc // 2, :], start=True, stop=True)
                    return ins
                S.op("pe", mm, reads=[("vnb", vi), "wsb"], writes=[("ps", pi)])
                g0 = f4 * 2
                S.op("dve", lambda e, ps=ps, m_=m_, g0=g0: e.tensor_tensor(
                    out=m_[:, :, :].rearrange("p (a b) t -> p a b t", a=2), in0=ps[:, :].rearrange("p (a b t) -> p a b t", a=2, b=2),
                    in1=bsb[:, g0:g0 + 2, :].unsqueeze(2).broadcast_to([128, 2, 2, 128]), op=ALU.add),
                    reads=[("ps", pi), "bsb"], writes=[("sgm", bi)])
                S.op("dve", lambda e, m_=m_, u_=u_: e.tensor_tensor(out=m_[:, :, :], in0=m_[:, :, :], in1=u_[:, :, :], op=ALU.mult),
                     reads=[("sgm", bi), ("sgu", f4)], writes=[("sgm", bi)])
                S.op("dve", lambda e, m_=m_, g_=g_, y_=y_: e.tensor_tensor(out=y_[:, :, :], in0=m_[:, :, :], in1=g_[:, :, :], op=ALU.mult),
                     reads=[("sgm", bi), ("sgg", f4)], writes=[("sgy", bi)])
                S.dma("sp", lambda e, y_=y_, f4=f4, tok=tok: e.dma_start(out=yv[:, f4 * 4:(f4 + 1) * 4, tok], in_=y_[:, :, :]),
                      reads=[("sgy", bi)], pwrites=[("dram", R.y2T.name)])


def build(phases=("p1", "p2", "p3", "p4", "p5", "p6", "p7"), debug_out=()):
    nc = bass.Bass("TRN2", target_bir_lowering=False)
    C = Ctx()
    C.nc = nc
    C.rs = {}

    def din(name, shape, dt=F32):
        return nc.dram_tensor(name, list(shape), dt, kind="ExternalInput").ap()

    def dscr(name, shape, dt=F32):
        kind = "ExternalOutput" if name in debug_out else "Internal"
        return nc.dram_tensor(name, list(shape), dt, kind=kind).ap()

    I = Ctx()
    C.I = I
    I.xT = din("xT", [D, T])
    I.ng = din("ng", [3, 128, 32])
    I.e_w_in = din("e_w_in", [D, EC])
    I.e_w_out = din("e_w_out", [D, D])
    I.o_w_in = din("o_w_in", [D, OC])
    I.o_w_out = din("o_w_out", [D, D])
    I.e_vec = din("e_vec", [10, 128, 16])
    I.e_mu_v = din("e_mu_v", [16, RW])
    I.e_gn = din("e_gn", [2, 16, RW])
    I.e_wdu = din("e_wdu", [128, RW])
    I.e_aup = din("e_aup", [128, RW])
    I.o_ln = din("o_ln", [2, D])
    I.o_wsT = din("o_wsT", [128, 16, 128])
    I.o_bs = din("o_bs", [1, 16 * 128])
    I.consts = din("consts", [128, NCONST])
    I.outT = nc.dram_tensor("outT", [D, T], F32, kind="ExternalOutput").ap()

    R = Ctx()
    C.R = R
    R.rT = dscr("s_rT", [RW, T])
    R.kT = dscr("s_kT", [RW, T])
    R.vtok = dscr("s_vtok", [T + 1, RW])
    R.loT = dscr("s_loT", [256, T])
    R.gtok = dscr("s_gtok", [T, RW])
    R.qT = dscr("s_qT", [SBW, T], BF16)
    R.ksT = dscr("s_ksT", [SBW, T], BF16)
    R.vstok = dscr("s_vstok", [T, SBW], BF16)
    R.gsT = dscr("s_gsT", [SBW, T])
    R.yT = dscr("s_yT", [D, T], BF16)
    R.x1T = dscr("s_x1T", [D, T])
    R.uT = dscr("s_uT", [D, T])
    R.vtk = dscr("s_vtk", [T, D])
    R.ggT = dscr("s_ggT", [D, T])
    R.y2T = dscr("s_y2T", [D, T], BF16)
    R.x2T = dscr("s_x2T", [D, T])

    with ExitStack() as es:
        S = Sched(nc, es)
        C.S = S

        def sb(name, shape, dt=F32):
            return es.enter_context(nc.sbuf_tensor(name, list(shape), dt))

        C.psum = [es.enter_context(nc.psum_tensor(f"ps{i}", [128, 512], F32)) for i in range(8)]
        C.ones_f = sb("ones_f", [128, 128])
        C.eps_rms = sb("eps_rms", [128, 1])
        C.ngs = sb("ngs", [128, 3, 32])
        S.op("pool", lambda e: e.memset(C.ones_f[:, :], 1.0), writes=["ones_f"])
        S.op("pool", lambda e: e.memset(C.eps_rms[:, :], RMS_EPS), writes=["eps_rms"])
        S.dma("sp", lambda e: e.dma_start(out=C.ngs[:, :, :], in_=I.ng.rearrange("a p c -> p a c")), writes=["ngs"])
        C.rstd = sb("rstd", [128, 512])

        from contextlib import contextmanager

        @contextmanager
        def proj_bufs(tag):
            with ExitStack() as es2:
                def sb2(name, shape, dt=F32):
                    return es2.enter_context(nc.sbuf_tensor(name + tag, list(shape), dt))
                C.hT = sb2("hT", [128, 32, 1024], BF16)
                C.wbuf = [sb2(f"wbuf{i}", [128, 32, 512], BF16) for i in range(2)]
                C.xst = [sb2(f"xst{i}", [128, XC, 512]) for i in range(2)]
                C.sqb = [sb2(f"sqb{i}", [128, 512]) for i in range(1)]
                C.stF = [sb2(f"stF{i}", [128, 1024]) for i in range(2)]
                C.stB = [sb2(f"stB{i}", [128, 1024], BF16) for i in range(2)]
                yield
                S.barrier()

        x1src = I.xT if C.skip_layer0 else R.x1T
        if "p1" in phases:
            with proj_bufs("a"):
                groups = [
                    dict(c0=0, n=RW, mode="F", dst=R.rT, dt=F32),
                    dict(c0=RW, n=RW, mode="F", dst=R.kT, dt=F32),
                    dict(c0=2 * RW, n=RW, mode="T", dst=R.vtok, dt=F32, row_off=1),
                    dict(c0=3 * RW, n=256, mode="F", dst=R.loT, dt=F32),
                    dict(c0=3 * RW + 256, n=RW, mode="T", dst=R.gtok, dt=F32),
                    dict(c0=4 * RW + 256, n=SBW, mode="F", dst=R.qT, dt=BF16),
                    dict(c0=4 * RW + 256 + SBW, n=SBW, mode="F", dst=R.ksT, dt=BF16),
                    dict(c0=4 * RW + 256 + 2 * SBW, n=SBW, mode="T", dst=R.vstok, dt=BF16),
                    dict(c0=4 * RW + 256 + 3 * SBW, n=SBW, mode="F", dst=R.gsT, dt=F32),
                ]
                g_sb = groups[5:9]
                g_rk = groups[0:5]
                if C.p1_groups is not None:
                    g_sb = [groups[i] for i in C.p1_groups if i >= 5]
                    g_rk = [groups[i] for i in C.p1_groups if i < 5]
                fuse_sb = C.fuse_sb and ("p3" in phases)
                for half in range(2):
                    t0 = half * 1024
                    norm_tokens(C, I.xT, C.ngs[:, 0, :], t0, 1024, hT=C.hT, hkey="hT")
                    project(C, C.hT, "hT", I.e_w_in, g_sb, t0)
                    if half == 1 and fuse_sb:
                        C.psm_banks = [4, 5]
                        pg = project_gen(C, C.hT, "hT", I.e_w_in, g_rk, t0)
                        sg = phase_sb_gen(C)
                        pdone = sdone = False
                        while not (pdone and sdone):
                            if not pdone:
                                try:
                                    next(pg)
                                except StopIteration:
                                    pdone = True
                            for _k in range(C.sb_per_group):
                                if not sdone:
                                    try:
                                        next(sg)
                                    except StopIteration:
                                        sdone = True
                        C.psm_banks = list(range(8))
                    else:
                        project(C, C.hT, "hT", I.e_w_in, g_rk, t0)
        if "p2" in phases:
            phase_rwkv(C)
            S.barrier()
        if "p3" in phases and not (C.fuse_sb and "p1" in phases):
            phase_sb(C)
            S.barrier()
        if "p4" in phases or "p5" in phases:
            with proj_bufs("b"):
                if "p4" in phases:
                    for half in range(2):
                        t0 = half * 1024
                        load_actT(C, R.yT, t0)
                        project(C, C.hT, "hT", I.e_w_out,
                                [dict(c0=0, n=D, mode="F", dst=R.x1T, dt=F32, resid=I.xT)], t0)
                if "p5" in phases:
                    groups = [
                        dict(c0=0, n=D, mode="F", dst=R.uT, dt=F32),
                        dict(c0=D, n=D, mode="T", dst=R.vtk, dt=F32),
                        dict(c0=2 * D, n=D, mode="F", dst=R.ggT, dt=F32),
                    ]
                    for half in range(2):
                        t0 = half * 1024
                        norm_tokens(C, x1src, C.ngs[:, 1, :], t0, 1024, hT=C.hT, hkey="hT")
                        project(C, C.hT, "hT", I.o_w_in, groups, t0)
        if "p6" in phases:
            phase_sgu(C)
            S.barrier()
        if "p7" in phases:
            with proj_bufs("c"):
                for half in range(2):
                    t0 = half * 1024
                    load_actT(C, R.y2T, t0)
                    project(C, C.hT, "hT", I.o_w_out,
                            [dict(c0=0, n=D, mode="F", dst=R.x2T, dt=F32, resid=x1src)], t0)
                norm_tokens(C, R.x2T, C.ngs[:, 2, :], 0, T, dstT=I.outT)

        S.final_wait("sp", S.all_tokens())
        S.emit()
    return nc


Ctx.p1_groups = None
Ctx.fuse_sb = True
Ctx.sb_per_group = 5
Ctx.psm_banks = list(range(8))
Ctx.rk_pairs = 16
Ctx.rk_stop = 0
Ctx.sb_heads = 16
Ctx.skip_layer0 = False


def _consts():
    p = np.arange(128)[:, None]
    f = np.arange(128)[None, :]
    c = np.zeros((128, NCONST), np.float32)
    c[:, M0:M0 + 128] = (p > f)
    c[:, M1:M1 + 128] = (f > p)
    c[:, M2:M2 + 128] = (f >= p)
    c[:, M3:M3 + 128] = (p >= f)
    c[:, IDO:IDO + 128] = (p == f)
    c[:, BDO:BDO + 128] = ((p // 64) == (f // 64))
    c[:, INDO:INDO + 2] = ((p // 64) == np.arange(2)[None, :])
    fb = np.arange(896)[None, :]
    c[:, MBIG:MBIG + 896] = ((fb - p) > 384)
    t = np.arange(2048)[None, :]
    c[:, RMASK:RMASK + 2048] = np.broadcast_to((t % 128) != 0, (128, 2048))
    return c


def make_shared(inp):
    f = lambda a: np.asarray(a, dtype=np.float32)
    sh = {}
    pc = lambda v: np.ascontiguousarray(f(v).reshape(-1, 128).T)
    ng = f(inp["norm_g"])
    sh["ng"] = np.stack([pc(ng[0]), pc(ng[1]), pc(inp["final_norm_g"])])
    sh["e_w_in"] = f(inp["e_w_in"])[0]
    sh["e_w_out"] = f(inp["e_w_out"])[0]
    sh["o_w_in"] = f(inp["o_w_in"])[0]
    sh["o_w_out"] = f(inp["o_w_out"])[0]
    mu = f(inp["e_shift_mu"])[0]
    ev = np.zeros((10, 128, 16), np.float32)
    ev[0] = pc(inp["e_w0"][0]); ev[1] = pc(inp["e_a0"][0]); ev[2] = pc(inp["e_k_k"][0])
    ev[3] = pc(inp["e_k_a"][0]); ev[4] = pc(inp["e_r_k"][0])
    ev[5] = pc(mu[0:2048]); ev[6] = pc(mu[2048:4096])
    ev[7, :, 0] = mu[6144:6272]; ev[8, :, 0] = mu[6272:6400]
    sh["e_vec"] = ev
    rep = lambda v: np.ascontiguousarray(np.tile(f(v).reshape(16, 1, 128), (1, 16, 1)).reshape(16, 2048))
    sh["e_mu_v"] = rep(mu[4096:6144])
    sh["e_gn"] = np.stack([rep(inp["e_gn_g"][0]), rep(inp["e_gn_b"][0])])
    sh["e_wdu"] = f(inp["e_w_decay_up"])[0]
    sh["e_aup"] = f(inp["e_a_up"])[0]
    sh["o_ln"] = np.stack([f(inp["o_ln_g"])[0], f(inp["o_ln_b"])[0]])
    sh["o_wsT"] = np.ascontiguousarray(f(inp["o_w_s"])[0].transpose(2, 0, 1))
    sh["o_bs"] = np.ascontiguousarray(f(inp["o_b_s"])[0].reshape(1, -1))
    sh["consts"] = _consts()
    return sh


_NC_CACHE = {}


def kernel(**inputs):
    x = np.asarray(inputs["x"], dtype=np.float32)
    sh = make_shared(inputs)
    if "nc" not in _NC_CACHE:
        _NC_CACHE["nc"] = build()
    nc = _NC_CACHE["nc"]
    in_maps = []
    for b in range(NB):
        m = dict(sh)
        m["xT"] = np.ascontiguousarray(x[b].T)
        in_maps.append(m)
    res = run_bass_kernel_spmd(nc, in_maps, core_ids=list(range(NB)))
    out = np.stack([np.ascontiguousarray(np.asarray(r["outT"]).T) for r in res.results])
    return out.astype(np.float32)
```

```python
import numpy as np
from contextlib import ExitStack
import concourse.bass as bass
import concourse.mybir as mybir
from concourse.bass_utils import run_bass_kernel_spmd

F32 = mybir.dt.float32
BF16 = mybir.dt.bfloat16
AF = mybir.ActivationFunctionType
ALU = mybir.AluOpType
AX = mybir.AxisListType

D = 4096
T = 2048
NB = 8
RW = 2048
SBW = 2048
EC = 16640
OC = 12288
RMS_EPS = 1e-6
GN_EPS = 64e-5
LN_EPS = 1e-5
M0, M1, M2, M3, IDO, BDO, INDO, MBIG, RMASK = 0, 128, 256, 384, 512, 640, 768, 772, 1668
NCONST = 1668 + 2048
NSLOT = 4
XC = 1


class Sched:
    ENG = ("pe", "act", "dve", "pool", "sp")

    def __init__(self, nc, es, n_dma=12):
        self.nc = nc
        self.semobj = {}
        self.cnt = {}
        for e in self.ENG:
            self.semobj["s_" + e] = es.enter_context(nc.semaphore("s_" + e))
            self.cnt[e] = 0
        self.dq = {}
        for q in ("sp", "pool"):
            names = []
            for i in range(n_dma):
                nm = f"d_{q}{i}"
                self.semobj[nm] = es.enter_context(nc.semaphore(nm))
                names.append(nm)
            self.dq[q] = {"names": names, "val": [0] * n_dma, "next": 0}
        self.ops = {e: [] for e in self.ENG}
        self.wm = {e: {} for e in self.ENG}
        self.lastw = {}
        self.readers = {}

    def _deps(self, reads, writes, pwrites):
        deps = []
        for k in reads:
            deps.extend(self.lastw.get(k, {}).items())
        for k in writes:
            deps.extend(self.lastw.get(k, {}).items())
            deps.extend(self.readers.get(k, {}).items())
        for k in pwrites:
            deps.extend(self.readers.get(k, {}).items())
        return deps

    def _waits(self, eng, deps):
        need = {}
        for (s, v) in deps:
            if eng == "pe" and s == "s_pe":
                continue
            if self.wm[eng].get(s, 0) >= v:
                continue
            if need.get(s, 0) < v:
                need[s] = v
        for s, v in need.items():
            self.wm[eng][s] = v
        return list(need.items())

    def _commit(self, tok, reads, writes, pwrites):
        s, v = tok
        for k in writes:
            self.lastw[k] = {s: v}
            self.readers[k] = {}
        for k in pwrites:
            d = self.lastw.setdefault(k, {})
            d[s] = max(d.get(s, 0), v)
        for k in reads:
            d = self.readers.setdefault(k, {})
            d[s] = max(d.get(s, 0), v)

    def op(self, eng, fn, reads=(), writes=(), pwrites=()):
        reads, writes, pwrites = tuple(reads), tuple(writes), tuple(pwrites)
        waits = self._waits(eng, self._deps(reads, writes, pwrites))
        self.cnt[eng] += 1
        tok = ("s_" + eng, self.cnt[eng])
        self.ops[eng].append((waits, fn, "s_" + eng, 1))
        self._commit(tok, reads, writes, pwrites)
        return tok

    def dma(self, q, fn, reads=(), writes=(), pwrites=()):
        reads, writes, pwrites = tuple(reads), tuple(writes), tuple(pwrites)
        d = self.dq[q]
        i = d["next"]
        d["next"] = (i + 1) % len(d["names"])
        deps = self._deps(reads, writes, pwrites)
        if d["val"][i] > 0:
            deps.append((d["names"][i], d["val"][i]))
        waits = self._waits(q, deps)
        d["val"][i] += 16
        tok = (d["names"][i], d["val"][i])
        self.ops[q].append((waits, fn, d["names"][i], 16))
        self._commit(tok, reads, writes, pwrites)
        return tok

    def barrier(self):
        toks = self.all_tokens()
        for e in self.ENG:
            self.final_wait(e, toks)

    def final_wait(self, eng, toks):
        waits = self._waits(eng, list(toks))
        self.ops[eng].append((waits, None, None, 0))

    def all_tokens(self):
        toks = []
        for e in self.ENG:
            if self.cnt[e]:
                toks.append(("s_" + e, self.cnt[e]))
        for q in self.dq.values():
            for nm, v in zip(q["names"], q["val"]):
                if v:
                    toks.append((nm, v))
        return toks

    def emit(self):
        nc = self.nc
        engmap = {}

        def run(name, eng):
            for (waits, fn, sem, inc) in self.ops[name]:
                for (s, v) in waits:
                    eng.wait_ge(self.semobj[s], v)
                if fn is None:
                    continue
                ins = fn(eng)
                ins.then_inc(self.semobj[sem], inc)

        with nc.Block() as block:
            @block.tensor
            def _(e):
                run("pe", e)

            @block.scalar
            def _(e):
                run("act", e)

            @block.vector
            def _(e):
                run("dve", e)

            @block.gpsimd
            def _(e):
                run("pool", e)

            @block.sync
            def _(e):
                run("sp", e)


class Ctx:
    pass


def _rot(lst, state, name):
    i = state.get(name, 0)
    state[name] = (i + 1) % len(lst)
    return i


def norm_tokens(C, srcT, gcol, t0, ntok, hT=None, hkey=None, dstT=None):
    S = C.S
    nc = C.nc
    src_v = srcT.rearrange("(c p) t -> p c t", p=128)
    dst_v = dstT.rearrange("(c p) t -> p c t", p=128) if dstT is not None else None
    for tt in range(ntok // 512):
        tok = slice(t0 + tt * 512, t0 + (tt + 1) * 512)
        ps = C.psum[_rot(C.psum, C.rs, "psn")]
        pskey = ("ps", C.psum.index(ps))
        for c4 in range(32 // XC):
            xi = _rot(C.xst, C.rs, "xst")
            xb = C.xst[xi]
            S.dma("sp", lambda e, xb=xb, c4=c4, tok=tok: e.dma_start(out=xb[:, :, :], in_=src_v[:, c4 * XC:(c4 + 1) * XC, tok]),
                  reads=[("dram", srcT.name)], writes=[("xst", xi)])
            for cc in range(XC):
                c = c4 * XC + cc
                qi = _rot(C.sqb, C.rs, "sqb")
                qb = C.sqb[qi]
                S.op("act", lambda e, qb=qb, xb=xb, cc=cc: e.activation(out=qb[:, :], in_=xb[:, cc, :], func=AF.Square),
                     reads=[("xst", xi)], writes=[("sqb", qi)])
                S.op("pe", lambda e, ps=ps, qb=qb, c=c: e.matmul(ps[:, :], lhsT=C.ones_f[:, :], rhs=qb[:, :], start=(c == 0), stop=(c == 31)),
                     reads=[("sqb", qi)], writes=[pskey])
        S.op("act", lambda e, ps=ps: e.activation(out=C.rstd[:, :], in_=ps[:, :], func=AF.Sqrt, bias=C.eps_rms[:, 0:1], scale=1.0 / D),
             reads=[pskey], writes=["rstd"])
        S.op("dve", lambda e: e.reciprocal(out=C.rstd[:, :], in_=C.rstd[:, :]), reads=["rstd"], writes=["rstd"])
        for c4 in range(32 // XC):
            xi = _rot(C.xst, C.rs, "xst")
            xb = C.xst[xi]
            S.dma("sp", lambda e, xb=xb, c4=c4, tok=tok: e.dma_start(out=xb[:, :, :], in_=src_v[:, c4 * XC:(c4 + 1) * XC, tok]),
                  reads=[("dram", srcT.name)], writes=[("xst", xi)])
            if hT is not None:
                for cc in range(XC):
                    c = c4 * XC + cc
                    S.op("dve", lambda e, xb=xb, cc=cc, c=c, tt=tt: e.scalar_tensor_tensor(
                        out=hT[:, c, tt * 512:(tt + 1) * 512], in0=xb[:, cc, :], scalar=gcol[:, c:c + 1],
                        in1=C.rstd[:, :], op0=ALU.mult, op1=ALU.mult),
                        reads=[("xst", xi), "rstd"], pwrites=[hkey])
            else:
                for cc in range(XC):
                    c = c4 * XC + cc
                    S.op("dve", lambda e, xb=xb, cc=cc, c=c: e.scalar_tensor_tensor(
                        out=xb[:, cc, :], in0=xb[:, cc, :], scalar=gcol[:, c:c + 1],
                        in1=C.rstd[:, :], op0=ALU.mult, op1=ALU.mult),
                        reads=[("xst", xi), "rstd"], writes=[("xst", xi)])
                S.dma("sp", lambda e, xb=xb, c4=c4, tok=tok: e.dma_start(out=dst_v[:, c4 * XC:(c4 + 1) * XC, tok], in_=xb[:, :, :]),
                      reads=[("xst", xi)], pwrites=[("dram", dstT.name)])


def project_gen(C, actT, akey, w, groups, t0):
    S = C.S
    wv = w.rearrange("(kc p) n -> p kc n", p=128)
    for g in groups:
        dst = g["dst"]
        dkey = ("dram", dst.name)
        for ct in range(0, g["n"], 512):
            ncol = min(512, g["n"] - ct)
            col0 = g["c0"] + ct
            wi = _rot(C.wbuf, C.rs, "wbuf")
            wb = C.wbuf[wi]
            for hk in range(2):
                S.dma("pool", lambda e, wb=wb, col0=col0, ncol=ncol, hk=hk: e.dma_start(
                    out=wb[:, hk * 16:(hk + 1) * 16, 0:ncol], in_=wv[:, hk * 16:(hk + 1) * 16, col0:col0 + ncol]),
                    reads=[("dram", w.name)], writes=[("wbuf", wi, hk)])
            wkeys = [("wbuf", wi, 0), ("wbuf", wi, 1)]
            if g["mode"] == "F":
                for cc in range(ncol // 128):
                    st_list = C.stF if g["dt"] == F32 else C.stB
                    sname = "stF" if g["dt"] == F32 else "stB"
                    si = _rot(st_list, C.rs, sname)
                    st = st_list[si]
                    skey = (sname, si)
                    row0 = ct + cc * 128
                    if g.get("resid") is not None:
                        res = g["resid"]
                        S.dma("sp", lambda e, st=st, res=res, row0=row0: e.dma_start(
                            out=st[:, :], in_=res[row0:row0 + 128, t0:t0 + 1024]),
                            reads=[("dram", res.name)], writes=[skey])
                    for tt in range(2):
                        pi = C.psm_banks[_rot(C.psm_banks, C.rs, "psm%d" % len(C.psm_banks))]
                        ps = C.psum[pi]

                        NPc = C.npiece
                        for pc in range(NPc):
                            def mm(e, ps=ps, wb=wb, cc=cc, tt=tt, pc=pc, NPc=NPc):
                                ins = None
                                for kc in range(pc * 32 // NPc, (pc + 1) * 32 // NPc):
                                    ins = e.matmul(ps[:, :], lhsT=wb[:, kc, cc * 128:(cc + 1) * 128],
                                                   rhs=actT[:, kc, tt * 512:(tt + 1) * 512],
                                                   start=(kc == 0), stop=(kc == 31))
                                return ins
                            S.op("pe", mm, reads=wkeys + [akey], writes=[("ps", pi)])
                            yield
                        if g.get("resid") is not None:
                            S.op("dve", lambda e, ps=ps, st=st, tt=tt: e.tensor_tensor(
                                out=st[:, tt * 512:(tt + 1) * 512], in0=ps[:, :], in1=st[:, tt * 512:(tt + 1) * 512], op=ALU.add),
                                reads=[("ps", pi), skey], pwrites=[skey])
                        else:
                            ev = "act" if _rot([0, 1], C.rs, "evsel") == 0 else "dve"
                            if ev == "act":
                                S.op("act", lambda e, ps=ps, st=st, tt=tt: e.copy(out=st[:, tt * 512:(tt + 1) * 512], in_=ps[:, :]),
                                     reads=[("ps", pi)], pwrites=[skey])
                            else:
                                S.op("dve", lambda e, ps=ps, st=st, tt=tt: e.tensor_copy(out=st[:, tt * 512:(tt + 1) * 512], in_=ps[:, :]),
                                     reads=[("ps", pi)], pwrites=[skey])
                    S.dma("sp", lambda e, st=st, dst=dst, row0=row0: e.dma_start(
                        out=dst[row0:row0 + 128, t0:t0 + 1024], in_=st[:, :]),
                        reads=[skey], pwrites=[dkey])
            else:
                for tb in range(8):
                    st_list = C.stF if g["dt"] == F32 else C.stB
                    sname = "stF" if g["dt"] == F32 else "stB"
                    si = _rot(st_list, C.rs, sname)
                    st = st_list[si]
                    skey = (sname, si)
                    pi = C.psm_banks[_rot(C.psm_banks, C.rs, "psm%d" % len(C.psm_banks))]
                    ps = C.psum[pi]

                    NPc = C.npiece
                    for pc in range(NPc):
                        def mm(e, ps=ps, wb=wb, tb=tb, ncol=ncol, pc=pc, NPc=NPc):
                            ins = None
                            for kc in range(pc * 32 // NPc, (pc + 1) * 32 // NPc):
                                ins = e.matmul(ps[:, 0:ncol], lhsT=actT[:, kc, tb * 128:(tb + 1) * 128],
                                               rhs=wb[:, kc, 0:ncol], start=(kc == 0), stop=(kc == 31))
                            return ins
                        S.op("pe", mm, reads=wkeys + [akey], writes=[("ps", pi)])
                        yield
                    ev = "act" if _rot([0, 1], C.rs, "evsel") == 0 else "dve"
                    if ev == "act":
                        S.op("act", lambda e, ps=ps, st=st, ncol=ncol: e.copy(out=st[:, 0:ncol], in_=ps[:, 0:ncol]),
                             reads=[("ps", pi)], writes=[skey])
                    else:
                        S.op("dve", lambda e, ps=ps, st=st, ncol=ncol: e.tensor_copy(out=st[:, 0:ncol], in_=ps[:, 0:ncol]),
                             reads=[("ps", pi)], writes=[skey])
                    ro = g.get("row_off", 0)
                    S.dma("sp", lambda e, st=st, dst=dst, tb=tb, ct=ct, ncol=ncol, ro=ro: e.dma_start(
                        out=dst[ro + t0 + tb * 128:ro + t0 + (tb + 1) * 128, ct:ct + ncol], in_=st[:, 0:ncol]),
                        reads=[skey], pwrites=[dkey])


def project(C, actT, akey, w, groups, t0):
    for _ in project_gen(C, actT, akey, w, groups, t0):
        pass


def load_actT(C, srcT, t0):
    S = C.S
    v = srcT.rearrange("(c p) t -> p c t", p=128)
    for c8 in range(4):
        S.dma("sp", lambda e, c8=c8: e.dma_start(out=C.hT[:, c8 * 8:(c8 + 1) * 8, :], in_=v[:, c8 * 8:(c8 + 1) * 8, t0:t0 + 1024]),
              reads=[("dram", srcT.name)], pwrites=["hT"])


def phase_rwkv(C):
    S, nc, I, R = C.S, C.nc, C.I, C.R
    rs_ = C.rs
    NEG_E = -float(np.exp(-0.5))
    with ExitStack() as es:
        def sb(name, shape, dt=F32):
            return es.enter_context(nc.sbuf_tensor("rk_" + name, list(shape), dt))
        vec = sb("vec", [128, 10, 16])
        wdu = sb("wdu", [128, RW])
        aup = sb("aup", [128, RW])
        bd = sb("bd", [128, 128])
        ind = sb("ind", [128, 2])
        identf = sb("identf", [128, 128])
        identb = sb("identb", [128, 128], BF16)
        rmask = sb("rmask", [128, T])
        mk4 = sb("mk4", [128, 512])
        m04 = sb("m04", [128, 4, 128])
        twlo = sb("twlo", [128, T])
        alo = sb("alo", [128, T])
        gneps = sb("gneps", [128, 1])
        G = [sb(f"G{i}", [128, T]) for i in range(8)]
        bt = sb("bt", [128, T], BF16)
        kt = sb("kt", [128, T], BF16)
        QT = sb("QT", [128, 16, 2, 128], BF16)
        Btok = sb("Btok", [128, 16, 128], BF16)
        Ktok = sb("Ktok", [128, 16, 128], BF16)
        Vb = sb("Vb", [128, 16, 128], BF16)
        ATall = sb("ATall", [128, 32, 512], BF16)
        MTall = sb("MTall", [128, 32, 128], BF16)
        Xg = [[sb(f"Xg{s}{i}", [128, 4, 128], BF16) for i in range(2)] for s in range(NSLOT)]
        XTg = [[sb(f"XTg{s}{i}", [128, 4, 128], BF16) for i in range(2)] for s in range(NSLOT)]
        gl = sb("gl", [128, 16])
        bon = sb("bon", [128, 32])
        st1 = sb("st1", [128, 32])
        st2 = sb("st2", [128, 32])
        muv = sb("muv", [128, 128])
        gng = sb("gng", [128, 128])
        gnb = sb("gnb", [128, 128])
        Hf = sb("Hf", [128, 128])
        Hb = sb("Hb", [128, 128], BF16)
        th = sb("th", [128, 128])
        RHSb = [sb(f"RHSb{i}", [128, 128], BF16) for i in range(2)]
        Ub = [sb(f"Ub{i}", [128, 128], BF16) for i in range(2)]

        def ld(dst, src, key):
            S.dma("sp", lambda e: e.dma_start(out=dst, in_=src), writes=[key])
        ld(vec[:, :, :], I.e_vec.rearrange("a p j -> p a j"), "vec")
        ld(wdu[:, :], I.e_wdu[:, :], "wdu")
        ld(aup[:, :], I.e_aup[:, :], "aup")
        ld(bd[:, :], I.consts[:, BDO:BDO + 128], "bd")
        ld(ind[:, :], I.consts[:, INDO:INDO + 2], "ind")
        ld(identf[:, :], I.consts[:, IDO:IDO + 128], "identf")
        ld(rmask[:, :], I.consts[:, RMASK:RMASK + T], "rmask")
        for q in range(4):
            S.dma("sp", lambda e, q=q: e.dma_start(out=m04[:, q, :], in_=I.consts[:, M0:M0 + 128]), pwrites=["m04"])
            mo = M1 if q % 2 == 0 else M2
            S.dma("sp", lambda e, q=q, mo=mo: e.dma_start(out=mk4[:, q * 128:(q + 1) * 128], in_=I.consts[:, mo:mo + 128]), pwrites=["mk4"])
        S.op("dve", lambda e: e.tensor_copy(out=identb[:, :], in_=identf[:, :]), reads=["identf"], writes=["identb"])
        S.op("pool", lambda e: e.memset(G[7][0:1, :], 0.0), writes=["G7"])
        S.op("pool", lambda e: e.memset(gneps[:, :], GN_EPS), writes=["gneps"])
        S.op("pool", lambda e: e.memset(th[:, :], 0.0), writes=["th"])
        S.op("dve", lambda e: e.tensor_copy(out=C.psum[7][:, 0:128], in_=th[:, :]), reads=["th"], writes=[("ps", 7)])
        S.dma("sp", lambda e: e.dma_start(out=R.vtok[0:1, :], in_=G[7][0:1, :]), reads=["G7"], pwrites=[("dram", R.vtok.name)])

        def lerpF(src, skey, mu_col, tmp, tkey):
            S.op("dve", lambda e: e.tensor_tensor(out=tmp[:, 1:T], in0=src[:, 0:T - 1], in1=src[:, 1:T], op=ALU.subtract),
                 reads=[skey], pwrites=[tkey])
            S.op("dve", lambda e: e.tensor_scalar(out=tmp[:, 0:1], in0=src[:, 0:1], scalar1=-1.0, scalar2=None, op0=ALU.mult),
                 reads=[skey], pwrites=[tkey])
            S.op("dve", lambda e: e.scalar_tensor_tensor(out=src[:, :], in0=tmp[:, :], scalar=mu_col, in1=src[:, :],
                                                         op0=ALU.mult, op1=ALU.add),
                 reads=[skey, tkey, "vec"], writes=[skey])

        ld(twlo[:, :], R.loT[0:128, :], "twlo")
        ld(alo[:, :], R.loT[128:256, :], "alo")
        lerpF(twlo, "twlo", vec[:, 7, 0:1], G[2], "G2")
        lerpF(alo, "alo", vec[:, 8, 0:1], G[2], "G2")
        S.op("act", lambda e: e.activation(out=twlo[:, :], in_=twlo[:, :], func=AF.Tanh), reads=["twlo"], writes=["twlo"])

        def psr():
            i = 2 + _rot(list(range(5)), rs_, "rkps")
            return i, C.psum[i]

        def g3(t):
            return t[:, :].rearrange("p (c f) -> p c f", c=16)

        def g32(t):
            return t[:, :].rearrange("p (c f) -> p c f", c=32)

        def pair(j):
            jc = slice(j * 128, (j + 1) * 128)
            r_, k_, tmp, lw, a_, kkn, c_, ex = G
            S.dma("sp", lambda e: e.dma_start(out=r_[:, :], in_=R.rT[jc, :]), reads=[("dram", R.rT.name)], writes=["G0"])
            S.dma("sp", lambda e: e.dma_start(out=k_[:, :], in_=R.kT[jc, :]), reads=[("dram", R.kT.name)], writes=["G1"])
            lerpF(r_, "G0", vec[:, 5, j:j + 1], tmp, "G2")
            lerpF(k_, "G1", vec[:, 6, j:j + 1], tmp, "G2")
            for tc in range(4):
                ts_ = slice(tc * 512, (tc + 1) * 512)
                pi, ps = psr()
                S.op("pe", lambda e, ps=ps, ts_=ts_: e.matmul(ps[:, :], lhsT=wdu[:, jc], rhs=twlo[:, ts_], start=True, stop=True),
                     reads=["wdu", "twlo"], writes=[("ps", pi)])
                S.op("act", lambda e, ps=ps, ts_=ts_: e.activation(out=lw[:, ts_], in_=ps[:, :], func=AF.Sigmoid, bias=vec[:, 0, j:j + 1], scale=1.0),
                     reads=[("ps", pi), "vec"], pwrites=["G3"])
                pi, ps = psr()
                S.op("pe", lambda e, ps=ps, ts_=ts_: e.matmul(ps[:, :], lhsT=aup[:, jc], rhs=alo[:, ts_], start=True, stop=True),
                     reads=["aup", "alo"], writes=[("ps", pi)])
                S.op("act", lambda e, ps=ps, ts_=ts_: e.activation(out=a_[:, ts_], in_=ps[:, :], func=AF.Sigmoid, bias=vec[:, 1, j:j + 1], scale=1.0),
                     reads=[("ps", pi), "vec"], pwrites=["G4"])
            S.op("act", lambda e: e.activation(out=tmp[:, :], in_=k_[:, :], func=AF.Identity, scale=vec[:, 2, j:j + 1]),
                 reads=["G1", "vec"], writes=["G2"])
            S.op("act", lambda e: e.activation(out=ex[:, :], in_=tmp[:, :], func=AF.Square), reads=["G2"], writes=["G7"])
            for tc in range(4):
                ts_ = slice(tc * 512, (tc + 1) * 512)
                pi, ps = psr()
                S.op("pe", lambda e, ps=ps, ts_=ts_: e.matmul(ps[:, :], lhsT=bd[:, :], rhs=ex[:, ts_], start=True, stop=True),
                     reads=["bd", "G7"], writes=[("ps", pi)])
                S.op("act", lambda e, ps=ps, ts_=ts_: e.activation(out=c_[:, ts_], in_=ps[:, :], func=AF.Sqrt),
                     reads=[("ps", pi)], pwrites=["G6"])
            S.op("dve", lambda e: e.tensor_scalar(out=c_[:, :], in0=c_[:, :], scalar1=1e-12, scalar2=None, op0=ALU.max),
                 reads=["G6"], writes=["G6"])
            S.op("dve", lambda e: e.reciprocal(out=c_[:, :], in_=c_[:, :]), reads=["G6"], writes=["G6"])
            S.op("dve", lambda e: e.tensor_tensor(out=kkn[:, :], in0=tmp[:, :], in1=c_[:, :], op=ALU.mult),
                 reads=["G2", "G6"], writes=["G5"])
            S.op("dve", lambda e: e.tensor_scalar(out=tmp[:, :], in0=a_[:, :], scalar1=-1.0, scalar2=vec[:, 3, j:j + 1], op0=ALU.add, op1=ALU.mult),
                 reads=["G4", "vec"], writes=["G2"])
            S.op("dve", lambda e: e.scalar_tensor_tensor(out=k_[:, :], in0=tmp[:, :], scalar=1.0, in1=k_[:, :], op0=ALU.add, op1=ALU.mult),
                 reads=["G2", "G1"], writes=["G1"])
            S.op("dve", lambda e: e.scalar_tensor_tensor(out=tmp[:, :], in0=r_[:, :], scalar=vec[:, 4, j:j + 1], in1=k_[:, :], op0=ALU.mult, op1=ALU.mult),
                 reads=["G0", "G1", "vec"], writes=["G2"])
            pib, psb = psr()

            def mmb(e, psb=psb):
                ins = None
                for c in range(16):
                    ins = e.matmul(psb[:, 2 * c:2 * c + 2], lhsT=tmp[:, c * 128:(c + 1) * 128], rhs=ind[:, :], start=True, stop=True)
                return ins
            S.op("pe", mmb, reads=["G2", "ind"], writes=[("ps", pib)])
            S.op("act", lambda e, psb=psb: e.copy(out=bon[:, :], in_=psb[:, 0:32]), reads=[("ps", pib)], writes=["bon"])
            S.op("dve", lambda e: e.tensor_tensor(out=a_[:, :], in0=kkn[:, :], in1=a_[:, :], op=ALU.mult), reads=["G5", "G4"], writes=["G4"])
            S.op("dve", lambda e: e.tensor_tensor_scan(out=c_[:, :], data0=rmask[:, :], data1=lw[:, :], initial=0.0, op0=ALU.mult, op1=ALU.add),
                 reads=["rmask", "G3"], writes=["G6"])
            S.op("dve", lambda e: e.tensor_tensor(out=lw[:, :], in0=c_[:, :], in1=lw[:, :], op=ALU.subtract), reads=["G6", "G3"], writes=["G3"])
            S.op("act", lambda e: e.activation(out=ex[:, :], in_=c_[:, :], func=AF.Exp, scale=NEG_E), reads=["G6"], writes=["G7"])
            S.op("dve", lambda e: e.tensor_tensor(out=QT[:, :, 1, :], in0=g3(r_), in1=g3(ex), op=ALU.mult), reads=["G0", "G7"], pwrites=["QT"])
            S.op("dve", lambda e: e.tensor_copy(out=gl[:, :], in_=g3(ex)[:, :, 127]), reads=["G7"], writes=["gl"])
            S.op("act", lambda e: e.activation(out=ex[:, :], in_=lw[:, :], func=AF.Exp, scale=NEG_E), reads=["G3"], writes=["G7"])
            S.op("dve", lambda e: e.scalar_tensor_tensor(out=QT[:, :, 0, :], in0=g3(kkn), scalar=-1.0, in1=g3(ex), op0=ALU.mult, op1=ALU.mult),
                 reads=["G5", "G7"], pwrites=["QT"])
            S.op("act", lambda e: e.activation(out=ex[:, :], in_=c_[:, :], func=AF.Exp, scale=-NEG_E), reads=["G6"], writes=["G7"])
            S.op("dve", lambda e: e.tensor_tensor(out=bt[:, :], in0=a_[:, :], in1=ex[:, :], op=ALU.mult), reads=["G4", "G7"], writes=["bt"])
            S.op("dve", lambda e: e.tensor_tensor(out=kt[:, :], in0=k_[:, :], in1=ex[:, :], op=ALU.mult), reads=["G1", "G7"], writes=["kt"])
            if C.rk_stop == 1:
                return
            for (srcT_, skey, dstk, dkey) in ((bt, "bt", Btok, "Btok"), (kt, "kt", Ktok, "Ktok")):
                for hf in range(2):
                    pi, ps = psr()
                    psv = ps[:, :].bitcast(BF16)

                    def tr(e, psv=psv, srcT_=srcT_, hf=hf):
                        ins = None
                        for cc in range(8):
                            c = hf * 8 + cc
                            ins = e.transpose(out=psv[:, cc * 128:(cc + 1) * 128], in_=srcT_[:, c * 128:(c + 1) * 128], identity=identb[:, :])
                        return ins
                    S.op("pe", tr, reads=[skey, "identb"], writes=[("ps", pi)])
                    S.op("act", lambda e, psv=psv, dstk=dstk, hf=hf: e.copy(
                        out=dstk[:, hf * 8:(hf + 1) * 8, :].rearrange("p c f -> p (c f)"), in_=psv[:, :]),
                        reads=[("ps", pi)], pwrites=[dkey])
            vcur, vprev, sg = G[0], G[1], G[2]
            S.dma("sp", lambda e: e.dma_start(out=g3(vcur), in_=R.vtok[1:T + 1, jc].rearrange("(c p) f -> p c f", p=128)),
                  reads=[("dram", R.vtok.name)], writes=["G0"])
            S.dma("sp", lambda e: e.dma_start(out=g3(vprev), in_=R.vtok[0:T, jc].rearrange("(c p) f -> p c f", p=128)),
                  reads=[("dram", R.vtok.name)], writes=["G1"])
            S.dma("sp", lambda e: e.dma_start(out=g3(sg), in_=R.gtok[:, jc].rearrange("(c p) f -> p c f", p=128)),
                  reads=[("dram", R.gtok.name)], writes=["G2"])
            S.dma("sp", lambda e: e.dma_start(out=muv[:, :], in_=I.e_mu_v[j:j + 1, 0:128].partition_broadcast(128)), writes=["muv"])
            S.dma("sp", lambda e: e.dma_start(out=gng[:, :], in_=I.e_gn[0, j:j + 1, 0:128].partition_broadcast(128)), writes=["gng"])
            S.dma("sp", lambda e: e.dma_start(out=gnb[:, :], in_=I.e_gn[1, j:j + 1, 0:128].partition_broadcast(128)), writes=["gnb"])
            bc16 = lambda t: t[:, :].unsqueeze(1).broadcast_to([128, 16, 128])
            S.op("pool", lambda e: e.tensor_tensor(out=vprev[:, :], in0=vprev[:, :], in1=vcur[:, :], op=ALU.subtract), reads=["G0", "G1"], writes=["G1"])
            S.op("pool", lambda e: e.tensor_tensor(out=g3(vprev), in0=g3(vprev), in1=bc16(muv), op=ALU.mult), reads=["G1", "muv"], writes=["G1"])
            S.op("pool", lambda e: e.tensor_tensor(out=vcur[:, :], in0=vcur[:, :], in1=vprev[:, :], op=ALU.add), reads=["G0", "G1"], writes=["G0"])
            S.op("pool", lambda e: e.tensor_copy(out=Vb[:, :, :], in_=g3(vcur)), reads=["G0"], writes=["Vb"])
            S.op("act", lambda e: e.activation(out=sg[:, :], in_=sg[:, :], func=AF.Silu), reads=["G2"], writes=["G2"])

            if C.rk_stop == 2:
                return
            def group(cg, slot):
                p0 = cg * 4
                X, XT = Xg[slot], XTg[slot]
                psBs = [psr(), psr()]
                for q in range(4):
                    hp = q // 2
                    c = cg * 2 + q % 2
                    rows = slice(64 * hp, 64 * hp + 64)
                    cs_ = slice(c * 128, (c + 1) * 128)
                    pia = hp
                    psA = C.psum[pia]
                    pib, psB = psBs[hp]
                    cl = q % 2

                    def mma(e, psA=psA, rows=rows, cs_=cs_, c=c):
                        qv = QT[rows, c, :, :].rearrange("p a t -> p (a t)")
                        e.matmul(psA[:, 0:256], lhsT=bt[rows, cs_], rhs=qv, start=True, stop=True)
                        return e.matmul(psA[:, 256:512], lhsT=kt[rows, cs_], rhs=qv, start=True, stop=True)
                    S.op("pe", mma, reads=["bt", "kt", "QT"], writes=[("ps", pia)])
                    S.op("dve", lambda e, psA=psA, q=q: e.tensor_tensor(out=ATall[:, p0 + q, :], in0=psA[:, :], in1=mk4[:, :], op=ALU.mult),
                         reads=[("ps", pia), "mk4"], pwrites=[("ATall", cg)])
                    S.op("pe", lambda e, psB=psB, rows=rows, cs_=cs_, c=c, cl=cl: e.matmul(
                        psB[:, cl * 128:(cl + 1) * 128], lhsT=QT[rows, c, 0, :], rhs=bt[rows, cs_], start=True, stop=True),
                        reads=["QT", "bt"], pwrites=[("ps", pib)])
                for hp in range(2):
                    pib, psB = psBs[hp]
                    S.op("dve", lambda e, psB=psB, hp=hp: e.tensor_tensor(
                        out=X[0][:, 2 * hp:2 * hp + 2, :].rearrange("p q f -> p (q f)"), in0=psB[:, 0:256],
                        in1=m04[:, 0:2, :].rearrange("p q f -> p (q f)"), op=ALU.mult),
                        reads=[("ps", pib), "m04"], pwrites=[("Xg", slot, 0)])
                S.op("dve", lambda e: e.tensor_tensor(out=MTall[:, p0:p0 + 4, :], in0=ATall[:, p0:p0 + 4, 0:128],
                                                      in1=identb[:, :].unsqueeze(1).broadcast_to([128, 4, 128]), op=ALU.add),
                     reads=[("ATall", cg), "identb"], writes=[("MTall", cg)])
                yield
                xi = 0
                xtb = None
                xtkey = ("ATall", cg)

                def xt_ap(level_buf, q):
                    if level_buf is None:
                        return ATall[:, p0 + q, 0:128]
                    return XT[level_buf][:, q, :]
                for lvl in range(1, 7):
                    nxi = 1 - xi
                    piX, psX = psr()

                    def mmx(e, psX=psX, xi=xi, xtb=xtb):
                        ins = None
                        for q in range(4):
                            ins = e.matmul(psX[:, q * 128:(q + 1) * 128], lhsT=xt_ap(xtb, q), rhs=X[xi][:, q, :], start=True, stop=True)
                        return ins
                    S.op("pe", mmx, reads=[xtkey, ("Xg", slot, xi)], writes=[("ps", piX)])
                    if lvl < 6:
                        piT, psT = psr()
                        nxt = 0 if xtb is None else 1 - xtb

                        def mmt(e, psT=psT, xi=xi, xtb=xtb):
                            ins = None
                            for q in range(4):
                                ins = e.matmul(psT[:, q * 128:(q + 1) * 128], lhsT=X[xi][:, q, :], rhs=xt_ap(xtb, q), start=True, stop=True)
                            return ins
                        S.op("pe", mmt, reads=[xtkey, ("Xg", slot, xi)], writes=[("ps", piT)])
                    S.op("act", lambda e, psX=psX, nxi=nxi: e.copy(out=X[nxi][:, :, :].rearrange("p q f -> p (q f)"), in_=psX[:, :]),
                         reads=[("ps", piX)], writes=[("Xg", slot, nxi)])
                    if lvl < 6:
                        S.op("act", lambda e, psT=psT, nxt=nxt: e.copy(out=XT[nxt][:, :, :].rearrange("p q f -> p (q f)"), in_=psT[:, :]),
                             reads=[("ps", piT)], writes=[("XTg", slot, nxt)])
                        xtb = nxt
                        xtkey = ("XTg", slot, nxt)
                    xi = nxi
                    piD, psD = psr()

                    def mmd(e, psD=psD, xi=xi):
                        ins = None
                        for q in range(4):
                            ins = e.matmul(psD[:, q * 128:(q + 1) * 128], lhsT=X[xi][:, q, :], rhs=MTall[:, p0 + q, :], start=True, stop=True)
                        return ins
                    S.op("pe", mmd, reads=[("Xg", slot, xi), ("MTall", cg)], writes=[("ps", piD)])
                    S.op("dve", lambda e, psD=psD: e.tensor_tensor(out=MTall[:, p0:p0 + 4, :].rearrange("p q f -> p (q f)"),
                                                                   in0=MTall[:, p0:p0 + 4, :].rearrange("p q f -> p (q f)"), in1=psD[:, :], op=ALU.add),
                         reads=[("MTall", cg), ("ps", piD)], writes=[("MTall", cg)])
                    yield

            if C.rk_stop == 3:
                return
            for gp_ in range(8 // NSLOT):
                gens = [group(NSLOT * gp_ + s_, s_) for s_ in range(NSLOT)]
                for _step in range(7):
                    for g_ in gens:
                        next(g_)
            if C.rk_stop == 4:
                return
            S.op("pool", lambda e: e.memset(Hf[:, :], 0.0), writes=["Hf"])
            S.op("pool", lambda e: e.memset(Hb[:, :], 0.0), writes=["Hb"])
            Yall = G[1]
            psH = C.psum[7]
            for c in range(16):
                cg = c // 2
                bi = c % 2
                ix = [(c // 2) * 4 + hp * 2 + (c % 2) for hp in range(2)]
                piR, psR = psr()

                def mmr(e, psR=psR, c=c, ix=ix):
                    ins = e.matmul(psR[:, 0:128], lhsT=QT[:, c, 0, :], rhs=Hb[:, :], start=True, stop=False)
                    for hp in range(2):
                        vs = slice(64 * hp, 64 * hp + 64)
                        ins = e.matmul(psR[:, vs], lhsT=ATall[:, ix[hp], 256:384], rhs=Vb[:, c, vs], start=False, stop=(hp == 1))
                    return ins
                S.op("pe", mmr, reads=["QT", "Hb", ("ATall", cg), "Vb"], writes=[("ps", piR)])
                S.op("act", lambda e, psR=psR, bi=bi: e.copy(out=RHSb[bi][:, :], in_=psR[:, 0:128]), reads=[("ps", piR)], writes=[("RHSb", bi)])
                piU, psU = psr()

                def mmu(e, psU=psU, c=c, bi=bi, ix=ix):
                    ins = None
                    for hp in range(2):
                        vs = slice(64 * hp, 64 * hp + 64)
                        ins = e.matmul(psU[:, vs], lhsT=MTall[:, ix[hp], :], rhs=RHSb[bi][:, vs], start=True, stop=True)
                    return ins
                S.op("pe", mmu, reads=[("MTall", cg), ("RHSb", bi)], writes=[("ps", piU)])
                S.op("dve", lambda e, psU=psU, bi=bi: e.tensor_copy(out=Ub[bi][:, :], in_=psU[:, 0:128]), reads=[("ps", piU)], writes=[("Ub", bi)])

                def mmh(e, c=c, bi=bi):
                    ins = None
                    for hp in range(2):
                        vs = slice(64 * hp, 64 * hp + 64)
                        e.matmul(psH[vs, vs], lhsT=Btok[:, c, vs], rhs=Ub[bi][:, vs], start=True, stop=False)
                        ins = e.matmul(psH[vs, vs], lhsT=Ktok[:, c, vs], rhs=Vb[:, c, vs], start=False, stop=True)
                    return ins
                S.op("pe", mmh, reads=["Btok", "Ktok", "Vb", ("Ub", bi)], writes=[("ps", 7)])
                piY, psY = psr()

                def mmy(e, psY=psY, c=c, bi=bi, ix=ix):
                    ins = e.matmul(psY[:, 0:128], lhsT=QT[:, c, 1, :], rhs=Hb[:, :], start=True, stop=False)
                    for hp in range(2):
                        vs = slice(64 * hp, 64 * hp + 64)
                        e.matmul(psY[:, vs], lhsT=ATall[:, ix[hp], 128:256], rhs=Ub[bi][:, vs], start=False, stop=False)
                        ins = e.matmul(psY[:, vs], lhsT=ATall[:, ix[hp], 384:512], rhs=Vb[:, c, vs], start=False, stop=(hp == 1))
                    return ins
                S.op("pe", mmy, reads=["QT", "Hb", ("ATall", cg), "Vb", ("Ub", bi)], writes=[("ps", piY)])
                S.op("act", lambda e, psY=psY, c=c: e.copy(out=Yall[:, c * 128:(c + 1) * 128], in_=psY[:, 0:128]),
                     reads=[("ps", piY)], pwrites=["G1"])
                S.op("dve", lambda e: e.tensor_tensor(out=th[:, :], in0=psH[:, 0:128], in1=Hf[:, :], op=ALU.add),
                     reads=[("ps", 7), "Hf"], writes=["th"])
                S.op("dve", lambda e, c=c: e.tensor_scalar(out=Hb[:, :], in0=th[:, :], scalar1=gl[:, c:c + 1], scalar2=None, op0=ALU.mult),
                     reads=["th", "gl"], writes=["Hb"])
                S.op("act", lambda e, c=c: e.activation(out=Hf[:, :], in_=th[:, :], func=AF.Identity, scale=gl[:, c:c + 1]),
                     reads=["th", "gl"], writes=["Hf"])

            if C.rk_stop == 5:
                return
            Y3 = g32(Yall)
            sq3 = g32(G[3])
            bc64 = lambda t: t[:, :].unsqueeze(2).broadcast_to([128, 32, 64])
            S.op("dve", lambda e: e.tensor_reduce(out=st1[:, :], in_=Y3, axis=AX.X, op=ALU.add), reads=["G1", "G1"], writes=["st1"])
            S.op("dve", lambda e: e.tensor_scalar(out=st1[:, :], in0=st1[:, :], scalar1=1.0 / 64, scalar2=None, op0=ALU.mult), reads=["st1"], writes=["st1"])
            S.op("dve", lambda e: e.tensor_tensor(out=Y3, in0=Y3, in1=bc64(st1), op=ALU.subtract), reads=["G1", "st1"], writes=["G1"])
            S.op("act", lambda e: e.activation(out=G[3][:, :], in_=Yall[:, :], func=AF.Square), reads=["G1"], writes=["G3"])
            S.op("dve", lambda e: e.tensor_reduce(out=st2[:, :], in_=sq3, axis=AX.X, op=ALU.add), reads=["G3"], writes=["st2"])
            S.op("act", lambda e: e.activation(out=st2[:, :], in_=st2[:, :], func=AF.Sqrt, bias=gneps[:, 0:1], scale=1.0 / 64),
                 reads=["st2", "gneps"], writes=["st2"])
            S.op("dve", lambda e: e.reciprocal(out=st2[:, :], in_=st2[:, :]), reads=["st2"], writes=["st2"])
            S.op("dve", lambda e: e.tensor_tensor(out=Y3, in0=Y3, in1=bc64(st2), op=ALU.mult), reads=["G1", "st2"], writes=["G1"])
            S.op("dve", lambda e: e.tensor_tensor(out=g3(Yall), in0=g3(Yall), in1=bc16(gng), op=ALU.mult), reads=["G1", "gng"], writes=["G1"])
            S.op("dve", lambda e: e.tensor_tensor(out=g3(Yall), in0=g3(Yall), in1=bc16(gnb), op=ALU.add), reads=["G1", "gnb"], writes=["G1"])
            S.op("dve", lambda e: e.tensor_tensor(out=sq3, in0=g32(G[0]), in1=bc64(bon), op=ALU.mult), reads=["G0", "bon"], writes=["G3"])
            S.op("dve", lambda e: e.tensor_tensor(out=Yall[:, :], in0=Yall[:, :], in1=G[3][:, :], op=ALU.add), reads=["G1", "G3"], writes=["G1"])
            Ytok = kt[:, :].rearrange("p (c f) -> p c f", c=16)
            yTsb = bt
            S.op("dve", lambda e: e.tensor_tensor(out=Ytok, in0=g3(Yall), in1=g3(G[2]), op=ALU.mult), reads=["G1", "G2"], writes=["kt"])
            for hf in range(2):
                pi, ps = psr()
                psv = ps[:, :].bitcast(BF16)

                def tr(e, psv=psv, hf=hf):
                    ins = None
                    for cc in range(8):
                        c = hf * 8 + cc
                        ins = e.transpose(out=psv[:, cc * 128:(cc + 1) * 128], in_=Ytok[:, c, :], identity=identb[:, :])
                    return ins
                S.op("pe", tr, reads=["kt", "identb"], writes=[("ps", pi)])
                S.op("act", lambda e, psv=psv, hf=hf: e.copy(out=yTsb[:, hf * 1024:(hf + 1) * 1024], in_=psv[:, :]),
                     reads=[("ps", pi)], pwrites=["bt"])
            S.dma("sp", lambda e: e.dma_start(out=R.yT[jc, :], in_=yTsb[:, :]), reads=["bt"], pwrites=[("dram", R.yT.name)])

        for j_ in range(C.rk_pairs):
            pair(j_)


def phase_sb_gen(C):
    S, nc, I, R = C.S, C.nc, C.I, C.R
    scale = 1.0 / float(np.sqrt(128.0))
    with ExitStack() as es:
        def sb(name, shape, dt=F32):
            return es.enter_context(nc.sbuf_tensor(name, list(shape), dt))
        qT = [sb(f"sbq{i}", [128, T], BF16) for i in range(2)]
        kT = [sb(f"sbk{i}", [128, T], BF16) for i in range(2)]
        kTs = [sb(f"sbks{i}", [128, T], BF16) for i in range(1)]
        vh = [sb(f"sbv{i}", [128, 16, 128], BF16) for i in range(2)]
        gch = [sb(f"sbg{i}", [128, 512]) for i in range(1)]
        egc = [sb(f"sbeg{i}", [128, 512]) for i in range(1)]
        mtmp = sb("sb_mtmp", [128, 128])
        Lst = sb("sb_L", [128, 128], BF16)
        onb = sb("sb_ones", [128, 128], BF16)
        mbig = sb("sb_mbig", [128, 896], BF16)
        mbigf = e_sb = None
        onec = sb("sb_onec", [128, 1])
        e_sb = [sb(f"sb_e{i}", [128, 512]) for i in range(1)]
        sp_sb = [sb(f"sb_sp{i}", [128, 512]) for i in range(2)]
        arg_sb = [sb(f"sb_arg{i}", [128, 512]) for i in range(2)]
        spb = [sb(f"sb_spb{i}", [128, 512], BF16) for i in range(2)]
        att = [sb(f"sb_att{i}", [128, 512], BF16) for i in range(2)]
        acc = [sb(f"sb_acc{i}", [128, 512]) for i in range(1)]
        accb = [sb(f"sb_accb{i}", [128, 512], BF16) for i in range(2)]
        ost = [sb(f"sb_ost{i}", [128, 512], BF16) for i in range(2)]

        S.dma("sp", lambda e: e.dma_start(out=mtmp[:, :], in_=I.consts[:, M0:M0 + 128]), writes=["sb_mtmp"])
        S.dma("pool", lambda e: e.dma_start(out=mbig[:, :], in_=I.consts[:, MBIG:MBIG + 896]), writes=["sb_mbig"])
        S.op("dve", lambda e: e.tensor_scalar(out=Lst[:, :], in0=mtmp[:, :], scalar1=-1.0, scalar2=None, op0=ALU.mult), reads=["sb_mtmp"], writes=["sb_L"])
        S.op("pool", lambda e: e.memset(onb[:, :], -1.0), writes=["sb_ones"])
        S.op("pool", lambda e: e.memset(onec[:, :], 1.0), writes=["sb_onec"])
        vv = R.vstok.rearrange("(kb p) c -> p kb c", p=128)
        PZ, PL, PO = (0, 1), (2, 3), (6, 7)
        cnt = {"item": 0, "qcg": 0, "accb": 0}

        def f_z(it):
            kb, qc, hi, bz = it["kb"], it["qc"], it["hi"], it["bz"]
            pz = PZ[it["zb"]]
            S.op("pe", lambda e: e.matmul(C.psum[pz][:, :], lhsT=kT[hi][:, kb * 128:(kb + 1) * 128],
                                          rhs=qT[hi][:, qc * 512:(qc + 1) * 512], start=True, stop=True),
                 reads=[("sbq", hi), ("sbk", hi)], writes=[("ps", pz)])

        def f_sp(it):
            kb, qc, bz, be = it["kb"], it["qc"], it["bz"], it["be"]
            pz = PZ[it["zb"]]
            S.op("act", lambda e: e.activation(out=e_sb[be][:, :], in_=C.psum[pz][:, :], func=AF.Exp, scale=scale),
                 reads=[("ps", pz)], writes=[("sb_e", be)])
            S.op("act", lambda e: e.activation(out=sp_sb[bz][:, :], in_=e_sb[be][:, :], func=AF.Ln, bias=onec[:, 0:1], scale=1.0),
                 reads=[("sb_e", be), "sb_onec"], writes=[("sb_sp", bz)])
            off = kb - 4 * qc
            if off >= 0:
                m0 = 384 - off * 128
                S.op("dve", lambda e: e.tensor_tensor(out=sp_sb[bz][:, :], in0=sp_sb[bz][:, :], in1=mbig[:, m0:m0 + 512], op=ALU.mult),
                     reads=[("sb_sp", bz), "sb_mbig"], writes=[("sb_sp", bz)])
            S.op("act", lambda e: e.copy(out=spb[bz][:, :], in_=sp_sb[bz][:, :]),
                 reads=[("sb_sp", bz)], writes=[("sb_spb", bz)])

        def f_later(it):
            kb, qc, first, bz, bl, ai = it["kb"], it["qc"], it["first"], it["bz"], it["bl"], it["ai"]
            pl = PL[bl]
            hi = it["hi"]
            if kb == first and qc == 0:
                S.op("act", lambda e: e.activation(out=kTs[0][:, :], in_=kT[hi][:, :], func=AF.Copy, scale=scale),
                     reads=[("sbk", hi)], writes=[("sbks", 0)])
            if kb == first:
                def mm(e):
                    e.matmul(C.psum[pl][:, :], lhsT=Lst[:, :], rhs=spb[bz][:, :], start=True, stop=False)
                    return e.matmul(C.psum[pl][:, :], lhsT=kTs[0][:, kb * 128:(kb + 1) * 128], rhs=qT[hi][:, qc * 512:(qc + 1) * 512],
                                    start=False, stop=True)
                S.op("pe", mm, reads=["sb_L", ("sb_spb", bz), ("sbks", 0), ("sbq", hi)], writes=[("ps", pl)])
            else:
                abi = it["abi"]

                def mm(e):
                    e.matmul(C.psum[pl][:, :], lhsT=Lst[:, :], rhs=spb[bz][:, :], start=True, stop=False)
                    e.matmul(C.psum[pl][:, :], lhsT=onb[:, :], rhs=accb[abi][:, :], start=False, stop=False)
                    return e.matmul(C.psum[pl][:, :], lhsT=kTs[0][:, kb * 128:(kb + 1) * 128], rhs=qT[hi][:, qc * 512:(qc + 1) * 512],
                                    start=False, stop=True)
                S.op("pe", mm, reads=["sb_L", "sb_ones", ("sb_spb", bz), ("sb_accb", abi), ("sbks", 0), ("sbq", hi)], writes=[("ps", pl)])
            S.op("dve", lambda e: e.tensor_tensor(out=arg_sb[bl][:, :], in0=C.psum[pl][:, :], in1=sp_sb[bz][:, :], op=ALU.subtract),
                 reads=[("sb_sp", bz), ("ps", pl)], writes=[("sb_arg", bl)])
            if kb > 0:
                if kb == first:
                    S.op("pool", lambda e: e.tensor_copy(out=acc[ai][:, :], in_=sp_sb[bz][:, :]),
                         reads=[("sb_sp", bz)], writes=[("sb_acc", ai)])
                else:
                    S.op("pool", lambda e: e.tensor_tensor(out=acc[ai][:, :], in0=acc[ai][:, :], in1=sp_sb[bz][:, :], op=ALU.add),
                         reads=[("sb_sp", bz), ("sb_acc", ai)], writes=[("sb_acc", ai)])
                nb = it["nabi"]
                S.op("dve", lambda e: e.tensor_copy(out=accb[nb][:, :], in_=acc[ai][:, :]),
                     reads=[("sb_acc", ai)], writes=[("sb_accb", nb)])

        def f_att(it):
            h, kb, qc, first, hi, bl, oi = it["h"], it["kb"], it["qc"], it["first"], it["hi"], it["bl"], it["oi"]
            po = PO[oi]
            S.op("act", lambda e: e.activation(out=att[bl][:, :], in_=arg_sb[bl][:, :], func=AF.Exp),
                 reads=[("sb_arg", bl)], writes=[("sb_att", bl)])
            off = kb - 4 * qc
            if off >= 0:
                m0 = 384 - off * 128
                S.op("pool", lambda e: e.tensor_tensor(out=att[bl][:, :], in0=att[bl][:, :], in1=mbig[:, m0:m0 + 512], op=ALU.mult),
                     reads=[("sb_att", bl), "sb_mbig"], writes=[("sb_att", bl)])
            S.op("pe", lambda e: e.matmul(C.psum[po][:, :], lhsT=vh[hi][:, kb, :], rhs=att[bl][:, :],
                                          start=(kb == first), stop=(kb == 0)),
                 reads=[("sbv", hi), ("sb_att", bl)], writes=[("ps", po)])
            if kb == 0:
                gi = it["gi"]
                S.op("dve", lambda e: e.tensor_scalar(out=egc[gi][:, :], in0=egc[gi][:, :], scalar1=1.0, scalar2=None, op0=ALU.add),
                     reads=[("sbeg", gi)], writes=[("sbeg", gi)])
                S.op("dve", lambda e: e.reciprocal(out=egc[gi][:, :], in_=egc[gi][:, :]), reads=[("sbeg", gi)], writes=[("sbeg", gi)])
                S.op("dve", lambda e: e.tensor_tensor(out=gch[gi][:, :], in0=C.psum[po][:, :], in1=gch[gi][:, :], op=ALU.mult),
                     reads=[("ps", po), ("sbg", gi)], writes=[("sbg", gi)])
                S.op("dve", lambda e: e.tensor_tensor(out=ost[oi][:, :], in0=gch[gi][:, :], in1=egc[gi][:, :], op=ALU.mult),
                     reads=[("sbg", gi), ("sbeg", gi)], writes=[("sb_ost", oi)])
                S.dma("sp", lambda e: e.dma_start(out=R.yT[RW + h * 128:RW + (h + 1) * 128, qc * 512:(qc + 1) * 512], in_=ost[oi][:, :]),
                      reads=[("sb_ost", oi)], pwrites=[("dram", R.yT.name)])

        items = []
        n = 0
        for h in range(C.sb_heads):
            hi = h % 2
            for qc in range(4):
                first = 4 * qc + 3
                g = h * 4 + qc
                for kb in range(first, -1, -1):
                    items.append(dict(h=h, hi=hi, qc=qc, kb=kb, first=first, bz=n % 2, zb=n % 2, be=0, bl=n % 2, ai=0, oi=g % 2, gi=0,
                                      abi=(n - 1) % 2, nabi=n % 2))
                    n += 1

        def load_head(h):
            hi = h % 2
            S.dma("sp", lambda e: e.dma_start(out=qT[hi][:, :], in_=R.qT[h * 128:(h + 1) * 128, :]),
                  reads=[("dram", R.qT.name)], writes=[("sbq", hi)])
            S.dma("sp", lambda e: e.dma_start(out=kT[hi][:, :], in_=R.ksT[h * 128:(h + 1) * 128, :]),
                  reads=[("dram", R.ksT.name)], writes=[("sbk", hi)])
            S.dma("sp", lambda e: e.dma_start(out=vh[hi][:, :, :], in_=vv[:, :, h * 128:(h + 1) * 128]),
                  reads=[("dram", R.vstok.name)], writes=[("sbv", hi)])

        def load_gate(it):
            h, qc, gi = it["h"], it["qc"], it["gi"]
            S.dma("sp", lambda e: e.dma_start(out=gch[gi][:, :], in_=R.gsT[h * 128:(h + 1) * 128, qc * 512:(qc + 1) * 512]),
                  reads=[("dram", R.gsT.name)], writes=[("sbg", gi)])
            S.op("act", lambda e: e.activation(out=egc[gi][:, :], in_=gch[gi][:, :], func=AF.Exp, scale=-1.0),
                 reads=[("sbg", gi)], writes=[("sbeg", gi)])

        NI = len(items)
        loaded = set()
        for s in range(NI + 3):
            if s < NI:
                hh = items[s]["h"]
                for h2 in (hh, hh + 1):
                    if h2 < C.sb_heads and h2 not in loaded and (h2 == hh or items[s]["qc"] >= 2):
                        load_head(h2)
                        loaded.add(h2)
                if s == 0:
                    load_gate(items[0])
                f_z(items[s])
            if 0 <= s - 1 < NI:
                f_sp(items[s - 1])
            if 0 <= s - 2 < NI:
                f_later(items[s - 2])
            if 0 <= s - 3 < NI:
                f_att(items[s - 3])
                if items[s - 3]["kb"] == 0 and s - 2 < NI:
                    load_gate(items[s - 2])
            yield


def phase_sb(C):
    for _ in phase_sb_gen(C):
        pass


def phase_sgu(C):
    S, nc, I, R = C.S, C.nc, C.I, C.R
    with ExitStack() as es:
        def sb(name, shape, dt=F32):
            return es.enter_context(nc.sbuf_tensor(name, list(shape), dt))
        lng = sb("lng", [128, D])
        lnb = sb("lnb", [128, D])
        wsf = sb("wsf", [128, 16, 128])
        wsb = sb("wsb", [128, 16, 128], BF16)
        msk = sb("sg_msk", [128, 128])
        bsb = sb("bsb", [128, 16, 128])
        vb = [sb(f"vb{i}", [128, D]) for i in range(2)]
        vnb = [sb(f"vnb{i}", [128, D], BF16) for i in range(2)]
        stats = sb("sg_stats", [128, 8, 6])
        mv = sb("sg_mv", [128, 2])
        rs = sb("sg_rs", [128, 1])
        epsc = sb("sg_eps", [128, 1])
        ub = [sb(f"ub{i}", [128, 4, 128]) for i in range(8)]
        gb = [sb(f"gb{i}", [128, 4, 128]) for i in range(8)]
        mb = [sb(f"mb{i}", [128, 4, 128]) for i in range(2)]
        yb = [sb(f"yb{i}", [128, 4, 128], BF16) for i in range(2)]

        S.dma("sp", lambda e: e.dma_start(out=lng[:, :], in_=I.o_ln[0:1, :].partition_broadcast(128)), writes=["lng"])
        S.dma("sp", lambda e: e.dma_start(out=lnb[:, :], in_=I.o_ln[1:2, :].partition_broadcast(128)), writes=["lnb"])
        S.dma("sp", lambda e: e.dma_start(out=bsb[:, :, :].rearrange("p g t -> p (g t)"), in_=I.o_bs[0:1, :].partition_broadcast(128)), writes=["bsb"])
        S.dma("sp", lambda e: e.dma_start(out=wsf[:, :, :], in_=I.o_wsT[:, :, :]), writes=["wsf"])
        S.dma("sp", lambda e: e.dma_start(out=msk[:, :], in_=I.consts[:, M2:M2 + 128]), writes=["sg_msk"])
        S.op("pool", lambda e: e.memset(epsc[:, :], LN_EPS), writes=["sg_eps"])
        for g in range(16):
            S.op("dve", lambda e, g=g: e.tensor_tensor(out=wsb[:, g, :], in0=wsf[:, g, :], in1=msk[:, :], op=ALU.mult),
                 reads=["wsf", "sg_msk"], pwrites=["wsb"])

        uv = R.uT.rearrange("(fc p) t -> p fc t", p=128)
        gv = R.ggT.rearrange("(fc p) t -> p fc t", p=128)
        yv = R.y2T.rearrange("(fc p) t -> p fc t", p=128)
        rs_ = C.rs
        for c in range(16):
            tok = slice(c * 128, (c + 1) * 128)
            vi = c % 2
            v_, vn_ = vb[vi], vnb[vi]
            for q in range(4):
                S.dma("sp", lambda e, v_=v_, q=q, tok=tok: e.dma_start(out=v_[:, q * 1024:(q + 1) * 1024], in_=R.vtk[tok, q * 1024:(q + 1) * 1024]),
                      reads=[("dram", R.vtk.name)], pwrites=[("vb", vi)])
            S.op("act", lambda e, v_=v_: e.activation(out=v_[:, :], in_=v_[:, :], func=AF.Gelu),
                 reads=[("vb", vi)], writes=[("vb", vi)])
            for q in range(8):
                S.op("dve", lambda e, v_=v_, q=q: e.bn_stats(out=stats[:, q, :], in_=v_[:, q * 512:(q + 1) * 512]),
                     reads=[("vb", vi)], pwrites=["sg_stats"])
            S.op("dve", lambda e: e.bn_aggr(out=mv[:, :], in_=stats[:, :, :].rearrange("p a b -> p (a b)")),
                 reads=["sg_stats"], writes=["sg_mv"])
            S.op("act", lambda e: e.activation(out=rs[:, :], in_=mv[:, 1:2], func=AF.Sqrt, bias=epsc[:, 0:1], scale=1.0),
                 reads=["sg_mv", "sg_eps"], writes=["sg_rs"])
            S.op("dve", lambda e: e.reciprocal(out=rs[:, :], in_=rs[:, :]), reads=["sg_rs"], writes=["sg_rs"])
            S.op("dve", lambda e, v_=v_: e.tensor_scalar(out=v_[:, :], in0=v_[:, :], scalar1=mv[:, 0:1], scalar2=rs[:, 0:1],
                                                        op0=ALU.subtract, op1=ALU.mult),
                 reads=[("vb", vi), "sg_mv", "sg_rs"], writes=[("vb", vi)])
            S.op("pool", lambda e, v_=v_: e.tensor_tensor(out=v_[:, :], in0=v_[:, :], in1=lng[:, :], op=ALU.mult),
                 reads=[("vb", vi), "lng"], writes=[("vb", vi)])
            S.op("dve", lambda e, v_=v_, vn_=vn_: e.tensor_tensor(out=vn_[:, :], in0=v_[:, :], in1=lnb[:, :], op=ALU.add),
                 reads=[("vb", vi), "lnb"], writes=[("vnb", vi)])
            for f4 in range(8):
                u_, g_ = ub[f4], gb[f4]
                S.dma("sp", lambda e, u_=u_, f4=f4, tok=tok: e.dma_start(out=u_[:, :, :], in_=uv[:, f4 * 4:(f4 + 1) * 4, tok]),
                      reads=[("dram", R.uT.name)], writes=[("sgu", f4)])
                S.dma("sp", lambda e, g_=g_, f4=f4, tok=tok: e.dma_start(out=g_[:, :, :], in_=gv[:, f4 * 4:(f4 + 1) * 4, tok]),
                      reads=[("dram", R.ggT.name)], writes=[("sgg", f4)])
                S.op("act", lambda e, u_=u_: e.activation(out=u_[:, :, :], in_=u_[:, :, :], func=AF.Gelu),
                     reads=[("sgu", f4)], writes=[("sgu", f4)])
            for f4 in range(8):
                g_ = gb[f4]
                S.op("act", lambda e, g_=g_: e.activation(out=g_[:, :, :], in_=g_[:, :, :], func=AF.Silu),
                     reads=[("sgg", f4)], writes=[("sgg", f4)])
            for f4 in range(8):
                u_, g_ = ub[f4], gb[f4]
                bi = f4 % 2
                m_, y_ = mb[bi], yb[bi]
                pi = _rot(C.psum, rs_, "psm")
                ps = C.psum[pi]

                def mm(e, ps=ps, vn_=vn_, f4=f4):
                    ins = None
                    for k in range(4):
                        fc = f4 * 4 + k
                        ins = e.matmul(ps[:, k * 128:(k + 1) * 128], lhsT=vn_[:, fc * 128:(fc + 1) * 128],
                                       rhs=wsb[:, fc // 2, :], start=True, stop=True)
                    return ins
                S.op("pe", mm, reads=[("vnb", vi), "wsb"], writes=[("ps", pi)])
                g0 = f4 * 2
                S.op("dve", lambda e, ps=ps, m_=m_, g0=g0: e.tensor_tensor(
                    out=m_[:, :, :].rearrange("p (a b) t -> p a b t", a=2), in0=ps[:, :].rearrange("p (a b t) -> p a b t", a=2, b=2),
                    in1=bsb[:, g0:g0 + 2, :].unsqueeze(2).broadcast_to([128, 2, 2, 128]), op=ALU.add),
                    reads=[("ps", pi), "bsb"], writes=[("sgm", bi)])
                S.op("dve", lambda e, m_=m_, u_=u_: e.tensor_tensor(out=m_[:, :, :], in0=m_[:, :, :], in1=u_[:, :, :], op=ALU.mult),
                     reads=[("sgm", bi), ("sgu", f4)], writes=[("sgm", bi)])
                S.op("dve", lambda e, m_=m_, g_=g_, y_=y_: e.tensor_tensor(out=y_[:, :, :], in0=m_[:, :, :], in1=g_[:, :, :], op=ALU.mult),
                     reads=[("sgm", bi), ("sgg", f4)], writes=[("sgy", bi)])
                S.dma("sp", lambda e, y_=y_, f4=f4, tok=tok: e.dma_start(out=yv[:, f4 * 4:(f4 + 1) * 4, tok], in_=y_[:, :, :]),
                      reads=[("sgy", bi)], pwrites=[("dram", R.y2T.name)])


def build(phases=("p1", "p2", "p3", "p4", "p5", "p6", "p7"), debug_out=()):
    nc = bass.Bass("TRN2", target_bir_lowering=False)
    C = Ctx()
    C.nc = nc
    C.rs = {}

    def din(name, shape, dt=F32):
        return nc.dram_tensor(name, list(shape), dt, kind="ExternalInput").ap()

    def dscr(name, shape, dt=F32):
        kind = "ExternalOutput" if name in debug_out else "Internal"
        return nc.dram_tensor(name, list(shape), dt, kind=kind).ap()

    I = Ctx()
    C.I = I
    I.xT = din("xT", [D, T])
    I.ng = din("ng", [3, 128, 32])
    I.e_w_in = din("e_w_in", [D, EC])
    I.e_w_out = din("e_w_out", [D, D])
    I.o_w_in = din("o_w_in", [D, OC])
    I.o_w_out = din("o_w_out", [D, D])
    I.e_vec = din("e_vec", [10, 128, 16])
    I.e_mu_v = din("e_mu_v", [16, RW])
    I.e_gn = din("e_gn", [2, 16, RW])
    I.e_wdu = din("e_wdu", [128, RW])
    I.e_aup = din("e_aup", [128, RW])
    I.o_ln = din("o_ln", [2, D])
    I.o_wsT = din("o_wsT", [128, 16, 128])
    I.o_bs = din("o_bs", [1, 16 * 128])
    I.consts = din("consts", [128, NCONST])
    I.outT = nc.dram_tensor("outT", [D, T], F32, kind="ExternalOutput").ap()

    R = Ctx()
    C.R = R
    R.rT = dscr("s_rT", [RW, T])
    R.kT = dscr("s_kT", [RW, T])
    R.vtok = dscr("s_vtok", [T + 1, RW])
    R.loT = dscr("s_loT", [256, T])
    R.gtok = dscr("s_gtok", [T, RW])
    R.qT = dscr("s_qT", [SBW, T], BF16)
    R.ksT = dscr("s_ksT", [SBW, T], BF16)
    R.vstok = dscr("s_vstok", [T, SBW], BF16)
    R.gsT = dscr("s_gsT", [SBW, T])
    R.yT = dscr("s_yT", [D, T], BF16)
    R.x1T = dscr("s_x1T", [D, T])
    R.uT = dscr("s_uT", [D, T])
    R.vtk = dscr("s_vtk", [T, D])
    R.ggT = dscr("s_ggT", [D, T])
    R.y2T = dscr("s_y2T", [D, T], BF16)
    R.x2T = dscr("s_x2T", [D, T])

    with ExitStack() as es:
        S = Sched(nc, es)
        C.S = S

        def sb(name, shape, dt=F32):
            return es.enter_context(nc.sbuf_tensor(name, list(shape), dt))

        C.psum = [es.enter_context(nc.psum_tensor(f"ps{i}", [128, 512], F32)) for i in range(8)]
        C.ones_f = sb("ones_f", [128, 128])
        C.eps_rms = sb("eps_rms", [128, 1])
        C.ngs = sb("ngs", [128, 3, 32])
        S.op("pool", lambda e: e.memset(C.ones_f[:, :], 1.0), writes=["ones_f"])
        S.op("pool", lambda e: e.memset(C.eps_rms[:, :], RMS_EPS), writes=["eps_rms"])
        S.dma("sp", lambda e: e.dma_start(out=C.ngs[:, :, :], in_=I.ng.rearrange("a p c -> p a c")), writes=["ngs"])
        C.rstd = sb("rstd", [128, 512])

        from contextlib import contextmanager

        @contextmanager
        def proj_bufs(tag):
            with ExitStack() as es2:
                def sb2(name, shape, dt=F32):
                    return es2.enter_context(nc.sbuf_tensor(name + tag, list(shape), dt))
                C.hT = sb2("hT", [128, 32, 1024], BF16)
                C.wbuf = [sb2(f"wbuf{i}", [128, 32, 512], BF16) for i in range(2)]
                C.xst = [sb2(f"xst{i}", [128, XC, 512]) for i in range(2)]
                C.sqb = [sb2(f"sqb{i}", [128, 512]) for i in range(1)]
                C.stF = [sb2(f"stF{i}", [128, 1024]) for i in range(2)]
                C.stB = [sb2(f"stB{i}", [128, 1024], BF16) for i in range(2)]
                yield
                S.barrier()

        x1src = I.xT if C.skip_layer0 else R.x1T
        if "p1" in phases:
            with proj_bufs("a"):
                groups = [
                    dict(c0=0, n=RW, mode="F", dst=R.rT, dt=F32),
                    dict(c0=RW, n=RW, mode="F", dst=R.kT, dt=F32),
                    dict(c0=2 * RW, n=RW, mode="T", dst=R.vtok, dt=F32, row_off=1),
                    dict(c0=3 * RW, n=256, mode="F", dst=R.loT, dt=F32),
                    dict(c0=3 * RW + 256, n=RW, mode="T", dst=R.gtok, dt=F32),
                    dict(c0=4 * RW + 256, n=SBW, mode="F", dst=R.qT, dt=BF16),
                    dict(c0=4 * RW + 256 + SBW, n=SBW, mode="F", dst=R.ksT, dt=BF16),
                    dict(c0=4 * RW + 256 + 2 * SBW, n=SBW, mode="T", dst=R.vstok, dt=BF16),
                    dict(c0=4 * RW + 256 + 3 * SBW, n=SBW, mode="F", dst=R.gsT, dt=F32),
                ]
                g_sb = groups[5:9]
                g_rk = groups[0:5]
                if C.p1_groups is not None:
                    g_sb = [groups[i] for i in C.p1_groups if i >= 5]
                    g_rk = [groups[i] for i in C.p1_groups if i < 5]
                fuse_sb = C.fuse_sb and ("p3" in phases)
                for half in range(2):
                    t0 = half * 1024
                    norm_tokens(C, I.xT, C.ngs[:, 0, :], t0, 1024, hT=C.hT, hkey="hT")
                    project(C, C.hT, "hT", I.e_w_in, g_sb, t0)
                    if half == 1 and fuse_sb:
                        C.psm_banks = [4, 5]
                        C.npiece = 4
                        pg = project_gen(C, C.hT, "hT", I.e_w_in, g_rk, t0)
                        sg = phase_sb_gen(C)
                        pdone = sdone = False
                        while not (pdone and sdone):
                            if not sdone:
                                try:
                                    next(sg)
                                except StopIteration:
                                    sdone = True
                            if not pdone:
                                try:
                                    next(pg)
                                except StopIteration:
                                    pdone = True
                        C.psm_banks = list(range(8))
                        C.npiece = 1
                    else:
                        project(C, C.hT, "hT", I.e_w_in, g_rk, t0)
        if "p2" in phases:
            phase_rwkv(C)
            S.barrier()
        if "p3" in phases and not (C.fuse_sb and "p1" in phases):
            phase_sb(C)
            S.barrier()
        if "p4" in phases or "p5" in phases:
            with proj_bufs("b"):
                if "p4" in phases:
                    for half in range(2):
                        t0 = half * 1024
                        load_actT(C, R.yT, t0)
                        project(C, C.hT, "hT", I.e_w_out,
                                [dict(c0=0, n=D, mode="F", dst=R.x1T, dt=F32, resid=I.xT)], t0)
                if "p5" in phases:
                    groups = [
                        dict(c0=0, n=D, mode="F", dst=R.uT, dt=F32),
                        dict(c0=D, n=D, mode="T", dst=R.vtk, dt=F32),
                        dict(c0=2 * D, n=D, mode="F", dst=R.ggT, dt=F32),
                    ]
                    for half in range(2):
                        t0 = half * 1024
                        norm_tokens(C, x1src, C.ngs[:, 1, :], t0, 1024, hT=C.hT, hkey="hT")
                        project(C, C.hT, "hT", I.o_w_in, groups, t0)
        if "p6" in phases:
            phase_sgu(C)
            S.barrier()
        if "p7" in phases:
            with proj_bufs("c"):
                for half in range(2):
                    t0 = half * 1024
                    load_actT(C, R.y2T, t0)
                    project(C, C.hT, "hT", I.o_w_out,
                            [dict(c0=0, n=D, mode="F", dst=R.x2T, dt=F32, resid=x1src)], t0)
                norm_tokens(C, R.x2T, C.ngs[:, 2, :], 0, T, dstT=I.outT)

        S.final_wait("sp", S.all_tokens())
        S.emit()
    return nc


Ctx.p1_groups = None
Ctx.npiece = 1
Ctx.fuse_sb = True
Ctx.sb_per_group = 5
Ctx.psm_banks = list(range(8))
Ctx.rk_pairs = 16
Ctx.rk_stop = 0
Ctx.sb_heads = 16
Ctx.skip_layer0 = False


def _consts():
    p = np.arange(128)[:, None]
    f = np.arange(128)[None, :]
    c = np.zeros((128, NCONST), np.float32)
    c[:, M0:M0 + 128] = (p > f)
    c[:, M1:M1 + 128] = (f > p)
    c[:, M2:M2 + 128] = (f >= p)
    c[:, M3:M3 + 128] = (p >= f)
    c[:, IDO:IDO + 128] = (p == f)
    c[:, BDO:BDO + 128] = ((p // 64) == (f // 64))
    c[:, INDO:INDO + 2] = ((p // 64) == np.arange(2)[None, :])
    fb = np.arange(896)[None, :]
    c[:, MBIG:MBIG + 896] = ((fb - p) > 384)
    t = np.arange(2048)[None, :]
    c[:, RMASK:RMASK + 2048] = np.broadcast_to((t % 128) != 0, (128, 2048))
    return c


def make_shared(inp):
    f = lambda a: np.asarray(a, dtype=np.float32)
    sh = {}
    pc = lambda v: np.ascontiguousarray(f(v).reshape(-1, 128).T)
    ng = f(inp["norm_g"])
    sh["ng"] = np.stack([pc(ng[0]), pc(ng[1]), pc(inp["final_norm_g"])])
    sh["e_w_in"] = f(inp["e_w_in"])[0]
    sh["e_w_out"] = f(inp["e_w_out"])[0]
    sh["o_w_in"] = f(inp["o_w_in"])[0]
    sh["o_w_out"] = f(inp["o_w_out"])[0]
    mu = f(inp["e_shift_mu"])[0]
    ev = np.zeros((10, 128, 16), np.float32)
    ev[0] = pc(inp["e_w0"][0]); ev[1] = pc(inp["e_a0"][0]); ev[2] = pc(inp["e_k_k"][0])
    ev[3] = pc(inp["e_k_a"][0]); ev[4] = pc(inp["e_r_k"][0])
    ev[5] = pc(mu[0:2048]); ev[6] = pc(mu[2048:4096])
    ev[7, :, 0] = mu[6144:6272]; ev[8, :, 0] = mu[6272:6400]
    sh["e_vec"] = ev
    rep = lambda v: np.ascontiguousarray(np.tile(f(v).reshape(16, 1, 128), (1, 16, 1)).reshape(16, 2048))
    sh["e_mu_v"] = rep(mu[4096:6144])
    sh["e_gn"] = np.stack([rep(inp["e_gn_g"][0]), rep(inp["e_gn_b"][0])])
    sh["e_wdu"] = f(inp["e_w_decay_up"])[0]
    sh["e_aup"] = f(inp["e_a_up"])[0]
    sh["o_ln"] = np.stack([f(inp["o_ln_g"])[0], f(inp["o_ln_b"])[0]])
    sh["o_wsT"] = np.ascontiguousarray(f(inp["o_w_s"])[0].transpose(2, 0, 1))
    sh["o_bs"] = np.ascontiguousarray(f(inp["o_b_s"])[0].reshape(1, -1))
    sh["consts"] = _consts()
    return sh


_NC_CACHE = {}


def kernel(**inputs):
    x = np.asarray(inputs["x"], dtype=np.float32)
    sh = make_shared(inputs)
    if "nc" not in _NC_CACHE:
        _NC_CACHE["nc"] = build()
    nc = _NC_CACHE["nc"]
    in_maps = []
    for b in range(NB):
        m = dict(sh)
        m["xT"] = np.ascontiguousarray(x[b].T)
        in_maps.append(m)
    res = run_bass_kernel_spmd(nc, in_maps, core_ids=list(range(NB)))
    out = np.stack([np.ascontiguousarray(np.asarray(r["outT"]).T) for r in res.results])
    return out.astype(np.float32)
```

```python
import numpy as np
from contextlib import ExitStack
import concourse.bass as bass
import concourse.mybir as mybir
from concourse.bass_utils import run_bass_kernel_spmd

F32 = mybir.dt.float32
BF16 = mybir.dt.bfloat16
AF = mybir.ActivationFunctionType
ALU = mybir.AluOpType
AX = mybir.AxisListType

D = 4096
T = 2048
NB = 8
RW = 2048
SBW = 2048
EC = 16640
OC = 12288
RMS_EPS = 1e-6
GN_EPS = 64e-5
LN_EPS = 1e-5
M0, M1, M2, M3, IDO, BDO, INDO, MBIG, RMASK = 0, 128, 256, 384, 512, 640, 768, 772, 1668
NCONST = 1668 + 2048
NSLOT = 4


class Sched:
    ENG = ("pe", "act", "dve", "pool", "sp")

    def __init__(self, nc, es, n_dma=12):
        self.nc = nc
        self.semobj = {}
        self.cnt = {}
        for e in self.ENG:
            self.semobj["s_" + e] = es.enter_context(nc.semaphore("s_" + e))
            self.cnt[e] = 0
        self.dq = {}
        for q in ("sp", "pool"):
            names = []
            for i in range(n_dma):
                nm = f"d_{q}{i}"
                self.semobj[nm] = es.enter_context(nc.semaphore(nm))
                names.append(nm)
            self.dq[q] = {"names": names, "val": [0] * n_dma, "next": 0}
        self.ops = {e: [] for e in self.ENG}
        self.wm = {e: {} for e in self.ENG}
        self.lastw = {}
        self.readers = {}

    def _deps(self, reads, writes, pwrites):
        deps = []
        for k in reads:
            deps.extend(self.lastw.get(k, {}).items())
        for k in writes:
            deps.extend(self.lastw.get(k, {}).items())
            deps.extend(self.readers.get(k, {}).items())
        for k in pwrites:
            deps.extend(self.readers.get(k, {}).items())
        return deps

    def _waits(self, eng, deps):
        need = {}
        for (s, v) in deps:
            if eng == "pe" and s == "s_pe":
                continue
            if self.wm[eng].get(s, 0) >= v:
                continue
            if need.get(s, 0) < v:
                need[s] = v
        for s, v in need.items():
            self.wm[eng][s] = v
        return list(need.items())

    def _commit(self, tok, reads, writes, pwrites):
        s, v = tok
        for k in writes:
            self.lastw[k] = {s: v}
            self.readers[k] = {}
        for k in pwrites:
            d = self.lastw.setdefault(k, {})
            d[s] = max(d.get(s, 0), v)
        for k in reads:
            d = self.readers.setdefault(k, {})
            d[s] = max(d.get(s, 0), v)

    def op(self, eng, fn, reads=(), writes=(), pwrites=()):
        reads, writes, pwrites = tuple(reads), tuple(writes), tuple(pwrites)
        waits = self._waits(eng, self._deps(reads, writes, pwrites))
        self.cnt[eng] += 1
        tok = ("s_" + eng, self.cnt[eng])
        self.ops[eng].append((waits, fn, "s_" + eng, 1))
        self._commit(tok, reads, writes, pwrites)
        return tok

    def dma(self, q, fn, reads=(), writes=(), pwrites=()):
        reads, writes, pwrites = tuple(reads), tuple(writes), tuple(pwrites)
        d = self.dq[q]
        i = d["next"]
        d["next"] = (i + 1) % len(d["names"])
        deps = self._deps(reads, writes, pwrites)
        if d["val"][i] > 0:
            deps.append((d["names"][i], d["val"][i]))
        waits = self._waits(q, deps)
        d["val"][i] += 16
        tok = (d["names"][i], d["val"][i])
        self.ops[q].append((waits, fn, d["names"][i], 16))
        self._commit(tok, reads, writes, pwrites)
        return tok

    def barrier(self):
        toks = self.all_tokens()
        for e in self.ENG:
            self.final_wait(e, toks)

    def final_wait(self, eng, toks):
        waits = self._waits(eng, list(toks))
        self.ops[eng].append((waits, None, None, 0))

    def all_tokens(self):
        toks = []
        for e in self.ENG:
            if self.cnt[e]:
                toks.append(("s_" + e, self.cnt[e]))
        for q in self.dq.values():
            for nm, v in zip(q["names"], q["val"]):
                if v:
                    toks.append((nm, v))
        return toks

    def emit(self):
        nc = self.nc
        engmap = {}

        def run(name, eng):
            for (waits, fn, sem, inc) in self.ops[name]:
                for (s, v) in waits:
                    eng.wait_ge(self.semobj[s], v)
                if fn is None:
                    continue
                ins = fn(eng)
                ins.then_inc(self.semobj[sem], inc)

        with nc.Block() as block:
            @block.tensor
            def _(e):
                run("pe", e)

            @block.scalar
            def _(e):
                run("act", e)

            @block.vector
            def _(e):
                run("dve", e)

            @block.gpsimd
            def _(e):
                run("pool", e)

            @block.sync
            def _(e):
                run("sp", e)


class Ctx:
    pass


def _rot(lst, state, name):
    i = state.get(name, 0)
    state[name] = (i + 1) % len(lst)
    return i


def norm_tokens(C, srcT, gcol, t0, ntok, hT=None, hkey=None, dstT=None):
    S = C.S
    nc = C.nc
    src_v = srcT.rearrange("(c p) t -> p c t", p=128)
    dst_v = dstT.rearrange("(c p) t -> p c t", p=128) if dstT is not None else None
    for tt in range(ntok // 512):
        tok = slice(t0 + tt * 512, t0 + (tt + 1) * 512)
        ps = C.psum[_rot(C.psum, C.rs, "psn")]
        pskey = ("ps", C.psum.index(ps))
        for c4 in range(8):
            xi = _rot(C.xst, C.rs, "xst")
            xb = C.xst[xi]
            S.dma("sp", lambda e, xb=xb, c4=c4, tok=tok: e.dma_start(out=xb[:, :, :], in_=src_v[:, c4 * 4:(c4 + 1) * 4, tok]),
                  reads=[("dram", srcT.name)], writes=[("xst", xi)])
            for cc in range(4):
                c = c4 * 4 + cc
                qi = _rot(C.sqb, C.rs, "sqb")
                qb = C.sqb[qi]
                S.op("act", lambda e, qb=qb, xb=xb, cc=cc: e.activation(out=qb[:, :], in_=xb[:, cc, :], func=AF.Square),
                     reads=[("xst", xi)], writes=[("sqb", qi)])
                S.op("pe", lambda e, ps=ps, qb=qb, c=c: e.matmul(ps[:, :], lhsT=C.ones_f[:, :], rhs=qb[:, :], start=(c == 0), stop=(c == 31)),
                     reads=[("sqb", qi)], writes=[pskey])
        S.op("act", lambda e, ps=ps: e.activation(out=C.rstd[:, :], in_=ps[:, :], func=AF.Sqrt, bias=C.eps_rms[:, 0:1], scale=1.0 / D),
             reads=[pskey], writes=["rstd"])
        S.op("dve", lambda e: e.reciprocal(out=C.rstd[:, :], in_=C.rstd[:, :]), reads=["rstd"], writes=["rstd"])
        for c4 in range(8):
            xi = _rot(C.xst, C.rs, "xst")
            xb = C.xst[xi]
            S.dma("sp", lambda e, xb=xb, c4=c4, tok=tok: e.dma_start(out=xb[:, :, :], in_=src_v[:, c4 * 4:(c4 + 1) * 4, tok]),
                  reads=[("dram", srcT.name)], writes=[("xst", xi)])
            if hT is not None:
                for cc in range(4):
                    c = c4 * 4 + cc
                    S.op("dve", lambda e, xb=xb, cc=cc, c=c, tt=tt: e.scalar_tensor_tensor(
                        out=hT[:, c, tt * 512:(tt + 1) * 512], in0=xb[:, cc, :], scalar=gcol[:, c:c + 1],
                        in1=C.rstd[:, :], op0=ALU.mult, op1=ALU.mult),
                        reads=[("xst", xi), "rstd"], pwrites=[hkey])
            else:
                for cc in range(4):
                    c = c4 * 4 + cc
                    S.op("dve", lambda e, xb=xb, cc=cc, c=c: e.scalar_tensor_tensor(
                        out=xb[:, cc, :], in0=xb[:, cc, :], scalar=gcol[:, c:c + 1],
                        in1=C.rstd[:, :], op0=ALU.mult, op1=ALU.mult),
                        reads=[("xst", xi), "rstd"], writes=[("xst", xi)])
                S.dma("sp", lambda e, xb=xb, c4=c4, tok=tok: e.dma_start(out=dst_v[:, c4 * 4:(c4 + 1) * 4, tok], in_=xb[:, :, :]),
                      reads=[("xst", xi)], pwrites=[("dram", dstT.name)])


def project(C, actT, akey, w, groups, t0):
    S = C.S
    wv = w.rearrange("(kc p) n -> p kc n", p=128)
    for g in groups:
        dst = g["dst"]
        dkey = ("dram", dst.name)
        for ct in range(0, g["n"], 512):
            ncol = min(512, g["n"] - ct)
            col0 = g["c0"] + ct
            wi = _rot(C.wbuf, C.rs, "wbuf")
            wb = C.wbuf[wi]
            for hk in range(2):
                S.dma("pool", lambda e, wb=wb, col0=col0, ncol=ncol, hk=hk: e.dma_start(
                    out=wb[:, hk * 16:(hk + 1) * 16, 0:ncol], in_=wv[:, hk * 16:(hk + 1) * 16, col0:col0 + ncol]),
                    reads=[("dram", w.name)], writes=[("wbuf", wi, hk)])
            wkeys = [("wbuf", wi, 0), ("wbuf", wi, 1)]
            if g["mode"] == "F":
                for cc in range(ncol // 128):
                    st_list = C.stF if g["dt"] == F32 else C.stB
                    sname = "stF" if g["dt"] == F32 else "stB"
                    si = _rot(st_list, C.rs, sname)
                    st = st_list[si]
                    skey = (sname, si)
                    row0 = ct + cc * 128
                    if g.get("resid") is not None:
                        res = g["resid"]
                        S.dma("sp", lambda e, st=st, res=res, row0=row0: e.dma_start(
                            out=st[:, :], in_=res[row0:row0 + 128, t0:t0 + 1024]),
                            reads=[("dram", res.name)], writes=[skey])
                    for tt in range(2):
                        pi = _rot(C.psum, C.rs, "psm")
                        ps = C.psum[pi]

                        def mm(e, ps=ps, wb=wb, cc=cc, tt=tt):
                            ins = None
                            for kc in range(32):
                                ins = e.matmul(ps[:, :], lhsT=wb[:, kc, cc * 128:(cc + 1) * 128],
                                               rhs=actT[:, kc, tt * 512:(tt + 1) * 512],
                                               start=(kc == 0), stop=(kc == 31))
                            return ins
                        S.op("pe", mm, reads=wkeys + [akey], writes=[("ps", pi)])
                        if g.get("resid") is not None:
                            S.op("dve", lambda e, ps=ps, st=st, tt=tt: e.tensor_tensor(
                                out=st[:, tt * 512:(tt + 1) * 512], in0=ps[:, :], in1=st[:, tt * 512:(tt + 1) * 512], op=ALU.add),
                                reads=[("ps", pi), skey], pwrites=[skey])
                        else:
                            ev = "act" if _rot([0, 1], C.rs, "evsel") == 0 else "dve"
                            if ev == "act":
                                S.op("act", lambda e, ps=ps, st=st, tt=tt: e.copy(out=st[:, tt * 512:(tt + 1) * 512], in_=ps[:, :]),
                                     reads=[("ps", pi)], pwrites=[skey])
                            else:
                                S.op("dve", lambda e, ps=ps, st=st, tt=tt: e.tensor_copy(out=st[:, tt * 512:(tt + 1) * 512], in_=ps[:, :]),
                                     reads=[("ps", pi)], pwrites=[skey])
                    S.dma("sp", lambda e, st=st, dst=dst, row0=row0: e.dma_start(
                        out=dst[row0:row0 + 128, t0:t0 + 1024], in_=st[:, :]),
                        reads=[skey], pwrites=[dkey])
            else:
                for tb in range(8):
                    st_list = C.stF if g["dt"] == F32 else C.stB
                    sname = "stF" if g["dt"] == F32 else "stB"
                    si = _rot(st_list, C.rs, sname)
                    st = st_list[si]
                    skey = (sname, si)
                    pi = _rot(C.psum, C.rs, "psm")
                    ps = C.psum[pi]

                    def mm(e, ps=ps, wb=wb, tb=tb, ncol=ncol):
                        ins = None
                        for kc in range(32):
                            ins = e.matmul(ps[:, 0:ncol], lhsT=actT[:, kc, tb * 128:(tb + 1) * 128],
                                           rhs=wb[:, kc, 0:ncol], start=(kc == 0), stop=(kc == 31))
                        return ins
                    S.op("pe", mm, reads=wkeys + [akey], writes=[("ps", pi)])
                    ev = "act" if _rot([0, 1], C.rs, "evsel") == 0 else "dve"
                    if ev == "act":
                        S.op("act", lambda e, ps=ps, st=st, ncol=ncol: e.copy(out=st[:, 0:ncol], in_=ps[:, 0:ncol]),
                             reads=[("ps", pi)], writes=[skey])
                    else:
                        S.op("dve", lambda e, ps=ps, st=st, ncol=ncol: e.tensor_copy(out=st[:, 0:ncol], in_=ps[:, 0:ncol]),
                             reads=[("ps", pi)], writes=[skey])
                    ro = g.get("row_off", 0)
                    S.dma("sp", lambda e, st=st, dst=dst, tb=tb, ct=ct, ncol=ncol, ro=ro: e.dma_start(
                        out=dst[ro + t0 + tb * 128:ro + t0 + (tb + 1) * 128, ct:ct + ncol], in_=st[:, 0:ncol]),
                        reads=[skey], pwrites=[dkey])


def load_actT(C, srcT, t0):
    S = C.S
    v = srcT.rearrange("(c p) t -> p c t", p=128)
    for c8 in range(4):
        S.dma("sp", lambda e, c8=c8: e.dma_start(out=C.hT[:, c8 * 8:(c8 + 1) * 8, :], in_=v[:, c8 * 8:(c8 + 1) * 8, t0:t0 + 1024]),
              reads=[("dram", srcT.name)], pwrites=["hT"])


def phase_rwkv(C):
    S, nc, I, R = C.S, C.nc, C.I, C.R
    rs_ = C.rs
    NEG_E = -float(np.exp(-0.5))
    with ExitStack() as es:
        def sb(name, shape, dt=F32):
            return es.enter_context(nc.sbuf_tensor("rk_" + name, list(shape), dt))
        vec = sb("vec", [128, 10, 16])
        wdu = sb("wdu", [128, RW])
        aup = sb("aup", [128, RW])
        bd = sb("bd", [128, 128])
        ind = sb("ind", [128, 2])
        identf = sb("identf", [128, 128])
        identb = sb("identb", [128, 128], BF16)
        rmask = sb("rmask", [128, T])
        mk4 = sb("mk4", [128, 512])
        m04 = sb("m04", [128, 4, 128])
        twlo = sb("twlo", [128, T])
        alo = sb("alo", [128, T])
        gneps = sb("gneps", [128, 1])
        G = [sb(f"G{i}", [128, T]) for i in range(8)]
        bt = sb("bt", [128, T], BF16)
        kt = sb("kt", [128, T], BF16)
        QT = sb("QT", [128, 16, 2, 128], BF16)
        Btok = sb("Btok", [128, 16, 128], BF16)
        Ktok = sb("Ktok", [128, 16, 128], BF16)
        Vb = sb("Vb", [128, 16, 128], BF16)
        ATall = sb("ATall", [128, 32, 512], BF16)
        MTall = sb("MTall", [128, 32, 128], BF16)
        Xg = [[sb(f"Xg{s}{i}", [128, 4, 128], BF16) for i in range(2)] for s in range(NSLOT)]
        XTg = [[sb(f"XTg{s}{i}", [128, 4, 128], BF16) for i in range(2)] for s in range(NSLOT)]
        gl = sb("gl", [128, 16])
        bon = sb("bon", [128, 32])
        st1 = sb("st1", [128, 32])
        st2 = sb("st2", [128, 32])
        muv = sb("muv", [128, 128])
        gng = sb("gng", [128, 128])
        gnb = sb("gnb", [128, 128])
        Hf = sb("Hf", [128, 128])
        Hb = sb("Hb", [128, 128], BF16)
        th = sb("th", [128, 128])
        RHSb = [sb(f"RHSb{i}", [128, 128], BF16) for i in range(2)]
        Ub = [sb(f"Ub{i}", [128, 128], BF16) for i in range(2)]

        def ld(dst, src, key):
            S.dma("sp", lambda e: e.dma_start(out=dst, in_=src), writes=[key])
        ld(vec[:, :, :], I.e_vec.rearrange("a p j -> p a j"), "vec")
        ld(wdu[:, :], I.e_wdu[:, :], "wdu")
        ld(aup[:, :], I.e_aup[:, :], "aup")
        ld(bd[:, :], I.consts[:, BDO:BDO + 128], "bd")
        ld(ind[:, :], I.consts[:, INDO:INDO + 2], "ind")
        ld(identf[:, :], I.consts[:, IDO:IDO + 128], "identf")
        ld(rmask[:, :], I.consts[:, RMASK:RMASK + T], "rmask")
        for q in range(4):
            S.dma("sp", lambda e, q=q: e.dma_start(out=m04[:, q, :], in_=I.consts[:, M0:M0 + 128]), pwrites=["m04"])
            mo = M1 if q % 2 == 0 else M2
            S.dma("sp", lambda e, q=q, mo=mo: e.dma_start(out=mk4[:, q * 128:(q + 1) * 128], in_=I.consts[:, mo:mo + 128]), pwrites=["mk4"])
        S.op("dve", lambda e: e.tensor_copy(out=identb[:, :], in_=identf[:, :]), reads=["identf"], writes=["identb"])
        S.op("pool", lambda e: e.memset(G[7][0:1, :], 0.0), writes=["G7"])
        S.op("pool", lambda e: e.memset(gneps[:, :], GN_EPS), writes=["gneps"])
        S.op("pool", lambda e: e.memset(th[:, :], 0.0), writes=["th"])
        S.op("dve", lambda e: e.tensor_copy(out=C.psum[7][:, 0:128], in_=th[:, :]), reads=["th"], writes=[("ps", 7)])
        S.dma("sp", lambda e: e.dma_start(out=R.vtok[0:1, :], in_=G[7][0:1, :]), reads=["G7"], pwrites=[("dram", R.vtok.name)])

        def lerpF(src, skey, mu_col, tmp, tkey):
            S.op("dve", lambda e: e.tensor_tensor(out=tmp[:, 1:T], in0=src[:, 0:T - 1], in1=src[:, 1:T], op=ALU.subtract),
                 reads=[skey], pwrites=[tkey])
            S.op("dve", lambda e: e.tensor_scalar(out=tmp[:, 0:1], in0=src[:, 0:1], scalar1=-1.0, scalar2=None, op0=ALU.mult),
                 reads=[skey], pwrites=[tkey])
            S.op("dve", lambda e: e.scalar_tensor_tensor(out=src[:, :], in0=tmp[:, :], scalar=mu_col, in1=src[:, :],
                                                         op0=ALU.mult, op1=ALU.add),
                 reads=[skey, tkey, "vec"], writes=[skey])

        ld(twlo[:, :], R.loT[0:128, :], "twlo")
        ld(alo[:, :], R.loT[128:256, :], "alo")
        lerpF(twlo, "twlo", vec[:, 7, 0:1], G[2], "G2")
        lerpF(alo, "alo", vec[:, 8, 0:1], G[2], "G2")
        S.op("act", lambda e: e.activation(out=twlo[:, :], in_=twlo[:, :], func=AF.Tanh), reads=["twlo"], writes=["twlo"])

        def psr():
            i = 2 + _rot(list(range(5)), rs_, "rkps")
            return i, C.psum[i]

        def g3(t):
            return t[:, :].rearrange("p (c f) -> p c f", c=16)

        def g32(t):
            return t[:, :].rearrange("p (c f) -> p c f", c=32)

        def pair(j):
            jc = slice(j * 128, (j + 1) * 128)
            r_, k_, tmp, lw, a_, kkn, c_, ex = G
            S.dma("sp", lambda e: e.dma_start(out=r_[:, :], in_=R.rT[jc, :]), reads=[("dram", R.rT.name)], writes=["G0"])
            S.dma("sp", lambda e: e.dma_start(out=k_[:, :], in_=R.kT[jc, :]), reads=[("dram", R.kT.name)], writes=["G1"])
            lerpF(r_, "G0", vec[:, 5, j:j + 1], tmp, "G2")
            lerpF(k_, "G1", vec[:, 6, j:j + 1], tmp, "G2")
            for tc in range(4):
                ts_ = slice(tc * 512, (tc + 1) * 512)
                pi, ps = psr()
                S.op("pe", lambda e, ps=ps, ts_=ts_: e.matmul(ps[:, :], lhsT=wdu[:, jc], rhs=twlo[:, ts_], start=True, stop=True),
                     reads=["wdu", "twlo"], writes=[("ps", pi)])
                S.op("act", lambda e, ps=ps, ts_=ts_: e.activation(out=lw[:, ts_], in_=ps[:, :], func=AF.Sigmoid, bias=vec[:, 0, j:j + 1], scale=1.0),
                     reads=[("ps", pi), "vec"], pwrites=["G3"])
                pi, ps = psr()
                S.op("pe", lambda e, ps=ps, ts_=ts_: e.matmul(ps[:, :], lhsT=aup[:, jc], rhs=alo[:, ts_], start=True, stop=True),
                     reads=["aup", "alo"], writes=[("ps", pi)])
                S.op("act", lambda e, ps=ps, ts_=ts_: e.activation(out=a_[:, ts_], in_=ps[:, :], func=AF.Sigmoid, bias=vec[:, 1, j:j + 1], scale=1.0),
                     reads=[("ps", pi), "vec"], pwrites=["G4"])
            S.op("act", lambda e: e.activation(out=tmp[:, :], in_=k_[:, :], func=AF.Identity, scale=vec[:, 2, j:j + 1]),
                 reads=["G1", "vec"], writes=["G2"])
            S.op("act", lambda e: e.activation(out=ex[:, :], in_=tmp[:, :], func=AF.Square), reads=["G2"], writes=["G7"])
            for tc in range(4):
                ts_ = slice(tc * 512, (tc + 1) * 512)
                pi, ps = psr()
                S.op("pe", lambda e, ps=ps, ts_=ts_: e.matmul(ps[:, :], lhsT=bd[:, :], rhs=ex[:, ts_], start=True, stop=True),
                     reads=["bd", "G7"], writes=[("ps", pi)])
                S.op("act", lambda e, ps=ps, ts_=ts_: e.activation(out=c_[:, ts_], in_=ps[:, :], func=AF.Sqrt),
                     reads=[("ps", pi)], pwrites=["G6"])
            S.op("dve", lambda e: e.tensor_scalar(out=c_[:, :], in0=c_[:, :], scalar1=1e-12, scalar2=None, op0=ALU.max),
                 reads=["G6"], writes=["G6"])
            S.op("dve", lambda e: e.reciprocal(out=c_[:, :], in_=c_[:, :]), reads=["G6"], writes=["G6"])
            S.op("dve", lambda e: e.tensor_tensor(out=kkn[:, :], in0=tmp[:, :], in1=c_[:, :], op=ALU.mult),
                 reads=["G2", "G6"], writes=["G5"])
            S.op("dve", lambda e: e.tensor_scalar(out=tmp[:, :], in0=a_[:, :], scalar1=-1.0, scalar2=vec[:, 3, j:j + 1], op0=ALU.add, op1=ALU.mult),
                 reads=["G4", "vec"], writes=["G2"])
            S.op("dve", lambda e: e.scalar_tensor_tensor(out=k_[:, :], in0=tmp[:, :], scalar=1.0, in1=k_[:, :], op0=ALU.add, op1=ALU.mult),
                 reads=["G2", "G1"], writes=["G1"])
            S.op("dve", lambda e: e.scalar_tensor_tensor(out=tmp[:, :], in0=r_[:, :], scalar=vec[:, 4, j:j + 1], in1=k_[:, :], op0=ALU.mult, op1=ALU.mult),
                 reads=["G0", "G1", "vec"], writes=["G2"])
            pib, psb = psr()

            def mmb(e, psb=psb):
                ins = None
                for c in range(16):
                    ins = e.matmul(psb[:, 2 * c:2 * c + 2], lhsT=tmp[:, c * 128:(c + 1) * 128], rhs=ind[:, :], start=True, stop=True)
                return ins
            S.op("pe", mmb, reads=["G2", "ind"], writes=[("ps", pib)])
            S.op("act", lambda e, psb=psb: e.copy(out=bon[:, :], in_=psb[:, 0:32]), reads=[("ps", pib)], writes=["bon"])
            S.op("dve", lambda e: e.tensor_tensor(out=a_[:, :], in0=kkn[:, :], in1=a_[:, :], op=ALU.mult), reads=["G5", "G4"], writes=["G4"])
            S.op("dve", lambda e: e.tensor_tensor_scan(out=c_[:, :], data0=rmask[:, :], data1=lw[:, :], initial=0.0, op0=ALU.mult, op1=ALU.add),
                 reads=["rmask", "G3"], writes=["G6"])
            S.op("dve", lambda e: e.tensor_tensor(out=lw[:, :], in0=c_[:, :], in1=lw[:, :], op=ALU.subtract), reads=["G6", "G3"], writes=["G3"])
            S.op("act", lambda e: e.activation(out=ex[:, :], in_=c_[:, :], func=AF.Exp, scale=NEG_E), reads=["G6"], writes=["G7"])
            S.op("dve", lambda e: e.tensor_tensor(out=QT[:, :, 1, :], in0=g3(r_), in1=g3(ex), op=ALU.mult), reads=["G0", "G7"], pwrites=["QT"])
            S.op("dve", lambda e: e.tensor_copy(out=gl[:, :], in_=g3(ex)[:, :, 127]), reads=["G7"], writes=["gl"])
            S.op("act", lambda e: e.activation(out=ex[:, :], in_=lw[:, :], func=AF.Exp, scale=NEG_E), reads=["G3"], writes=["G7"])
            S.op("dve", lambda e: e.scalar_tensor_tensor(out=QT[:, :, 0, :], in0=g3(kkn), scalar=-1.0, in1=g3(ex), op0=ALU.mult, op1=ALU.mult),
                 reads=["G5", "G7"], pwrites=["QT"])
            S.op("act", lambda e: e.activation(out=ex[:, :], in_=c_[:, :], func=AF.Exp, scale=-NEG_E), reads=["G6"], writes=["G7"])
            S.op("dve", lambda e: e.tensor_tensor(out=bt[:, :], in0=a_[:, :], in1=ex[:, :], op=ALU.mult), reads=["G4", "G7"], writes=["bt"])
            S.op("dve", lambda e: e.tensor_tensor(out=kt[:, :], in0=k_[:, :], in1=ex[:, :], op=ALU.mult), reads=["G1", "G7"], writes=["kt"])
            if C.rk_stop == 1:
                return
            for (srcT_, skey, dstk, dkey) in ((bt, "bt", Btok, "Btok"), (kt, "kt", Ktok, "Ktok")):
                for hf in range(2):
                    pi, ps = psr()
                    psv = ps[:, :].bitcast(BF16)

                    def tr(e, psv=psv, srcT_=srcT_, hf=hf):
                        ins = None
                        for cc in range(8):
                            c = hf * 8 + cc
                            ins = e.transpose(out=psv[:, cc * 128:(cc + 1) * 128], in_=srcT_[:, c * 128:(c + 1) * 128], identity=identb[:, :])
                        return ins
                    S.op("pe", tr, reads=[skey, "identb"], writes=[("ps", pi)])
                    S.op("act", lambda e, psv=psv, dstk=dstk, hf=hf: e.copy(
                        out=dstk[:, hf * 8:(hf + 1) * 8, :].rearrange("p c f -> p (c f)"), in_=psv[:, :]),
                        reads=[("ps", pi)], pwrites=[dkey])
            vcur, vprev, sg = G[0], G[1], G[2]
            S.dma("sp", lambda e: e.dma_start(out=g3(vcur), in_=R.vtok[1:T + 1, jc].rearrange("(c p) f -> p c f", p=128)),
                  reads=[("dram", R.vtok.name)], writes=["G0"])
            S.dma("sp", lambda e: e.dma_start(out=g3(vprev), in_=R.vtok[0:T, jc].rearrange("(c p) f -> p c f", p=128)),
                  reads=[("dram", R.vtok.name)], writes=["G1"])
            S.dma("sp", lambda e: e.dma_start(out=g3(sg), in_=R.gtok[:, jc].rearrange("(c p) f -> p c f", p=128)),
                  reads=[("dram", R.gtok.name)], writes=["G2"])
            S.dma("sp", lambda e: e.dma_start(out=muv[:, :], in_=I.e_mu_v[j:j + 1, 0:128].partition_broadcast(128)), writes=["muv"])
            S.dma("sp", lambda e: e.dma_start(out=gng[:, :], in_=I.e_gn[0, j:j + 1, 0:128].partition_broadcast(128)), writes=["gng"])
            S.dma("sp", lambda e: e.dma_start(out=gnb[:, :], in_=I.e_gn[1, j:j + 1, 0:128].partition_broadcast(128)), writes=["gnb"])
            bc16 = lambda t: t[:, :].unsqueeze(1).broadcast_to([128, 16, 128])
            S.op("pool", lambda e: e.tensor_tensor(out=vprev[:, :], in0=vprev[:, :], in1=vcur[:, :], op=ALU.subtract), reads=["G0", "G1"], writes=["G1"])
            S.op("pool", lambda e: e.tensor_tensor(out=g3(vprev), in0=g3(vprev), in1=bc16(muv), op=ALU.mult), reads=["G1", "muv"], writes=["G1"])
            S.op("pool", lambda e: e.tensor_tensor(out=vcur[:, :], in0=vcur[:, :], in1=vprev[:, :], op=ALU.add), reads=["G0", "G1"], writes=["G0"])
            S.op("pool", lambda e: e.tensor_copy(out=Vb[:, :, :], in_=g3(vcur)), reads=["G0"], writes=["Vb"])
            S.op("act", lambda e: e.activation(out=sg[:, :], in_=sg[:, :], func=AF.Silu), reads=["G2"], writes=["G2"])

            if C.rk_stop == 2:
                return
            def group(cg, slot):
                p0 = cg * 4
                X, XT = Xg[slot], XTg[slot]
                psBs = [psr(), psr()]
                for q in range(4):
                    hp = q // 2
                    c = cg * 2 + q % 2
                    rows = slice(64 * hp, 64 * hp + 64)
                    cs_ = slice(c * 128, (c + 1) * 128)
                    pia = hp
                    psA = C.psum[pia]
                    pib, psB = psBs[hp]
                    cl = q % 2

                    def mma(e, psA=psA, rows=rows, cs_=cs_, c=c):
                        qv = QT[rows, c, :, :].rearrange("p a t -> p (a t)")
                        e.matmul(psA[:, 0:256], lhsT=bt[rows, cs_], rhs=qv, start=True, stop=True)
                        return e.matmul(psA[:, 256:512], lhsT=kt[rows, cs_], rhs=qv, start=True, stop=True)
                    S.op("pe", mma, reads=["bt", "kt", "QT"], writes=[("ps", pia)])
                    S.op("dve", lambda e, psA=psA, q=q: e.tensor_tensor(out=ATall[:, p0 + q, :], in0=psA[:, :], in1=mk4[:, :], op=ALU.mult),
                         reads=[("ps", pia), "mk4"], pwrites=[("ATall", cg)])
                    S.op("pe", lambda e, psB=psB, rows=rows, cs_=cs_, c=c, cl=cl: e.matmul(
                        psB[:, cl * 128:(cl + 1) * 128], lhsT=QT[rows, c, 0, :], rhs=bt[rows, cs_], start=True, stop=True),
                        reads=["QT", "bt"], pwrites=[("ps", pib)])
                for hp in range(2):
                    pib, psB = psBs[hp]
                    S.op("dve", lambda e, psB=psB, hp=hp: e.tensor_tensor(
                        out=X[0][:, 2 * hp:2 * hp + 2, :].rearrange("p q f -> p (q f)"), in0=psB[:, 0:256],
                        in1=m04[:, 0:2, :].rearrange("p q f -> p (q f)"), op=ALU.mult),
                        reads=[("ps", pib), "m04"], pwrites=[("Xg", slot, 0)])
                S.op("dve", lambda e: e.tensor_tensor(out=MTall[:, p0:p0 + 4, :], in0=ATall[:, p0:p0 + 4, 0:128],
                                                      in1=identb[:, :].unsqueeze(1).broadcast_to([128, 4, 128]), op=ALU.add),
                     reads=[("ATall", cg), "identb"], writes=[("MTall", cg)])
                yield
                xi = 0
                xtb = None
                xtkey = ("ATall", cg)

                def xt_ap(level_buf, q):
                    if level_buf is None:
                        return ATall[:, p0 + q, 0:128]
                    return XT[level_buf][:, q, :]
                for lvl in range(1, 7):
                    nxi = 1 - xi
                    piX, psX = psr()

                    def mmx(e, psX=psX, xi=xi, xtb=xtb):
                        ins = None
                        for q in range(4):
                            ins = e.matmul(psX[:, q * 128:(q + 1) * 128], lhsT=xt_ap(xtb, q), rhs=X[xi][:, q, :], start=True, stop=True)
                        return ins
                    S.op("pe", mmx, reads=[xtkey, ("Xg", slot, xi)], writes=[("ps", piX)])
                    if lvl < 6:
                        piT, psT = psr()
                        nxt = 0 if xtb is None else 1 - xtb

                        def mmt(e, psT=psT, xi=xi, xtb=xtb):
                            ins = None
                            for q in range(4):
                                ins = e.matmul(psT[:, q * 128:(q + 1) * 128], lhsT=X[xi][:, q, :], rhs=xt_ap(xtb, q), start=True, stop=True)
                            return ins
                        S.op("pe", mmt, reads=[xtkey, ("Xg", slot, xi)], writes=[("ps", piT)])
                    S.op("act", lambda e, psX=psX, nxi=nxi: e.copy(out=X[nxi][:, :, :].rearrange("p q f -> p (q f)"), in_=psX[:, :]),
                         reads=[("ps", piX)], writes=[("Xg", slot, nxi)])
                    if lvl < 6:
                        S.op("act", lambda e, psT=psT, nxt=nxt: e.copy(out=XT[nxt][:, :, :].rearrange("p q f -> p (q f)"), in_=psT[:, :]),
                             reads=[("ps", piT)], writes=[("XTg", slot, nxt)])
                        xtb = nxt
                        xtkey = ("XTg", slot, nxt)
                    xi = nxi
                    piD, psD = psr()

                    def mmd(e, psD=psD, xi=xi):
                        ins = None
                        for q in range(4):
                            ins = e.matmul(psD[:, q * 128:(q + 1) * 128], lhsT=X[xi][:, q, :], rhs=MTall[:, p0 + q, :], start=True, stop=True)
                        return ins
                    S.op("pe", mmd, reads=[("Xg", slot, xi), ("MTall", cg)], writes=[("ps", piD)])
                    S.op("dve", lambda e, psD=psD: e.tensor_tensor(out=MTall[:, p0:p0 + 4, :].rearrange("p q f -> p (q f)"),
                                                                   in0=MTall[:, p0:p0 + 4, :].rearrange("p q f -> p (q f)"), in1=psD[:, :], op=ALU.add),
                         reads=[("MTall", cg), ("ps", piD)], writes=[("MTall", cg)])
                    yield

            if C.rk_stop == 3:
                return
            for gp_ in range(8 // NSLOT):
                gens = [group(NSLOT * gp_ + s_, s_) for s_ in range(NSLOT)]
                for _step in range(7):
                    for g_ in gens:
                        next(g_)
            if C.rk_stop == 4:
                return
            S.op("pool", lambda e: e.memset(Hf[:, :], 0.0), writes=["Hf"])
            S.op("pool", lambda e: e.memset(Hb[:, :], 0.0), writes=["Hb"])
            Yall = G[1]
            psH = C.psum[7]
            for c in range(16):
                cg = c // 2
                bi = c % 2
                ix = [(c // 2) * 4 + hp * 2 + (c % 2) for hp in range(2)]
                piR, psR = psr()

                def mmr(e, psR=psR, c=c, ix=ix):
                    ins = e.matmul(psR[:, 0:128], lhsT=QT[:, c, 0, :], rhs=Hb[:, :], start=True, stop=False)
                    for hp in range(2):
                        vs = slice(64 * hp, 64 * hp + 64)
                        ins = e.matmul(psR[:, vs], lhsT=ATall[:, ix[hp], 256:384], rhs=Vb[:, c, vs], start=False, stop=(hp == 1))
                    return ins
                S.op("pe", mmr, reads=["QT", "Hb", ("ATall", cg), "Vb"], writes=[("ps", piR)])
                S.op("act", lambda e, psR=psR, bi=bi: e.copy(out=RHSb[bi][:, :], in_=psR[:, 0:128]), reads=[("ps", piR)], writes=[("RHSb", bi)])
                piU, psU = psr()

                def mmu(e, psU=psU, c=c, bi=bi, ix=ix):
                    ins = None
                    for hp in range(2):
                        vs = slice(64 * hp, 64 * hp + 64)
                        ins = e.matmul(psU[:, vs], lhsT=MTall[:, ix[hp], :], rhs=RHSb[bi][:, vs], start=True, stop=True)
                    return ins
                S.op("pe", mmu, reads=[("MTall", cg), ("RHSb", bi)], writes=[("ps", piU)])
                S.op("dve", lambda e, psU=psU, bi=bi: e.tensor_copy(out=Ub[bi][:, :], in_=psU[:, 0:128]), reads=[("ps", piU)], writes=[("Ub", bi)])

                def mmh(e, c=c, bi=bi):
                    ins = None
                    for hp in range(2):
                        vs = slice(64 * hp, 64 * hp + 64)
                        e.matmul(psH[vs, vs], lhsT=Btok[:, c, vs], rhs=Ub[bi][:, vs], start=True, stop=False)
                        ins = e.matmul(psH[vs, vs], lhsT=Ktok[:, c, vs], rhs=Vb[:, c, vs], start=False, stop=True)
                    return ins
                S.op("pe", mmh, reads=["Btok", "Ktok", "Vb", ("Ub", bi)], writes=[("ps", 7)])
                piY, psY = psr()

                def mmy(e, psY=psY, c=c, bi=bi, ix=ix):
                    ins = e.matmul(psY[:, 0:128], lhsT=QT[:, c, 1, :], rhs=Hb[:, :], start=True, stop=False)
                    for hp in range(2):
                        vs = slice(64 * hp, 64 * hp + 64)
                        e.matmul(psY[:, vs], lhsT=ATall[:, ix[hp], 128:256], rhs=Ub[bi][:, vs], start=False, stop=False)
                        ins = e.matmul(psY[:, vs], lhsT=ATall[:, ix[hp], 384:512], rhs=Vb[:, c, vs], start=False, stop=(hp == 1))
                    return ins
                S.op("pe", mmy, reads=["QT", "Hb", ("ATall", cg), "Vb", ("Ub", bi)], writes=[("ps", piY)])
                S.op("act", lambda e, psY=psY, c=c: e.copy(out=Yall[:, c * 128:(c + 1) * 128], in_=psY[:, 0:128]),
                     reads=[("ps", piY)], pwrites=["G1"])
                S.op("dve", lambda e: e.tensor_tensor(out=th[:, :], in0=psH[:, 0:128], in1=Hf[:, :], op=ALU.add),
                     reads=[("ps", 7), "Hf"], writes=["th"])
                S.op("dve", lambda e, c=c: e.tensor_scalar(out=Hb[:, :], in0=th[:, :], scalar1=gl[:, c:c + 1], scalar2=None, op0=ALU.mult),
                     reads=["th", "gl"], writes=["Hb"])
                S.op("act", lambda e, c=c: e.activation(out=Hf[:, :], in_=th[:, :], func=AF.Identity, scale=gl[:, c:c + 1]),
                     reads=["th", "gl"], writes=["Hf"])

            if C.rk_stop == 5:
                return
            Y3 = g32(Yall)
            sq3 = g32(G[3])
            bc64 = lambda t: t[:, :].unsqueeze(2).broadcast_to([128, 32, 64])
            S.op("dve", lambda e: e.tensor_reduce(out=st1[:, :], in_=Y3, axis=AX.X, op=ALU.add), reads=["G1", "G1"], writes=["st1"])
            S.op("dve", lambda e: e.tensor_scalar(out=st1[:, :], in0=st1[:, :], scalar1=1.0 / 64, scalar2=None, op0=ALU.mult), reads=["st1"], writes=["st1"])
            S.op("dve", lambda e: e.tensor_tensor(out=Y3, in0=Y3, in1=bc64(st1), op=ALU.subtract), reads=["G1", "st1"], writes=["G1"])
            S.op("act", lambda e: e.activation(out=G[3][:, :], in_=Yall[:, :], func=AF.Square), reads=["G1"], writes=["G3"])
            S.op("dve", lambda e: e.tensor_reduce(out=st2[:, :], in_=sq3, axis=AX.X, op=ALU.add), reads=["G3"], writes=["st2"])
            S.op("act", lambda e: e.activation(out=st2[:, :], in_=st2[:, :], func=AF.Sqrt, bias=gneps[:, 0:1], scale=1.0 / 64),
                 reads=["st2", "gneps"], writes=["st2"])
            S.op("dve", lambda e: e.reciprocal(out=st2[:, :], in_=st2[:, :]), reads=["st2"], writes=["st2"])
            S.op("dve", lambda e: e.tensor_tensor(out=Y3, in0=Y3, in1=bc64(st2), op=ALU.mult), reads=["G1", "st2"], writes=["G1"])
            S.op("dve", lambda e: e.tensor_tensor(out=g3(Yall), in0=g3(Yall), in1=bc16(gng), op=ALU.mult), reads=["G1", "gng"], writes=["G1"])
            S.op("dve", lambda e: e.tensor_tensor(out=g3(Yall), in0=g3(Yall), in1=bc16(gnb), op=ALU.add), reads=["G1", "gnb"], writes=["G1"])
            S.op("dve", lambda e: e.tensor_tensor(out=sq3, in0=g32(G[0]), in1=bc64(bon), op=ALU.mult), reads=["G0", "bon"], writes=["G3"])
            S.op("dve", lambda e: e.tensor_tensor(out=Yall[:, :], in0=Yall[:, :], in1=G[3][:, :], op=ALU.add), reads=["G1", "G3"], writes=["G1"])
            Ytok = kt[:, :].rearrange("p (c f) -> p c f", c=16)
            yTsb = bt
            S.op("dve", lambda e: e.tensor_tensor(out=Ytok, in0=g3(Yall), in1=g3(G[2]), op=ALU.mult), reads=["G1", "G2"], writes=["kt"])
            for hf in range(2):
                pi, ps = psr()
                psv = ps[:, :].bitcast(BF16)

                def tr(e, psv=psv, hf=hf):
                    ins = None
                    for cc in range(8):
                        c = hf * 8 + cc
                        ins = e.transpose(out=psv[:, cc * 128:(cc + 1) * 128], in_=Ytok[:, c, :], identity=identb[:, :])
                    return ins
                S.op("pe", tr, reads=["kt", "identb"], writes=[("ps", pi)])
                S.op("act", lambda e, psv=psv, hf=hf: e.copy(out=yTsb[:, hf * 1024:(hf + 1) * 1024], in_=psv[:, :]),
                     reads=[("ps", pi)], pwrites=["bt"])
            S.dma("sp", lambda e: e.dma_start(out=R.yT[jc, :], in_=yTsb[:, :]), reads=["bt"], pwrites=[("dram", R.yT.name)])

        for j_ in range(C.rk_pairs):
            pair(j_)


def phase_sb(C):
    S, nc, I, R = C.S, C.nc, C.I, C.R
    scale = 1.0 / float(np.sqrt(128.0))
    with ExitStack() as es:
        def sb(name, shape, dt=F32):
            return es.enter_context(nc.sbuf_tensor(name, list(shape), dt))
        qT = [sb(f"sbq{i}", [128, T], BF16) for i in range(2)]
        kT = [sb(f"sbk{i}", [128, T], BF16) for i in range(2)]
        kTs = [sb(f"sbks{i}", [128, T], BF16) for i in range(2)]
        vh = [sb(f"sbv{i}", [128, 16, 128], BF16) for i in range(2)]
        gs = [sb(f"sbg{i}", [128, T]) for i in range(2)]
        mtmp = sb("sb_mtmp", [128, 128])
        Lst = sb("sb_L", [128, 128], BF16)
        onb = sb("sb_ones", [128, 128], BF16)
        mbig = sb("sb_mbig", [128, 896])
        onec = sb("sb_onec", [128, 1])
        e_sb = [sb(f"sb_e{i}", [128, 512]) for i in range(2)]
        sp_sb = [sb(f"sb_sp{i}", [128, 512]) for i in range(3)]
        tmp_sb = [sb(f"sb_tmp{i}", [128, 512]) for i in range(2)]
        arg_sb = [sb(f"sb_arg{i}", [128, 512]) for i in range(2)]
        spb = [sb(f"sb_spb{i}", [128, 512], BF16) for i in range(3)]
        att = [sb(f"sb_att{i}", [128, 512], BF16) for i in range(2)]
        acc = [sb(f"sb_acc{i}", [128, 512]) for i in range(2)]
        accb = [sb(f"sb_accb{i}", [128, 512], BF16) for i in range(2)]
        ost = [sb(f"sb_ost{i}", [128, 512], BF16) for i in range(2)]

        S.dma("sp", lambda e: e.dma_start(out=mtmp[:, :], in_=I.consts[:, M0:M0 + 128]), writes=["sb_mtmp"])
        S.dma("sp", lambda e: e.dma_start(out=mbig[:, :], in_=I.consts[:, MBIG:MBIG + 896]), writes=["sb_mbig"])
        S.op("dve", lambda e: e.tensor_scalar(out=Lst[:, :], in0=mtmp[:, :], scalar1=-1.0, scalar2=None, op0=ALU.mult), reads=["sb_mtmp"], writes=["sb_L"])
        S.op("pool", lambda e: e.memset(onb[:, :], -1.0), writes=["sb_ones"])
        S.op("pool", lambda e: e.memset(onec[:, :], 1.0), writes=["sb_onec"])
        vv = R.vstok.rearrange("(kb p) c -> p kb c", p=128)
        PZ, PL, PO = (0, 1, 4), (2, 3), (6, 7)
        cnt = {"item": 0, "qcg": 0, "accb": 0}

        def f_z(it):
            kb, qc, hi, bz = it["kb"], it["qc"], it["hi"], it["bz"]
            pz = PZ[bz]
            S.op("pe", lambda e: e.matmul(C.psum[pz][:, :], lhsT=kT[hi][:, kb * 128:(kb + 1) * 128],
                                          rhs=qT[hi][:, qc * 512:(qc + 1) * 512], start=True, stop=True),
                 reads=[("sbq", hi), ("sbk", hi)], writes=[("ps", pz)])

        def f_sp(it):
            kb, qc, bz, be = it["kb"], it["qc"], it["bz"], it["be"]
            pz = PZ[bz]
            S.op("act", lambda e: e.activation(out=e_sb[be][:, :], in_=C.psum[pz][:, :], func=AF.Exp, scale=scale),
                 reads=[("ps", pz)], writes=[("sb_e", be)])
            S.op("act", lambda e: e.activation(out=sp_sb[bz][:, :], in_=e_sb[be][:, :], func=AF.Ln, bias=onec[:, 0:1], scale=1.0),
                 reads=[("sb_e", be), "sb_onec"], writes=[("sb_sp", bz)])
            off = kb - 4 * qc
            if off >= 0:
                m0 = 384 - off * 128
                S.op("dve", lambda e: e.tensor_tensor(out=sp_sb[bz][:, :], in0=sp_sb[bz][:, :], in1=mbig[:, m0:m0 + 512], op=ALU.mult),
                     reads=[("sb_sp", bz), "sb_mbig"], writes=[("sb_sp", bz)])
            S.op("act", lambda e: e.copy(out=spb[bz][:, :], in_=sp_sb[bz][:, :]),
                 reads=[("sb_sp", bz)], writes=[("sb_spb", bz)])

        def f_later(it):
            kb, qc, first, bz, bl, ai = it["kb"], it["qc"], it["first"], it["bz"], it["bl"], it["ai"]
            pz, pl = PZ[bz], PL[bl]
            hi = it["hi"]
            if kb == first:
                def mm(e):
                    e.matmul(C.psum[pl][:, :], lhsT=Lst[:, :], rhs=spb[bz][:, :], start=True, stop=False)
                    return e.matmul(C.psum[pl][:, :], lhsT=kTs[hi][:, kb * 128:(kb + 1) * 128], rhs=qT[hi][:, qc * 512:(qc + 1) * 512],
                                    start=False, stop=True)
                S.op("pe", mm, reads=["sb_L", ("sb_spb", bz), ("sbks", hi), ("sbq", hi)], writes=[("ps", pl)])
            else:
                abi = it["abi"]

                def mm(e):
                    e.matmul(C.psum[pl][:, :], lhsT=Lst[:, :], rhs=spb[bz][:, :], start=True, stop=False)
                    e.matmul(C.psum[pl][:, :], lhsT=onb[:, :], rhs=accb[abi][:, :], start=False, stop=False)
                    return e.matmul(C.psum[pl][:, :], lhsT=kTs[hi][:, kb * 128:(kb + 1) * 128], rhs=qT[hi][:, qc * 512:(qc + 1) * 512],
                                    start=False, stop=True)
                S.op("pe", mm, reads=["sb_L", "sb_ones", ("sb_spb", bz), ("sb_accb", abi), ("sbks", hi), ("sbq", hi)], writes=[("ps", pl)])
            S.op("dve", lambda e: e.tensor_tensor(out=arg_sb[bl][:, :], in0=C.psum[pl][:, :], in1=sp_sb[bz][:, :], op=ALU.subtract),
                 reads=[("sb_sp", bz), ("ps", pl)], writes=[("sb_arg", bl)])
            if kb > 0:
                if kb == first:
                    S.op("pool", lambda e: e.tensor_copy(out=acc[ai][:, :], in_=sp_sb[bz][:, :]),
                         reads=[("sb_sp", bz)], writes=[("sb_acc", ai)])
                else:
                    S.op("pool", lambda e: e.tensor_tensor(out=acc[ai][:, :], in0=acc[ai][:, :], in1=sp_sb[bz][:, :], op=ALU.add),
                         reads=[("sb_sp", bz), ("sb_acc", ai)], writes=[("sb_acc", ai)])
                nb = it["nabi"]
                S.op("dve", lambda e: e.tensor_copy(out=accb[nb][:, :], in_=acc[ai][:, :]),
                     reads=[("sb_acc", ai)], writes=[("sb_accb", nb)])

        def f_att(it):
            h, kb, qc, first, hi, bl, oi = it["h"], it["kb"], it["qc"], it["first"], it["hi"], it["bl"], it["oi"]
            po = PO[oi]
            S.op("act", lambda e: e.activation(out=att[bl][:, :], in_=arg_sb[bl][:, :], func=AF.Exp),
                 reads=[("sb_arg", bl)], writes=[("sb_att", bl)])
            off = kb - 4 * qc
            if off >= 0:
                m0 = 384 - off * 128
                S.op("pool", lambda e: e.tensor_tensor(out=att[bl][:, :], in0=att[bl][:, :], in1=mbig[:, m0:m0 + 512], op=ALU.mult),
                     reads=[("sb_att", bl), "sb_mbig"], writes=[("sb_att", bl)])
            S.op("pe", lambda e: e.matmul(C.psum[po][:, :], lhsT=vh[hi][:, kb, :], rhs=att[bl][:, :],
                                          start=(kb == first), stop=(kb == 0)),
                 reads=[("sbv", hi), ("sb_att", bl)], writes=[("ps", po)])
            if kb == 0:
                S.op("dve", lambda e: e.tensor_tensor(out=ost[oi][:, :], in0=C.psum[po][:, :], in1=gs[hi][:, qc * 512:(qc + 1) * 512], op=ALU.mult),
                     reads=[("ps", po), ("sbg", hi)], writes=[("sb_ost", oi)])
                S.dma("sp", lambda e: e.dma_start(out=R.yT[RW + h * 128:RW + (h + 1) * 128, qc * 512:(qc + 1) * 512], in_=ost[oi][:, :]),
                      reads=[("sb_ost", oi)], pwrites=[("dram", R.yT.name)])

        items = []
        n = 0
        for h in range(C.sb_heads):
            hi = h % 2
            for qc in range(4):
                first = 4 * qc + 3
                g = h * 4 + qc
                for kb in range(first, -1, -1):
                    items.append(dict(h=h, hi=hi, qc=qc, kb=kb, first=first, bz=n % 3, be=n % 2, bl=n % 2, ai=g % 2, oi=g % 2,
                                      abi=(n - 1) % 2, nabi=n % 2))
                    n += 1

        def load_head(h):
            hi = h % 2
            S.dma("sp", lambda e: e.dma_start(out=qT[hi][:, :], in_=R.qT[h * 128:(h + 1) * 128, :]),
                  reads=[("dram", R.qT.name)], writes=[("sbq", hi)])
            S.dma("sp", lambda e: e.dma_start(out=kT[hi][:, :], in_=R.ksT[h * 128:(h + 1) * 128, :]),
                  reads=[("dram", R.ksT.name)], writes=[("sbk", hi)])
            S.dma("sp", lambda e: e.dma_start(out=vh[hi][:, :, :], in_=vv[:, :, h * 128:(h + 1) * 128]),
                  reads=[("dram", R.vstok.name)], writes=[("sbv", hi)])
            S.dma("sp", lambda e: e.dma_start(out=gs[hi][:, :], in_=R.gsT[h * 128:(h + 1) * 128, :]),
                  reads=[("dram", R.gsT.name)], writes=[("sbg", hi)])
            S.op("act", lambda e: e.activation(out=gs[hi][:, :], in_=gs[hi][:, :], func=AF.Silu),
                 reads=[("sbg", hi)], writes=[("sbg", hi)])
            S.op("act", lambda e: e.activation(out=kTs[hi][:, :], in_=kT[hi][:, :], func=AF.Copy, scale=scale),
                 reads=[("sbk", hi)], writes=[("sbks", hi)])

        NI = len(items)
        loaded = set()
        for s in range(NI + 3):
            if s < NI:
                hh = items[s]["h"]
                for h2 in (hh, hh + 1):
                    if h2 < C.sb_heads and h2 not in loaded and (h2 == hh or items[s]["qc"] >= 2):
                        load_head(h2)
                        loaded.add(h2)
                f_z(items[s])
            if 0 <= s - 1 < NI:
                f_sp(items[s - 1])
            if 0 <= s - 2 < NI:
                f_later(items[s - 2])
            if 0 <= s - 3 < NI:
                f_att(items[s - 3])
            yield_point = None


def phase_sgu(C):
    S, nc, I, R = C.S, C.nc, C.I, C.R
    with ExitStack() as es:
        def sb(name, shape, dt=F32):
            return es.enter_context(nc.sbuf_tensor(name, list(shape), dt))
        lng = sb("lng", [128, D])
        lnb = sb("lnb", [128, D])
        wsf = sb("wsf", [128, 16, 128])
        wsb = sb("wsb", [128, 16, 128], BF16)
        msk = sb("sg_msk", [128, 128])
        bsb = sb("bsb", [128, 16, 128])
        vb = [sb(f"vb{i}", [128, D]) for i in range(2)]
        vnb = [sb(f"vnb{i}", [128, D], BF16) for i in range(2)]
        stats = sb("sg_stats", [128, 8, 6])
        mv = sb("sg_mv", [128, 2])
        rs = sb("sg_rs", [128, 1])
        epsc = sb("sg_eps", [128, 1])
        ub = [sb(f"ub{i}", [128, 4, 128]) for i in range(8)]
        gb = [sb(f"gb{i}", [128, 4, 128]) for i in range(8)]
        mb = [sb(f"mb{i}", [128, 4, 128]) for i in range(2)]
        yb = [sb(f"yb{i}", [128, 4, 128], BF16) for i in range(2)]

        S.dma("sp", lambda e: e.dma_start(out=lng[:, :], in_=I.o_ln[0:1, :].partition_broadcast(128)), writes=["lng"])
        S.dma("sp", lambda e: e.dma_start(out=lnb[:, :], in_=I.o_ln[1:2, :].partition_broadcast(128)), writes=["lnb"])
        S.dma("sp", lambda e: e.dma_start(out=bsb[:, :, :].rearrange("p g t -> p (g t)"), in_=I.o_bs[0:1, :].partition_broadcast(128)), writes=["bsb"])
        S.dma("sp", lambda e: e.dma_start(out=wsf[:, :, :], in_=I.o_wsT[:, :, :]), writes=["wsf"])
        S.dma("sp", lambda e: e.dma_start(out=msk[:, :], in_=I.consts[:, M2:M2 + 128]), writes=["sg_msk"])
        S.op("pool", lambda e: e.memset(epsc[:, :], LN_EPS), writes=["sg_eps"])
        for g in range(16):
            S.op("dve", lambda e, g=g: e.tensor_tensor(out=wsb[:, g, :], in0=wsf[:, g, :], in1=msk[:, :], op=ALU.mult),
                 reads=["wsf", "sg_msk"], pwrites=["wsb"])

        uv = R.uT.rearrange("(fc p) t -> p fc t", p=128)
        gv = R.ggT.rearrange("(fc p) t -> p fc t", p=128)
        yv = R.y2T.rearrange("(fc p) t -> p fc t", p=128)
        rs_ = C.rs
        for c in range(16):
            tok = slice(c * 128, (c + 1) * 128)
            vi = c % 2
            v_, vn_ = vb[vi], vnb[vi]
            for q in range(4):
                S.dma("sp", lambda e, v_=v_, q=q, tok=tok: e.dma_start(out=v_[:, q * 1024:(q + 1) * 1024], in_=R.vtk[tok, q * 1024:(q + 1) * 1024]),
                      reads=[("dram", R.vtk.name)], pwrites=[("vb", vi)])
            S.op("act", lambda e, v_=v_: e.activation(out=v_[:, :], in_=v_[:, :], func=AF.Gelu),
                 reads=[("vb", vi)], writes=[("vb", vi)])
            for q in range(8):
                S.op("dve", lambda e, v_=v_, q=q: e.bn_stats(out=stats[:, q, :], in_=v_[:, q * 512:(q + 1) * 512]),
                     reads=[("vb", vi)], pwrites=["sg_stats"])
            S.op("dve", lambda e: e.bn_aggr(out=mv[:, :], in_=stats[:, :, :].rearrange("p a b -> p (a b)")),
                 reads=["sg_stats"], writes=["sg_mv"])
            S.op("act", lambda e: e.activation(out=rs[:, :], in_=mv[:, 1:2], func=AF.Sqrt, bias=epsc[:, 0:1], scale=1.0),
                 reads=["sg_mv", "sg_eps"], writes=["sg_rs"])
            S.op("dve", lambda e: e.reciprocal(out=rs[:, :], in_=rs[:, :]), reads=["sg_rs"], writes=["sg_rs"])
            S.op("dve", lambda e, v_=v_: e.tensor_scalar(out=v_[:, :], in0=v_[:, :], scalar1=mv[:, 0:1], scalar2=rs[:, 0:1],
                                                        op0=ALU.subtract, op1=ALU.mult),
                 reads=[("vb", vi), "sg_mv", "sg_rs"], writes=[("vb", vi)])
            S.op("pool", lambda e, v_=v_: e.tensor_tensor(out=v_[:, :], in0=v_[:, :], in1=lng[:, :], op=ALU.mult),
                 reads=[("vb", vi), "lng"], writes=[("vb", vi)])
            S.op("dve", lambda e, v_=v_, vn_=vn_: e.tensor_tensor(out=vn_[:, :], in0=v_[:, :], in1=lnb[:, :], op=ALU.add),
                 reads=[("vb", vi), "lnb"], writes=[("vnb", vi)])
            for f4 in range(8):
                u_, g_ = ub[f4], gb[f4]
                S.dma("sp", lambda e, u_=u_, f4=f4, tok=tok: e.dma_start(out=u_[:, :, :], in_=uv[:, f4 * 4:(f4 + 1) * 4, tok]),
                      reads=[("dram", R.uT.name)], writes=[("sgu", f4)])
                S.dma("sp", lambda e, g_=g_, f4=f4, tok=tok: e.dma_start(out=g_[:, :, :], in_=gv[:, f4 * 4:(f4 + 1) * 4, tok]),
                      reads=[("dram", R.ggT.name)], writes=[("sgg", f4)])
                S.op("act", lambda e, u_=u_: e.activation(out=u_[:, :, :], in_=u_[:, :, :], func=AF.Gelu),
                     reads=[("sgu", f4)], writes=[("sgu", f4)])
            for f4 in range(8):
                g_ = gb[f4]
                S.op("act", lambda e, g_=g_: e.activation(out=g_[:, :, :], in_=g_[:, :, :], func=AF.Silu),
                     reads=[("sgg", f4)], writes=[("sgg", f4)])
            for f4 in range(8):
                u_, g_ = ub[f4], gb[f4]
                bi = f4 % 2
                m_, y_ = mb[bi], yb[bi]
                pi = _rot(C.psum, rs_, "psm")
                ps = C.psum[pi]

                def mm(e, ps=ps, vn_=vn_, f4=f4):
                    ins = None
                    for k in range(4):
                        fc = f4 * 4 + k
                        ins = e.matmul(ps[:, k * 128:(k + 1) * 128], lhsT=vn_[:, fc * 128:(fc + 1) * 128],
                                       rhs=wsb[:, fc // 2, :], start=True, stop=True)
                    return ins
                S.op("pe", mm, reads=[("vnb", vi), "wsb"], writes=[("ps", pi)])
                g0 = f4 * 2
                S.op("dve", lambda e, ps=ps, m_=m_, g0=g0: e.tensor_tensor(
                    out=m_[:, :, :].rearrange("p (a b) t -> p a b t", a=2), in0=ps[:, :].rearrange("p (a b t) -> p a b t", a=2, b=2),
                    in1=bsb[:, g0:g0 + 2, :].unsqueeze(2).broadcast_to([128, 2, 2, 128]), op=ALU.add),
                    reads=[("ps", pi), "bsb"], writes=[("sgm", bi)])
                S.op("dve", lambda e, m_=m_, u_=u_: e.tensor_tensor(out=m_[:, :, :], in0=m_[:, :, :], in1=u_[:, :, :], op=ALU.mult),
                     reads=[("sgm", bi), ("sgu", f4)], writes=[("sgm", bi)])
                S.op("dve", lambda e, m_=m_, g_=g_, y_=y_: e.tensor_tensor(out=y_[:, :, :], in0=m_[:, :, :], in1=g_[:, :, :], op=ALU.mult),
                     reads=[("sgm", bi), ("sgg", f4)], writes=[("sgy", bi)])
                S.dma("sp", lambda e, y_=y_, f4=f4, tok=tok: e.dma_start(out=yv[:, f4 * 4:(f4 + 1) * 4, tok], in_=y_[:, :, :]),
                      reads=[("sgy", bi)], pwrites=[("dram", R.y2T.name)])


def build(phases=("p1", "p2", "p3", "p4", "p5", "p6", "p7"), debug_out=()):
    nc = bass.Bass("TRN2", target_bir_lowering=False)
    C = Ctx()
    C.nc = nc
    C.rs = {}

    def din(name, shape, dt=F32):
        return nc.dram_tensor(name, list(shape), dt, kind="ExternalInput").ap()

    def dscr(name, shape, dt=F32):
        kind = "ExternalOutput" if name in debug_out else "Internal"
        return nc.dram_tensor(name, list(shape), dt, kind=kind).ap()

    I = Ctx()
    C.I = I
    I.xT = din("xT", [D, T])
    I.ng = din("ng", [3, 128, 32])
    I.e_w_in = din("e_w_in", [D, EC])
    I.e_w_out = din("e_w_out", [D, D])
    I.o_w_in = din("o_w_in", [D, OC])
    I.o_w_out = din("o_w_out", [D, D])
    I.e_vec = din("e_vec", [10, 128, 16])
    I.e_mu_v = din("e_mu_v", [16, RW])
    I.e_gn = din("e_gn", [2, 16, RW])
    I.e_wdu = din("e_wdu", [128, RW])
    I.e_aup = din("e_aup", [128, RW])
    I.o_ln = din("o_ln", [2, D])
    I.o_wsT = din("o_wsT", [128, 16, 128])
    I.o_bs = din("o_bs", [1, 16 * 128])
    I.consts = din("consts", [128, NCONST])
    I.outT = nc.dram_tensor("outT", [D, T], F32, kind="ExternalOutput").ap()

    R = Ctx()
    C.R = R
    R.rT = dscr("s_rT", [RW, T])
    R.kT = dscr("s_kT", [RW, T])
    R.vtok = dscr("s_vtok", [T + 1, RW])
    R.loT = dscr("s_loT", [256, T])
    R.gtok = dscr("s_gtok", [T, RW])
    R.qT = dscr("s_qT", [SBW, T], BF16)
    R.ksT = dscr("s_ksT", [SBW, T], BF16)
    R.vstok = dscr("s_vstok", [T, SBW], BF16)
    R.gsT = dscr("s_gsT", [SBW, T])
    R.yT = dscr("s_yT", [D, T], BF16)
    R.x1T = dscr("s_x1T", [D, T])
    R.uT = dscr("s_uT", [D, T])
    R.vtk = dscr("s_vtk", [T, D])
    R.ggT = dscr("s_ggT", [D, T])
    R.y2T = dscr("s_y2T", [D, T], BF16)
    R.x2T = dscr("s_x2T", [D, T])

    with ExitStack() as es:
        S = Sched(nc, es)
        C.S = S

        def sb(name, shape, dt=F32):
            return es.enter_context(nc.sbuf_tensor(name, list(shape), dt))

        C.psum = [es.enter_context(nc.psum_tensor(f"ps{i}", [128, 512], F32)) for i in range(8)]
        C.ones_f = sb("ones_f", [128, 128])
        C.eps_rms = sb("eps_rms", [128, 1])
        C.ngs = sb("ngs", [128, 3, 32])
        S.op("pool", lambda e: e.memset(C.ones_f[:, :], 1.0), writes=["ones_f"])
        S.op("pool", lambda e: e.memset(C.eps_rms[:, :], RMS_EPS), writes=["eps_rms"])
        S.dma("sp", lambda e: e.dma_start(out=C.ngs[:, :, :], in_=I.ng.rearrange("a p c -> p a c")), writes=["ngs"])
        C.rstd = sb("rstd", [128, 512])

        from contextlib import contextmanager

        @contextmanager
        def proj_bufs(tag):
            with ExitStack() as es2:
                def sb2(name, shape, dt=F32):
                    return es2.enter_context(nc.sbuf_tensor(name + tag, list(shape), dt))
                C.hT = sb2("hT", [128, 32, 1024], BF16)
                C.wbuf = [sb2(f"wbuf{i}", [128, 32, 512], BF16) for i in range(2)]
                C.xst = [sb2(f"xst{i}", [128, 4, 512]) for i in range(3)]
                C.sqb = [sb2(f"sqb{i}", [128, 512]) for i in range(2)]
                C.stF = [sb2(f"stF{i}", [128, 1024]) for i in range(3)]
                C.stB = [sb2(f"stB{i}", [128, 1024], BF16) for i in range(3)]
                yield
                S.barrier()

        x1src = I.xT if C.skip_layer0 else R.x1T
        if "p1" in phases:
            with proj_bufs("a"):
                groups = [
                    dict(c0=0, n=RW, mode="F", dst=R.rT, dt=F32),
                    dict(c0=RW, n=RW, mode="F", dst=R.kT, dt=F32),
                    dict(c0=2 * RW, n=RW, mode="T", dst=R.vtok, dt=F32, row_off=1),
                    dict(c0=3 * RW, n=256, mode="F", dst=R.loT, dt=F32),
                    dict(c0=3 * RW + 256, n=RW, mode="T", dst=R.gtok, dt=F32),
                    dict(c0=4 * RW + 256, n=SBW, mode="F", dst=R.qT, dt=BF16),
                    dict(c0=4 * RW + 256 + SBW, n=SBW, mode="F", dst=R.ksT, dt=BF16),
                    dict(c0=4 * RW + 256 + 2 * SBW, n=SBW, mode="T", dst=R.vstok, dt=BF16),
                    dict(c0=4 * RW + 256 + 3 * SBW, n=SBW, mode="F", dst=R.gsT, dt=F32),
                ]
                if C.p1_groups is not None:
                    groups = [groups[i] for i in C.p1_groups]
                for half in range(2):
                    t0 = half * 1024
                    norm_tokens(C, I.xT, C.ngs[:, 0, :], t0, 1024, hT=C.hT, hkey="hT")
                    project(C, C.hT, "hT", I.e_w_in, groups, t0)
        if "p2" in phases:
            phase_rwkv(C)
            S.barrier()
        if "p3" in phases:
            phase_sb(C)
            S.barrier()
        if "p4" in phases or "p5" in phases:
            with proj_bufs("b"):
                if "p4" in phases:
                    for half in range(2):
                        t0 = half * 1024
                        load_actT(C, R.yT, t0)
                        project(C, C.hT, "hT", I.e_w_out,
                                [dict(c0=0, n=D, mode="F", dst=R.x1T, dt=F32, resid=I.xT)], t0)
                if "p5" in phases:
                    groups = [
                        dict(c0=0, n=D, mode="F", dst=R.uT, dt=F32),
                        dict(c0=D, n=D, mode="T", dst=R.vtk, dt=F32),
                        dict(c0=2 * D, n=D, mode="F", dst=R.ggT, dt=F32),
                    ]
                    for half in range(2):
                        t0 = half * 1024
                        norm_tokens(C, x1src, C.ngs[:, 1, :], t0, 1024, hT=C.hT, hkey="hT")
                        project(C, C.hT, "hT", I.o_w_in, groups, t0)
        if "p6" in phases:
            phase_sgu(C)
            S.barrier()
        if "p7" in phases:
            with proj_bufs("c"):
                for half in range(2):
                    t0 = half * 1024
                    load_actT(C, R.y2T, t0)
                    project(C, C.hT, "hT", I.o_w_out,
                            [dict(c0=0, n=D, mode="F", dst=R.x2T, dt=F32, resid=x1src)], t0)
                norm_tokens(C, R.x2T, C.ngs[:, 2, :], 0, T, dstT=I.outT)

        S.final_wait("sp", S.all_tokens())
        S.emit()
    return nc


Ctx.p1_groups = None
Ctx.rk_pairs = 16
Ctx.rk_stop = 0
Ctx.sb_heads = 16
Ctx.skip_layer0 = False


def _consts():
    p = np.arange(128)[:, None]
    f = np.arange(128)[None, :]
    c = np.zeros((128, NCONST), np.float32)
    c[:, M0:M0 + 128] = (p > f)
    c[:, M1:M1 + 128] = (f > p)
    c[:, M2:M2 + 128] = (f >= p)
    c[:, M3:M3 + 128] = (p >= f)
    c[:, IDO:IDO + 128] = (p == f)
    c[:, BDO:BDO + 128] = ((p // 64) == (f // 64))
    c[:, INDO:INDO + 2] = ((p // 64) == np.arange(2)[None, :])
    fb = np.arange(896)[None, :]
    c[:, MBIG:MBIG + 896] = ((fb - p) > 384)
    t = np.arange(2048)[None, :]
    c[:, RMASK:RMASK + 2048] = np.broadcast_to((t % 128) != 0, (128, 2048))
    return c


def make_shared(inp):
    f = lambda a: np.asarray(a, dtype=np.float32)
    sh = {}
    pc = lambda v: np.ascontiguousarray(f(v).reshape(-1, 128).T)
    ng = f(inp["norm_g"])
    sh["ng"] = np.stack([pc(ng[0]), pc(ng[1]), pc(inp["final_norm_g"])])
    sh["e_w_in"] = f(inp["e_w_in"])[0]
    sh["e_w_out"] = f(inp["e_w_out"])[0]
    sh["o_w_in"] = f(inp["o_w_in"])[0]
    sh["o_w_out"] = f(inp["o_w_out"])[0]
    mu = f(inp["e_shift_mu"])[0]
    ev = np.zeros((10, 128, 16), np.float32)
    ev[0] = pc(inp["e_w0"][0]); ev[1] = pc(inp["e_a0"][0]); ev[2] = pc(inp["e_k_k"][0])
    ev[3] = pc(inp["e_k_a"][0]); ev[4] = pc(inp["e_r_k"][0])
    ev[5] = pc(mu[0:2048]); ev[6] = pc(mu[2048:4096])
    ev[7, :, 0] = mu[6144:6272]; ev[8, :, 0] = mu[6272:6400]
    sh["e_vec"] = ev
    rep = lambda v: np.ascontiguousarray(np.tile(f(v).reshape(16, 1, 128), (1, 16, 1)).reshape(16, 2048))
    sh["e_mu_v"] = rep(mu[4096:6144])
    sh["e_gn"] = np.stack([rep(inp["e_gn_g"][0]), rep(inp["e_gn_b"][0])])
    sh["e_wdu"] = f(inp["e_w_decay_up"])[0]
    sh["e_aup"] = f(inp["e_a_up"])[0]
    sh["o_ln"] = np.stack([f(inp["o_ln_g"])[0], f(inp["o_ln_b"])[0]])
    sh["o_wsT"] = np.ascontiguousarray(f(inp["o_w_s"])[0].transpose(2, 0, 1))
    sh["o_bs"] = np.ascontiguousarray(f(inp["o_b_s"])[0].reshape(1, -1))
    sh["consts"] = _consts()
    return sh


_NC_CACHE = {}


def kernel(**inputs):
    x = np.asarray(inputs["x"], dtype=np.float32)
    sh = make_shared(inputs)
    if "nc" not in _NC_CACHE:
        _NC_CACHE["nc"] = build()
    nc = _NC_CACHE["nc"]
    in_maps = []
    for b in range(NB):
        m = dict(sh)
        m["xT"] = np.ascontiguousarray(x[b].T)
        in_maps.append(m)
    res = run_bass_kernel_spmd(nc, in_maps, core_ids=list(range(NB)))
    out = np.stack([np.ascontiguousarray(np.asarray(r["outT"]).T) for r in res.results])
    return out.astype(np.float32)
```

```python
import numpy as np
from contextlib import ExitStack
import concourse.bass as bass
import concourse.mybir as mybir
from concourse.bass_utils import run_bass_kernel_spmd

F32 = mybir.dt.float32
BF16 = mybir.dt.bfloat16
AF = mybir.ActivationFunctionType
ALU = mybir.AluOpType
AX = mybir.AxisListType

D = 4096
T = 2048
NB = 8
RW = 2048
SBW = 2048
EC = 16640
OC = 12288
RMS_EPS = 1e-6
GN_EPS = 64e-5
LN_EPS = 1e-5
M0, M1, M2, M3, IDO, BDO, INDO, MBIG, RMASK = 0, 128, 256, 384, 512, 640, 768, 772, 1668
NCONST = 1668 + 2048
NSLOT = 4


class Sched:
    ENG = ("pe", "act", "dve", "pool", "sp")

    def __init__(self, nc, es, n_dma=12):
        self.nc = nc
        self.semobj = {}
        self.cnt = {}
        for e in self.ENG:
            self.semobj["s_" + e] = es.enter_context(nc.semaphore("s_" + e))
            self.cnt[e] = 0
        self.dq = {}
        for q in ("sp", "pool"):
            names = []
            for i in range(n_dma):
                nm = f"d_{q}{i}"
                self.semobj[nm] = es.enter_context(nc.semaphore(nm))
                names.append(nm)
            self.dq[q] = {"names": names, "val": [0] * n_dma, "next": 0}
        self.ops = {e: [] for e in self.ENG}
        self.wm = {e: {} for e in self.ENG}
        self.lastw = {}
        self.readers = {}

    def _deps(self, reads, writes, pwrites):
        deps = []
        for k in reads:
            deps.extend(self.lastw.get(k, {}).items())
        for k in writes:
            deps.extend(self.lastw.get(k, {}).items())
            deps.extend(self.readers.get(k, {}).items())
        for k in pwrites:
            deps.extend(self.readers.get(k, {}).items())
        return deps

    def _waits(self, eng, deps):
        need = {}
        for (s, v) in deps:
            if eng == "pe" and s == "s_pe":
                continue
            if self.wm[eng].get(s, 0) >= v:
                continue
            if need.get(s, 0) < v:
                need[s] = v
        for s, v in need.items():
            self.wm[eng][s] = v
        return list(need.items())

    def _commit(self, tok, reads, writes, pwrites):
        s, v = tok
        for k in writes:
            self.lastw[k] = {s: v}
            self.readers[k] = {}
        for k in pwrites:
            d = self.lastw.setdefault(k, {})
            d[s] = max(d.get(s, 0), v)
        for k in reads:
            d = self.readers.setdefault(k, {})
            d[s] = max(d.get(s, 0), v)

    def op(self, eng, fn, reads=(), writes=(), pwrites=()):
        reads, writes, pwrites = tuple(reads), tuple(writes), tuple(pwrites)
        waits = self._waits(eng, self._deps(reads, writes, pwrites))
        self.cnt[eng] += 1
        tok = ("s_" + eng, self.cnt[eng])
        self.ops[eng].append((waits, fn, "s_" + eng, 1))
        self._commit(tok, reads, writes, pwrites)
        return tok

    def dma(self, q, fn, reads=(), writes=(), pwrites=()):
        reads, writes, pwrites = tuple(reads), tuple(writes), tuple(pwrites)
        d = self.dq[q]
        i = d["next"]
        d["next"] = (i + 1) % len(d["names"])
        deps = self._deps(reads, writes, pwrites)
        if d["val"][i] > 0:
            deps.append((d["names"][i], d["val"][i]))
        waits = self._waits(q, deps)
        d["val"][i] += 16
        tok = (d["names"][i], d["val"][i])
        self.ops[q].append((waits, fn, d["names"][i], 16))
        self._commit(tok, reads, writes, pwrites)
        return tok

    def barrier(self):
        toks = self.all_tokens()
        for e in self.ENG:
            self.final_wait(e, toks)

    def final_wait(self, eng, toks):
        waits = self._waits(eng, list(toks))
        self.ops[eng].append((waits, None, None, 0))

    def all_tokens(self):
        toks = []
        for e in self.ENG:
            if self.cnt[e]:
                toks.append(("s_" + e, self.cnt[e]))
        for q in self.dq.values():
            for nm, v in zip(q["names"], q["val"]):
                if v:
                    toks.append((nm, v))
        return toks

    def emit(self):
        nc = self.nc
        engmap = {}

        def run(name, eng):
            for (waits, fn, sem, inc) in self.ops[name]:
                for (s, v) in waits:
                    eng.wait_ge(self.semobj[s], v)
                if fn is None:
                    continue
                ins = fn(eng)
                ins.then_inc(self.semobj[sem], inc)

        with nc.Block() as block:
            @block.tensor
            def _(e):
                run("pe", e)

            @block.scalar
            def _(e):
                run("act", e)

            @block.vector
            def _(e):
                run("dve", e)

            @block.gpsimd
            def _(e):
                run("pool", e)

            @block.sync
            def _(e):
                run("sp", e)


class Ctx:
    pass


def _rot(lst, state, name):
    i = state.get(name, 0)
    state[name] = (i + 1) % len(lst)
    return i


def norm_tokens(C, srcT, gcol, t0, ntok, hT=None, hkey=None, dstT=None):
    S = C.S
    nc = C.nc
    src_v = srcT.rearrange("(c p) t -> p c t", p=128)
    dst_v = dstT.rearrange("(c p) t -> p c t", p=128) if dstT is not None else None
    for tt in range(ntok // 512):
        tok = slice(t0 + tt * 512, t0 + (tt + 1) * 512)
        ps = C.psum[_rot(C.psum, C.rs, "psn")]
        pskey = ("ps", C.psum.index(ps))
        for c4 in range(8):
            xi = _rot(C.xst, C.rs, "xst")
            xb = C.xst[xi]
            S.dma("sp", lambda e, xb=xb, c4=c4, tok=tok: e.dma_start(out=xb[:, :, :], in_=src_v[:, c4 * 4:(c4 + 1) * 4, tok]),
                  reads=[("dram", srcT.name)], writes=[("xst", xi)])
            for cc in range(4):
                c = c4 * 4 + cc
                qi = _rot(C.sqb, C.rs, "sqb")
                qb = C.sqb[qi]
                S.op("act", lambda e, qb=qb, xb=xb, cc=cc: e.activation(out=qb[:, :], in_=xb[:, cc, :], func=AF.Square),
                     reads=[("xst", xi)], writes=[("sqb", qi)])
                S.op("pe", lambda e, ps=ps, qb=qb, c=c: e.matmul(ps[:, :], lhsT=C.ones_f[:, :], rhs=qb[:, :], start=(c == 0), stop=(c == 31)),
                     reads=[("sqb", qi), "ones_f"], writes=[pskey])
        S.op("act", lambda e, ps=ps: e.activation(out=C.rstd[:, :], in_=ps[:, :], func=AF.Sqrt, bias=C.eps_rms[:, 0:1], scale=1.0 / D),
             reads=[pskey, "eps_rms"], writes=["rstd"])
        S.op("dve", lambda e: e.reciprocal(out=C.rstd[:, :], in_=C.rstd[:, :]), reads=["rstd"], writes=["rstd"])
        for c4 in range(8):
            xi = _rot(C.xst, C.rs, "xst")
            xb = C.xst[xi]
            S.dma("sp", lambda e, xb=xb, c4=c4, tok=tok: e.dma_start(out=xb[:, :, :], in_=src_v[:, c4 * 4:(c4 + 1) * 4, tok]),
                  reads=[("dram", srcT.name)], writes=[("xst", xi)])
            if hT is not None:
                for cc in range(4):
                    c = c4 * 4 + cc
                    S.op("dve", lambda e, xb=xb, cc=cc, c=c, tt=tt: e.scalar_tensor_tensor(
                        out=hT[:, c, tt * 512:(tt + 1) * 512], in0=xb[:, cc, :], scalar=gcol[:, c:c + 1],
                        in1=C.rstd[:, :], op0=ALU.mult, op1=ALU.mult),
                        reads=[("xst", xi), "rstd", "ngs"], pwrites=[hkey])
            else:
                for cc in range(4):
                    c = c4 * 4 + cc
                    S.op("dve", lambda e, xb=xb, cc=cc, c=c: e.scalar_tensor_tensor(
                        out=xb[:, cc, :], in0=xb[:, cc, :], scalar=gcol[:, c:c + 1],
                        in1=C.rstd[:, :], op0=ALU.mult, op1=ALU.mult),
                        reads=[("xst", xi), "rstd", "ngs"], writes=[("xst", xi)])
                S.dma("sp", lambda e, xb=xb, c4=c4, tok=tok: e.dma_start(out=dst_v[:, c4 * 4:(c4 + 1) * 4, tok], in_=xb[:, :, :]),
                      reads=[("xst", xi)], pwrites=[("dram", dstT.name)])


def project(C, actT, akey, w, groups, t0):
    S = C.S
    wv = w.rearrange("(kc p) n -> p kc n", p=128)
    for g in groups:
        dst = g["dst"]
        dkey = ("dram", dst.name)
        for ct in range(0, g["n"], 512):
            ncol = min(512, g["n"] - ct)
            col0 = g["c0"] + ct
            wi = _rot(C.wbuf, C.rs, "wbuf")
            wb = C.wbuf[wi]
            for hk in range(2):
                S.dma("pool", lambda e, wb=wb, col0=col0, ncol=ncol, hk=hk: e.dma_start(
                    out=wb[:, hk * 16:(hk + 1) * 16, 0:ncol], in_=wv[:, hk * 16:(hk + 1) * 16, col0:col0 + ncol]),
                    reads=[("dram", w.name)], writes=[("wbuf", wi, hk)])
            wkeys = [("wbuf", wi, 0), ("wbuf", wi, 1)]
            if g["mode"] == "F":
                for cc in range(ncol // 128):
                    st_list = C.stF if g["dt"] == F32 else C.stB
                    sname = "stF" if g["dt"] == F32 else "stB"
                    si = _rot(st_list, C.rs, sname)
                    st = st_list[si]
                    skey = (sname, si)
                    row0 = ct + cc * 128
                    if g.get("resid") is not None:
                        res = g["resid"]
                        S.dma("sp", lambda e, st=st, res=res, row0=row0: e.dma_start(
                            out=st[:, :], in_=res[row0:row0 + 128, t0:t0 + 1024]),
                            reads=[("dram", res.name)], writes=[skey])
                    for tt in range(2):
                        pi = _rot(C.psum, C.rs, "psm")
                        ps = C.psum[pi]

                        def mm(e, ps=ps, wb=wb, cc=cc, tt=tt):
                            ins = None
                            for kc in range(32):
                                ins = e.matmul(ps[:, :], lhsT=wb[:, kc, cc * 128:(cc + 1) * 128],
                                               rhs=actT[:, kc, tt * 512:(tt + 1) * 512],
                                               start=(kc == 0), stop=(kc == 31))
                            return ins
                        S.op("pe", mm, reads=wkeys + [akey], writes=[("ps", pi)])
                        if g.get("resid") is not None:
                            S.op("dve", lambda e, ps=ps, st=st, tt=tt: e.tensor_tensor(
                                out=st[:, tt * 512:(tt + 1) * 512], in0=ps[:, :], in1=st[:, tt * 512:(tt + 1) * 512], op=ALU.add),
                                reads=[("ps", pi), skey], pwrites=[skey])
                        else:
                            ev = "act" if _rot([0, 1], C.rs, "evsel") == 0 else "dve"
                            if ev == "act":
                                S.op("act", lambda e, ps=ps, st=st, tt=tt: e.copy(out=st[:, tt * 512:(tt + 1) * 512], in_=ps[:, :]),
                                     reads=[("ps", pi)], pwrites=[skey])
                            else:
                                S.op("dve", lambda e, ps=ps, st=st, tt=tt: e.tensor_copy(out=st[:, tt * 512:(tt + 1) * 512], in_=ps[:, :]),
                                     reads=[("ps", pi)], pwrites=[skey])
                    S.dma("sp", lambda e, st=st, dst=dst, row0=row0: e.dma_start(
                        out=dst[row0:row0 + 128, t0:t0 + 1024], in_=st[:, :]),
                        reads=[skey], pwrites=[dkey])
            else:
                for tb in range(8):
                    st_list = C.stF if g["dt"] == F32 else C.stB
                    sname = "stF" if g["dt"] == F32 else "stB"
                    si = _rot(st_list, C.rs, sname)
                    st = st_list[si]
                    skey = (sname, si)
                    pi = _rot(C.psum, C.rs, "psm")
                    ps = C.psum[pi]

                    def mm(e, ps=ps, wb=wb, tb=tb, ncol=ncol):
                        ins = None
                        for kc in range(32):
                            ins = e.matmul(ps[:, 0:ncol], lhsT=actT[:, kc, tb * 128:(tb + 1) * 128],
                                           rhs=wb[:, kc, 0:ncol], start=(kc == 0), stop=(kc == 31))
                        return ins
                    S.op("pe", mm, reads=wkeys + [akey], writes=[("ps", pi)])
                    ev = "act" if _rot([0, 1], C.rs, "evsel") == 0 else "dve"
                    if ev == "act":
                        S.op("act", lambda e, ps=ps, st=st, ncol=ncol: e.copy(out=st[:, 0:ncol], in_=ps[:, 0:ncol]),
                             reads=[("ps", pi)], writes=[skey])
                    else:
                        S.op("dve", lambda e, ps=ps, st=st, ncol=ncol: e.tensor_copy(out=st[:, 0:ncol], in_=ps[:, 0:ncol]),
                             reads=[("ps", pi)], writes=[skey])
                    ro = g.get("row_off", 0)
                    S.dma("sp", lambda e, st=st, dst=dst, tb=tb, ct=ct, ncol=ncol, ro=ro: e.dma_start(
                        out=dst[ro + t0 + tb * 128:ro + t0 + (tb + 1) * 128, ct:ct + ncol], in_=st[:, 0:ncol]),
                        reads=[skey], pwrites=[dkey])


def load_actT(C, srcT, t0):
    S = C.S
    v = srcT.rearrange("(c p) t -> p c t", p=128)
    for c8 in range(4):
        S.dma("sp", lambda e, c8=c8: e.dma_start(out=C.hT[:, c8 * 8:(c8 + 1) * 8, :], in_=v[:, c8 * 8:(c8 + 1) * 8, t0:t0 + 1024]),
              reads=[("dram", srcT.name)], pwrites=["hT"])


def phase_rwkv(C):
    S, nc, I, R = C.S, C.nc, C.I, C.R
    rs_ = C.rs
    NEG_E = -float(np.exp(-0.5))
    with ExitStack() as es:
        def sb(name, shape, dt=F32):
            return es.enter_context(nc.sbuf_tensor("rk_" + name, list(shape), dt))
        vec = sb("vec", [128, 10, 16])
        wdu = sb("wdu", [128, RW])
        aup = sb("aup", [128, RW])
        bd = sb("bd", [128, 128])
        ind = sb("ind", [128, 2])
        identf = sb("identf", [128, 128])
        identb = sb("identb", [128, 128], BF16)
        rmask = sb("rmask", [128, T])
        mk4 = sb("mk4", [128, 512])
        m04 = sb("m04", [128, 4, 128])
        twlo = sb("twlo", [128, T])
        alo = sb("alo", [128, T])
        gneps = sb("gneps", [128, 1])
        G = [sb(f"G{i}", [128, T]) for i in range(8)]
        bt = sb("bt", [128, T], BF16)
        kt = sb("kt", [128, T], BF16)
        QT = sb("QT", [128, 16, 2, 128], BF16)
        Btok = sb("Btok", [128, 16, 128], BF16)
        Ktok = sb("Ktok", [128, 16, 128], BF16)
        Vb = sb("Vb", [128, 16, 128], BF16)
        ATall = sb("ATall", [128, 32, 512], BF16)
        MTall = sb("MTall", [128, 32, 128], BF16)
        Xg = [[sb(f"Xg{s}{i}", [128, 4, 128], BF16) for i in range(2)] for s in range(NSLOT)]
        XTg = [[sb(f"XTg{s}{i}", [128, 4, 128], BF16) for i in range(2)] for s in range(NSLOT)]
        gl = sb("gl", [128, 16])
        bon = sb("bon", [128, 32])
        st1 = sb("st1", [128, 32])
        st2 = sb("st2", [128, 32])
        muv = sb("muv", [128, 128])
        gng = sb("gng", [128, 128])
        gnb = sb("gnb", [128, 128])
        Hf = sb("Hf", [128, 128])
        Hb = sb("Hb", [128, 128], BF16)
        th = sb("th", [128, 128])
        RHSb = [sb(f"RHSb{i}", [128, 128], BF16) for i in range(2)]
        Ub = [sb(f"Ub{i}", [128, 128], BF16) for i in range(2)]

        def ld(dst, src, key):
            S.dma("sp", lambda e: e.dma_start(out=dst, in_=src), writes=[key])
        ld(vec[:, :, :], I.e_vec.rearrange("a p j -> p a j"), "vec")
        ld(wdu[:, :], I.e_wdu[:, :], "wdu")
        ld(aup[:, :], I.e_aup[:, :], "aup")
        ld(bd[:, :], I.consts[:, BDO:BDO + 128], "bd")
        ld(ind[:, :], I.consts[:, INDO:INDO + 2], "ind")
        ld(identf[:, :], I.consts[:, IDO:IDO + 128], "identf")
        ld(rmask[:, :], I.consts[:, RMASK:RMASK + T], "rmask")
        for q in range(4):
            S.dma("sp", lambda e, q=q: e.dma_start(out=m04[:, q, :], in_=I.consts[:, M0:M0 + 128]), pwrites=["m04"])
            mo = M1 if q % 2 == 0 else M2
            S.dma("sp", lambda e, q=q, mo=mo: e.dma_start(out=mk4[:, q * 128:(q + 1) * 128], in_=I.consts[:, mo:mo + 128]), pwrites=["mk4"])
        S.op("dve", lambda e: e.tensor_copy(out=identb[:, :], in_=identf[:, :]), reads=["identf"], writes=["identb"])
        S.op("pool", lambda e: e.memset(G[7][0:1, :], 0.0), writes=["G7"])
        S.op("pool", lambda e: e.memset(gneps[:, :], GN_EPS), writes=["gneps"])
        S.op("pool", lambda e: e.memset(th[:, :], 0.0), writes=["th"])
        S.op("dve", lambda e: e.tensor_copy(out=C.psum[7][:, 0:128], in_=th[:, :]), reads=["th"], writes=[("ps", 7)])
        S.dma("sp", lambda e: e.dma_start(out=R.vtok[0:1, :], in_=G[7][0:1, :]), reads=["G7"], pwrites=[("dram", R.vtok.name)])

        def lerpF(src, skey, mu_col, tmp, tkey):
            S.op("dve", lambda e: e.tensor_tensor(out=tmp[:, 1:T], in0=src[:, 0:T - 1], in1=src[:, 1:T], op=ALU.subtract),
                 reads=[skey], pwrites=[tkey])
            S.op("dve", lambda e: e.tensor_scalar(out=tmp[:, 0:1], in0=src[:, 0:1], scalar1=-1.0, scalar2=None, op0=ALU.mult),
                 reads=[skey], pwrites=[tkey])
            S.op("dve", lambda e: e.scalar_tensor_tensor(out=src[:, :], in0=tmp[:, :], scalar=mu_col, in1=src[:, :],
                                                         op0=ALU.mult, op1=ALU.add),
                 reads=[skey, tkey, "vec"], writes=[skey])

        ld(twlo[:, :], R.loT[0:128, :], "twlo")
        ld(alo[:, :], R.loT[128:256, :], "alo")
        lerpF(twlo, "twlo", vec[:, 7, 0:1], G[2], "G2")
        lerpF(alo, "alo", vec[:, 8, 0:1], G[2], "G2")
        S.op("act", lambda e: e.activation(out=twlo[:, :], in_=twlo[:, :], func=AF.Tanh), reads=["twlo"], writes=["twlo"])

        def psr():
            i = 2 + _rot(list(range(5)), rs_, "rkps")
            return i, C.psum[i]

        def g3(t):
            return t[:, :].rearrange("p (c f) -> p c f", c=16)

        def g32(t):
            return t[:, :].rearrange("p (c f) -> p c f", c=32)

        def pair(j):
            jc = slice(j * 128, (j + 1) * 128)
            r_, k_, tmp, lw, a_, kkn, c_, ex = G
            S.dma("sp", lambda e: e.dma_start(out=r_[:, :], in_=R.rT[jc, :]), reads=[("dram", R.rT.name)], writes=["G0"])
            S.dma("sp", lambda e: e.dma_start(out=k_[:, :], in_=R.kT[jc, :]), reads=[("dram", R.kT.name)], writes=["G1"])
            lerpF(r_, "G0", vec[:, 5, j:j + 1], tmp, "G2")
            lerpF(k_, "G1", vec[:, 6, j:j + 1], tmp, "G2")
            for tc in range(4):
                ts_ = slice(tc * 512, (tc + 1) * 512)
                pi, ps = psr()
                S.op("pe", lambda e, ps=ps, ts_=ts_: e.matmul(ps[:, :], lhsT=wdu[:, jc], rhs=twlo[:, ts_], start=True, stop=True),
                     reads=["wdu", "twlo"], writes=[("ps", pi)])
                S.op("act", lambda e, ps=ps, ts_=ts_: e.activation(out=lw[:, ts_], in_=ps[:, :], func=AF.Sigmoid, bias=vec[:, 0, j:j + 1], scale=1.0),
                     reads=[("ps", pi), "vec"], pwrites=["G3"])
                pi, ps = psr()
                S.op("pe", lambda e, ps=ps, ts_=ts_: e.matmul(ps[:, :], lhsT=aup[:, jc], rhs=alo[:, ts_], start=True, stop=True),
                     reads=["aup", "alo"], writes=[("ps", pi)])
                S.op("act", lambda e, ps=ps, ts_=ts_: e.activation(out=a_[:, ts_], in_=ps[:, :], func=AF.Sigmoid, bias=vec[:, 1, j:j + 1], scale=1.0),
                     reads=[("ps", pi), "vec"], pwrites=["G4"])
            S.op("act", lambda e: e.activation(out=tmp[:, :], in_=k_[:, :], func=AF.Identity, scale=vec[:, 2, j:j + 1]),
                 reads=["G1", "vec"], writes=["G2"])
            S.op("act", lambda e: e.activation(out=ex[:, :], in_=tmp[:, :], func=AF.Square), reads=["G2"], writes=["G7"])
            for tc in range(4):
                ts_ = slice(tc * 512, (tc + 1) * 512)
                pi, ps = psr()
                S.op("pe", lambda e, ps=ps, ts_=ts_: e.matmul(ps[:, :], lhsT=bd[:, :], rhs=ex[:, ts_], start=True, stop=True),
                     reads=["bd", "G7"], writes=[("ps", pi)])
                S.op("act", lambda e, ps=ps, ts_=ts_: e.activation(out=c_[:, ts_], in_=ps[:, :], func=AF.Sqrt),
                     reads=[("ps", pi)], pwrites=["G6"])
            S.op("dve", lambda e: e.tensor_scalar(out=c_[:, :], in0=c_[:, :], scalar1=1e-12, scalar2=None, op0=ALU.max),
                 reads=["G6"], writes=["G6"])
            S.op("dve", lambda e: e.reciprocal(out=c_[:, :], in_=c_[:, :]), reads=["G6"], writes=["G6"])
            S.op("dve", lambda e: e.tensor_tensor(out=kkn[:, :], in0=tmp[:, :], in1=c_[:, :], op=ALU.mult),
                 reads=["G2", "G6"], writes=["G5"])
            S.op("dve", lambda e: e.tensor_scalar(out=tmp[:, :], in0=a_[:, :], scalar1=-1.0, scalar2=vec[:, 3, j:j + 1], op0=ALU.add, op1=ALU.mult),
                 reads=["G4", "vec"], writes=["G2"])
            S.op("dve", lambda e: e.scalar_tensor_tensor(out=k_[:, :], in0=tmp[:, :], scalar=1.0, in1=k_[:, :], op0=ALU.add, op1=ALU.mult),
                 reads=["G2", "G1"], writes=["G1"])
            S.op("dve", lambda e: e.scalar_tensor_tensor(out=tmp[:, :], in0=r_[:, :], scalar=vec[:, 4, j:j + 1], in1=k_[:, :], op0=ALU.mult, op1=ALU.mult),
                 reads=["G0", "G1", "vec"], writes=["G2"])
            pib, psb = psr()

            def mmb(e, psb=psb):
                ins = None
                for c in range(16):
                    ins = e.matmul(psb[:, 2 * c:2 * c + 2], lhsT=tmp[:, c * 128:(c + 1) * 128], rhs=ind[:, :], start=True, stop=True)
                return ins
            S.op("pe", mmb, reads=["G2", "ind"], writes=[("ps", pib)])
            S.op("act", lambda e, psb=psb: e.copy(out=bon[:, :], in_=psb[:, 0:32]), reads=[("ps", pib)], writes=["bon"])
            S.op("dve", lambda e: e.tensor_tensor(out=a_[:, :], in0=kkn[:, :], in1=a_[:, :], op=ALU.mult), reads=["G5", "G4"], writes=["G4"])
            S.op("dve", lambda e: e.tensor_tensor_scan(out=c_[:, :], data0=rmask[:, :], data1=lw[:, :], initial=0.0, op0=ALU.mult, op1=ALU.add),
                 reads=["rmask", "G3"], writes=["G6"])
            S.op("dve", lambda e: e.tensor_tensor(out=lw[:, :], in0=c_[:, :], in1=lw[:, :], op=ALU.subtract), reads=["G6", "G3"], writes=["G3"])
            S.op("act", lambda e: e.activation(out=ex[:, :], in_=c_[:, :], func=AF.Exp, scale=NEG_E), reads=["G6"], writes=["G7"])
            S.op("dve", lambda e: e.tensor_tensor(out=QT[:, :, 1, :], in0=g3(r_), in1=g3(ex), op=ALU.mult), reads=["G0", "G7"], pwrites=["QT"])
            S.op("dve", lambda e: e.tensor_copy(out=gl[:, :], in_=g3(ex)[:, :, 127]), reads=["G7"], writes=["gl"])
            S.op("act", lambda e: e.activation(out=ex[:, :], in_=lw[:, :], func=AF.Exp, scale=NEG_E), reads=["G3"], writes=["G7"])
            S.op("dve", lambda e: e.scalar_tensor_tensor(out=QT[:, :, 0, :], in0=g3(kkn), scalar=-1.0, in1=g3(ex), op0=ALU.mult, op1=ALU.mult),
                 reads=["G5", "G7"], pwrites=["QT"])
            S.op("act", lambda e: e.activation(out=ex[:, :], in_=c_[:, :], func=AF.Exp, scale=-NEG_E), reads=["G6"], writes=["G7"])
            S.op("dve", lambda e: e.tensor_tensor(out=bt[:, :], in0=a_[:, :], in1=ex[:, :], op=ALU.mult), reads=["G4", "G7"], writes=["bt"])
            S.op("dve", lambda e: e.tensor_tensor(out=kt[:, :], in0=k_[:, :], in1=ex[:, :], op=ALU.mult), reads=["G1", "G7"], writes=["kt"])
            if C.rk_stop == 1:
                return
            for (srcT_, skey, dstk, dkey) in ((bt, "bt", Btok, "Btok"), (kt, "kt", Ktok, "Ktok")):
                for hf in range(2):
                    pi, ps = psr()
                    psv = ps[:, :].bitcast(BF16)

                    def tr(e, psv=psv, srcT_=srcT_, hf=hf):
                        ins = None
                        for cc in range(8):
                            c = hf * 8 + cc
                            ins = e.transpose(out=psv[:, cc * 128:(cc + 1) * 128], in_=srcT_[:, c * 128:(c + 1) * 128], identity=identb[:, :])
                        return ins
                    S.op("pe", tr, reads=[skey, "identb"], writes=[("ps", pi)])
                    S.op("act", lambda e, psv=psv, dstk=dstk, hf=hf: e.copy(
                        out=dstk[:, hf * 8:(hf + 1) * 8, :].rearrange("p c f -> p (c f)"), in_=psv[:, :]),
                        reads=[("ps", pi)], pwrites=[dkey])
            vcur, vprev, sg = G[0], G[1], G[2]
            S.dma("sp", lambda e: e.dma_start(out=g3(vcur), in_=R.vtok[1:T + 1, jc].rearrange("(c p) f -> p c f", p=128)),
                  reads=[("dram", R.vtok.name)], writes=["G0"])
            S.dma("sp", lambda e: e.dma_start(out=g3(vprev), in_=R.vtok[0:T, jc].rearrange("(c p) f -> p c f", p=128)),
                  reads=[("dram", R.vtok.name)], writes=["G1"])
            S.dma("sp", lambda e: e.dma_start(out=g3(sg), in_=R.gtok[:, jc].rearrange("(c p) f -> p c f", p=128)),
                  reads=[("dram", R.gtok.name)], writes=["G2"])
            S.dma("sp", lambda e: e.dma_start(out=muv[:, :], in_=I.e_mu_v[j:j + 1, 0:128].partition_broadcast(128)), writes=["muv"])
            S.dma("sp", lambda e: e.dma_start(out=gng[:, :], in_=I.e_gn[0, j:j + 1, 0:128].partition_broadcast(128)), writes=["gng"])
            S.dma("sp", lambda e: e.dma_start(out=gnb[:, :], in_=I.e_gn[1, j:j + 1, 0:128].partition_broadcast(128)), writes=["gnb"])
            bc16 = lambda t: t[:, :].unsqueeze(1).broadcast_to([128, 16, 128])
            S.op("pool", lambda e: e.tensor_tensor(out=vprev[:, :], in0=vprev[:, :], in1=vcur[:, :], op=ALU.subtract), reads=["G0", "G1"], writes=["G1"])
            S.op("pool", lambda e: e.tensor_tensor(out=g3(vprev), in0=g3(vprev), in1=bc16(muv), op=ALU.mult), reads=["G1", "muv"], writes=["G1"])
            S.op("pool", lambda e: e.tensor_tensor(out=vcur[:, :], in0=vcur[:, :], in1=vprev[:, :], op=ALU.add), reads=["G0", "G1"], writes=["G0"])
            S.op("pool", lambda e: e.tensor_copy(out=Vb[:, :, :], in_=g3(vcur)), reads=["G0"], writes=["Vb"])
            S.op("act", lambda e: e.activation(out=sg[:, :], in_=sg[:, :], func=AF.Silu), reads=["G2"], writes=["G2"])

            if C.rk_stop == 2:
                return
            def group(cg, slot):
                p0 = cg * 4
                X, XT = Xg[slot], XTg[slot]
                psBs = [psr(), psr()]
                for q in range(4):
                    hp = q // 2
                    c = cg * 2 + q % 2
                    rows = slice(64 * hp, 64 * hp + 64)
                    cs_ = slice(c * 128, (c + 1) * 128)
                    pia = hp
                    psA = C.psum[pia]
                    pib, psB = psBs[hp]
                    cl = q % 2

                    def mma(e, psA=psA, rows=rows, cs_=cs_, c=c):
                        qv = QT[rows, c, :, :].rearrange("p a t -> p (a t)")
                        e.matmul(psA[:, 0:256], lhsT=bt[rows, cs_], rhs=qv, start=True, stop=True)
                        return e.matmul(psA[:, 256:512], lhsT=kt[rows, cs_], rhs=qv, start=True, stop=True)
                    S.op("pe", mma, reads=["bt", "kt", "QT"], writes=[("ps", pia)])
                    S.op("dve", lambda e, psA=psA, q=q: e.tensor_tensor(out=ATall[:, p0 + q, :], in0=psA[:, :], in1=mk4[:, :], op=ALU.mult),
                         reads=[("ps", pia), "mk4"], pwrites=[("ATall", cg)])
                    S.op("pe", lambda e, psB=psB, rows=rows, cs_=cs_, c=c, cl=cl: e.matmul(
                        psB[:, cl * 128:(cl + 1) * 128], lhsT=QT[rows, c, 0, :], rhs=bt[rows, cs_], start=True, stop=True),
                        reads=["QT", "bt"], pwrites=[("ps", pib)])
                for hp in range(2):
                    pib, psB = psBs[hp]
                    S.op("dve", lambda e, psB=psB, hp=hp: e.tensor_tensor(
                        out=X[0][:, 2 * hp:2 * hp + 2, :].rearrange("p q f -> p (q f)"), in0=psB[:, 0:256],
                        in1=m04[:, 0:2, :].rearrange("p q f -> p (q f)"), op=ALU.mult),
                        reads=[("ps", pib), "m04"], pwrites=[("Xg", slot, 0)])
                S.op("dve", lambda e: e.tensor_tensor(out=MTall[:, p0:p0 + 4, :], in0=ATall[:, p0:p0 + 4, 0:128],
                                                      in1=identb[:, :].unsqueeze(1).broadcast_to([128, 4, 128]), op=ALU.add),
                     reads=[("ATall", cg), "identb"], writes=[("MTall", cg)])
                yield
                xi = 0
                xtb = None
                xtkey = ("ATall", cg)

                def xt_ap(level_buf, q):
                    if level_buf is None:
                        return ATall[:, p0 + q, 0:128]
                    return XT[level_buf][:, q, :]
                for lvl in range(1, 7):
                    nxi = 1 - xi
                    piX, psX = psr()

                    def mmx(e, psX=psX, xi=xi, xtb=xtb):
                        ins = None
                        for q in range(4):
                            ins = e.matmul(psX[:, q * 128:(q + 1) * 128], lhsT=xt_ap(xtb, q), rhs=X[xi][:, q, :], start=True, stop=True)
                        return ins
                    S.op("pe", mmx, reads=[xtkey, ("Xg", slot, xi)], writes=[("ps", piX)])
                    if lvl < 6:
                        piT, psT = psr()
                        nxt = 0 if xtb is None else 1 - xtb

                        def mmt(e, psT=psT, xi=xi, xtb=xtb):
                            ins = None
                            for q in range(4):
                                ins = e.matmul(psT[:, q * 128:(q + 1) * 128], lhsT=X[xi][:, q, :], rhs=xt_ap(xtb, q), start=True, stop=True)
                            return ins
                        S.op("pe", mmt, reads=[xtkey, ("Xg", slot, xi)], writes=[("ps", piT)])
                    S.op("act", lambda e, psX=psX, nxi=nxi: e.copy(out=X[nxi][:, :, :].rearrange("p q f -> p (q f)"), in_=psX[:, :]),
                         reads=[("ps", piX)], writes=[("Xg", slot, nxi)])
                    if lvl < 6:
                        S.op("act", lambda e, psT=psT, nxt=nxt: e.copy(out=XT[nxt][:, :, :].rearrange("p q f -> p (q f)"), in_=psT[:, :]),
                             reads=[("ps", piT)], writes=[("XTg", slot, nxt)])
                        xtb = nxt
                        xtkey = ("XTg", slot, nxt)
                    xi = nxi
                    piD, psD = psr()

                    def mmd(e, psD=psD, xi=xi):
                        ins = None
                        for q in range(4):
                            ins = e.matmul(psD[:, q * 128:(q + 1) * 128], lhsT=X[xi][:, q, :], rhs=MTall[:, p0 + q, :], start=True, stop=True)
                        return ins
                    S.op("pe", mmd, reads=[("Xg", slot, xi), ("MTall", cg)], writes=[("ps", piD)])
                    S.op("dve", lambda e, psD=psD: e.tensor_tensor(out=MTall[:, p0:p0 + 4, :].rearrange("p q f -> p (q f)"),
                                                                   in0=MTall[:, p0:p0 + 4, :].rearrange("p q f -> p (q f)"), in1=psD[:, :], op=ALU.add),
                         reads=[("MTall", cg), ("ps", piD)], writes=[("MTall", cg)])
                    yield

            if C.rk_stop == 3:
                return
            for gp_ in range(8 // NSLOT):
                gens = [group(NSLOT * gp_ + s_, s_) for s_ in range(NSLOT)]
                for _step in range(7):
                    for g_ in gens:
                        next(g_)
            if C.rk_stop == 4:
                return
            S.op("pool", lambda e: e.memset(Hf[:, :], 0.0), writes=["Hf"])
            S.op("pool", lambda e: e.memset(Hb[:, :], 0.0), writes=["Hb"])
            Yall = G[1]
            psH = C.psum[7]
            for c in range(16):
                cg = c // 2
                bi = c % 2
                ix = [(c // 2) * 4 + hp * 2 + (c % 2) for hp in range(2)]
                piR, psR = psr()

                def mmr(e, psR=psR, c=c, ix=ix):
                    ins = e.matmul(psR[:, 0:128], lhsT=QT[:, c, 0, :], rhs=Hb[:, :], start=True, stop=False)
                    for hp in range(2):
                        vs = slice(64 * hp, 64 * hp + 64)
                        ins = e.matmul(psR[:, vs], lhsT=ATall[:, ix[hp], 256:384], rhs=Vb[:, c, vs], start=False, stop=(hp == 1))
                    return ins
                S.op("pe", mmr, reads=["QT", "Hb", ("ATall", cg), "Vb"], writes=[("ps", piR)])
                S.op("act", lambda e, psR=psR, bi=bi: e.copy(out=RHSb[bi][:, :], in_=psR[:, 0:128]), reads=[("ps", piR)], writes=[("RHSb", bi)])
                piU, psU = psr()

                def mmu(e, psU=psU, c=c, bi=bi, ix=ix):
                    ins = None
                    for hp in range(2):
                        vs = slice(64 * hp, 64 * hp + 64)
                        ins = e.matmul(psU[:, vs], lhsT=MTall[:, ix[hp], :], rhs=RHSb[bi][:, vs], start=True, stop=True)
                    return ins
                S.op("pe", mmu, reads=[("MTall", cg), ("RHSb", bi)], writes=[("ps", piU)])
                S.op("dve", lambda e, psU=psU, bi=bi: e.tensor_copy(out=Ub[bi][:, :], in_=psU[:, 0:128]), reads=[("ps", piU)], writes=[("Ub", bi)])

                def mmh(e, c=c, bi=bi):
                    ins = None
                    for hp in range(2):
                        vs = slice(64 * hp, 64 * hp + 64)
                        e.matmul(psH[vs, vs], lhsT=Btok[:, c, vs], rhs=Ub[bi][:, vs], start=True, stop=False)
                        ins = e.matmul(psH[vs, vs], lhsT=Ktok[:, c, vs], rhs=Vb[:, c, vs], start=False, stop=True)
                    return ins
                S.op("pe", mmh, reads=["Btok", "Ktok", "Vb", ("Ub", bi)], writes=[("ps", 7)])
                piY, psY = psr()

                def mmy(e, psY=psY, c=c, bi=bi, ix=ix):
                    ins = e.matmul(psY[:, 0:128], lhsT=QT[:, c, 1, :], rhs=Hb[:, :], start=True, stop=False)
                    for hp in range(2):
                        vs = slice(64 * hp, 64 * hp + 64)
                        e.matmul(psY[:, vs], lhsT=ATall[:, ix[hp], 128:256], rhs=Ub[bi][:, vs], start=False, stop=False)
                        ins = e.matmul(psY[:, vs], lhsT=ATall[:, ix[hp], 384:512], rhs=Vb[:, c, vs], start=False, stop=(hp == 1))
                    return ins
                S.op("pe", mmy, reads=["QT", "Hb", ("ATall", cg), "Vb", ("Ub", bi)], writes=[("ps", piY)])
                S.op("act", lambda e, psY=psY, c=c: e.copy(out=Yall[:, c * 128:(c + 1) * 128], in_=psY[:, 0:128]),
                     reads=[("ps", piY)], pwrites=["G1"])
                S.op("dve", lambda e: e.tensor_tensor(out=th[:, :], in0=psH[:, 0:128], in1=Hf[:, :], op=ALU.add),
                     reads=[("ps", 7), "Hf"], writes=["th"])
                S.op("dve", lambda e, c=c: e.tensor_scalar(out=Hb[:, :], in0=th[:, :], scalar1=gl[:, c:c + 1], scalar2=None, op0=ALU.mult),
                     reads=["th", "gl"], writes=["Hb"])
                S.op("act", lambda e, c=c: e.activation(out=Hf[:, :], in_=th[:, :], func=AF.Identity, scale=gl[:, c:c + 1]),
                     reads=["th", "gl"], writes=["Hf"])

            if C.rk_stop == 5:
                return
            Y3 = g32(Yall)
            sq3 = g32(G[3])
            bc64 = lambda t: t[:, :].unsqueeze(2).broadcast_to([128, 32, 64])
            S.op("dve", lambda e: e.tensor_reduce(out=st1[:, :], in_=Y3, axis=AX.X, op=ALU.add), reads=["G1", "G1"], writes=["st1"])
            S.op("dve", lambda e: e.tensor_scalar(out=st1[:, :], in0=st1[:, :], scalar1=1.0 / 64, scalar2=None, op0=ALU.mult), reads=["st1"], writes=["st1"])
            S.op("dve", lambda e: e.tensor_tensor(out=Y3, in0=Y3, in1=bc64(st1), op=ALU.subtract), reads=["G1", "st1"], writes=["G1"])
            S.op("act", lambda e: e.activation(out=G[3][:, :], in_=Yall[:, :], func=AF.Square), reads=["G1"], writes=["G3"])
            S.op("dve", lambda e: e.tensor_reduce(out=st2[:, :], in_=sq3, axis=AX.X, op=ALU.add), reads=["G3"], writes=["st2"])
            S.op("act", lambda e: e.activation(out=st2[:, :], in_=st2[:, :], func=AF.Sqrt, bias=gneps[:, 0:1], scale=1.0 / 64),
                 reads=["st2", "gneps"], writes=["st2"])
            S.op("dve", lambda e: e.reciprocal(out=st2[:, :], in_=st2[:, :]), reads=["st2"], writes=["st2"])
            S.op("dve", lambda e: e.tensor_tensor(out=Y3, in0=Y3, in1=bc64(st2), op=ALU.mult), reads=["G1", "st2"], writes=["G1"])
            S.op("dve", lambda e: e.tensor_tensor(out=g3(Yall), in0=g3(Yall), in1=bc16(gng), op=ALU.mult), reads=["G1", "gng"], writes=["G1"])
            S.op("dve", lambda e: e.tensor_tensor(out=g3(Yall), in0=g3(Yall), in1=bc16(gnb), op=ALU.add), reads=["G1", "gnb"], writes=["G1"])
            S.op("dve", lambda e: e.tensor_tensor(out=sq3, in0=g32(G[0]), in1=bc64(bon), op=ALU.mult), reads=["G0", "bon"], writes=["G3"])
            S.op("dve", lambda e: e.tensor_tensor(out=Yall[:, :], in0=Yall[:, :], in1=G[3][:, :], op=ALU.add), reads=["G1", "G3"], writes=["G1"])
            Ytok = kt[:, :].rearrange("p (c f) -> p c f", c=16)
            yTsb = bt
            S.op("dve", lambda e: e.tensor_tensor(out=Ytok, in0=g3(Yall), in1=g3(G[2]), op=ALU.mult), reads=["G1", "G2"], writes=["kt"])
            for hf in range(2):
                pi, ps = psr()
                psv = ps[:, :].bitcast(BF16)

                def tr(e, psv=psv, hf=hf):
                    ins = None
                    for cc in range(8):
                        c = hf * 8 + cc
                        ins = e.transpose(out=psv[:, cc * 128:(cc + 1) * 128], in_=Ytok[:, c, :], identity=identb[:, :])
                    return ins
                S.op("pe", tr, reads=["kt", "identb"], writes=[("ps", pi)])
                S.op("act", lambda e, psv=psv, hf=hf: e.copy(out=yTsb[:, hf * 1024:(hf + 1) * 1024], in_=psv[:, :]),
                     reads=[("ps", pi)], pwrites=["bt"])
            S.dma("sp", lambda e: e.dma_start(out=R.yT[jc, :], in_=yTsb[:, :]), reads=["bt"], pwrites=[("dram", R.yT.name)])

        for j_ in range(C.rk_pairs):
            pair(j_)


def phase_sb(C):
    S, nc, I, R = C.S, C.nc, C.I, C.R
    scale = 1.0 / float(np.sqrt(128.0))
    with ExitStack() as es:
        def sb(name, shape, dt=F32):
            return es.enter_context(nc.sbuf_tensor(name, list(shape), dt))
        qT = [sb(f"sbq{i}", [128, T], BF16) for i in range(2)]
        kT = [sb(f"sbk{i}", [128, T], BF16) for i in range(2)]
        kTs = [sb(f"sbks{i}", [128, T], BF16) for i in range(2)]
        vh = [sb(f"sbv{i}", [128, 16, 128], BF16) for i in range(2)]
        gs = [sb(f"sbg{i}", [128, T]) for i in range(2)]
        mtmp = sb("sb_mtmp", [128, 128])
        Lst = sb("sb_L", [128, 128], BF16)
        onb = sb("sb_ones", [128, 128], BF16)
        mbig = sb("sb_mbig", [128, 896])
        onec = sb("sb_onec", [128, 1])
        e_sb = [sb(f"sb_e{i}", [128, 512]) for i in range(2)]
        sp_sb = [sb(f"sb_sp{i}", [128, 512]) for i in range(3)]
        tmp_sb = [sb(f"sb_tmp{i}", [128, 512]) for i in range(2)]
        arg_sb = [sb(f"sb_arg{i}", [128, 512]) for i in range(2)]
        spb = [sb(f"sb_spb{i}", [128, 512], BF16) for i in range(3)]
        att = [sb(f"sb_att{i}", [128, 512], BF16) for i in range(2)]
        acc = [sb(f"sb_acc{i}", [128, 512]) for i in range(2)]
        accb = [sb(f"sb_accb{i}", [128, 512], BF16) for i in range(2)]
        ost = [sb(f"sb_ost{i}", [128, 512], BF16) for i in range(2)]

        S.dma("sp", lambda e: e.dma_start(out=mtmp[:, :], in_=I.consts[:, M0:M0 + 128]), writes=["sb_mtmp"])
        S.dma("sp", lambda e: e.dma_start(out=mbig[:, :], in_=I.consts[:, MBIG:MBIG + 896]), writes=["sb_mbig"])
        S.op("dve", lambda e: e.tensor_scalar(out=Lst[:, :], in0=mtmp[:, :], scalar1=-1.0, scalar2=None, op0=ALU.mult), reads=["sb_mtmp"], writes=["sb_L"])
        S.op("pool", lambda e: e.memset(onb[:, :], -1.0), writes=["sb_ones"])
        S.op("pool", lambda e: e.memset(onec[:, :], 1.0), writes=["sb_onec"])
        vv = R.vstok.rearrange("(kb p) c -> p kb c", p=128)
        PZ, PL, PO = (0, 1, 4), (2, 3), (6, 7)
        cnt = {"item": 0, "qcg": 0, "accb": 0}

        def f_z(it):
            kb, qc, hi, bz = it["kb"], it["qc"], it["hi"], it["bz"]
            pz = PZ[bz]
            S.op("pe", lambda e: e.matmul(C.psum[pz][:, :], lhsT=kT[hi][:, kb * 128:(kb + 1) * 128],
                                          rhs=qT[hi][:, qc * 512:(qc + 1) * 512], start=True, stop=True),
                 reads=[("sbq", hi), ("sbk", hi)], writes=[("ps", pz)])

        def f_sp(it):
            kb, qc, bz, be = it["kb"], it["qc"], it["bz"], it["be"]
            pz = PZ[bz]
            S.op("act", lambda e: e.activation(out=e_sb[be][:, :], in_=C.psum[pz][:, :], func=AF.Exp, scale=scale),
                 reads=[("ps", pz)], writes=[("sb_e", be)])
            S.op("act", lambda e: e.activation(out=sp_sb[bz][:, :], in_=e_sb[be][:, :], func=AF.Ln, bias=onec[:, 0:1], scale=1.0),
                 reads=[("sb_e", be), "sb_onec"], writes=[("sb_sp", bz)])
            off = kb - 4 * qc
            if off >= 0:
                m0 = 384 - off * 128
                S.op("dve", lambda e: e.tensor_tensor(out=sp_sb[bz][:, :], in0=sp_sb[bz][:, :], in1=mbig[:, m0:m0 + 512], op=ALU.mult),
                     reads=[("sb_sp", bz), "sb_mbig"], writes=[("sb_sp", bz)])
            S.op("act", lambda e: e.copy(out=spb[bz][:, :], in_=sp_sb[bz][:, :]),
                 reads=[("sb_sp", bz)], writes=[("sb_spb", bz)])

        def f_later(it):
            kb, qc, first, bz, bl, ai = it["kb"], it["qc"], it["first"], it["bz"], it["bl"], it["ai"]
            pz, pl = PZ[bz], PL[bl]
            hi = it["hi"]
            if kb == first:
                def mm(e):
                    e.matmul(C.psum[pl][:, :], lhsT=Lst[:, :], rhs=spb[bz][:, :], start=True, stop=False)
                    return e.matmul(C.psum[pl][:, :], lhsT=kTs[hi][:, kb * 128:(kb + 1) * 128], rhs=qT[hi][:, qc * 512:(qc + 1) * 512],
                                    start=False, stop=True)
                S.op("pe", mm, reads=["sb_L", ("sb_spb", bz), ("sbks", hi), ("sbq", hi)], writes=[("ps", pl)])
            else:
                abi = it["abi"]

                def mm(e):
                    e.matmul(C.psum[pl][:, :], lhsT=Lst[:, :], rhs=spb[bz][:, :], start=True, stop=False)
                    e.matmul(C.psum[pl][:, :], lhsT=onb[:, :], rhs=accb[abi][:, :], start=False, stop=False)
                    return e.matmul(C.psum[pl][:, :], lhsT=kTs[hi][:, kb * 128:(kb + 1) * 128], rhs=qT[hi][:, qc * 512:(qc + 1) * 512],
                                    start=False, stop=True)
                S.op("pe", mm, reads=["sb_L", "sb_ones", ("sb_spb", bz), ("sb_accb", abi), ("sbks", hi), ("sbq", hi)], writes=[("ps", pl)])
            S.op("dve", lambda e: e.tensor_tensor(out=arg_sb[bl][:, :], in0=C.psum[pl][:, :], in1=sp_sb[bz][:, :], op=ALU.subtract),
                 reads=[("sb_sp", bz), ("ps", pl)], writes=[("sb_arg", bl)])
            if kb > 0:
                if kb == first:
                    S.op("pool", lambda e: e.tensor_copy(out=acc[ai][:, :], in_=sp_sb[bz][:, :]),
                         reads=[("sb_sp", bz)], writes=[("sb_acc", ai)])
                else:
                    S.op("pool", lambda e: e.tensor_tensor(out=acc[ai][:, :], in0=acc[ai][:, :], in1=sp_sb[bz][:, :], op=ALU.add),
                         reads=[("sb_sp", bz), ("sb_acc", ai)], writes=[("sb_acc", ai)])
                nb = it["nabi"]
                S.op("dve", lambda e: e.tensor_copy(out=accb[nb][:, :], in_=acc[ai][:, :]),
                     reads=[("sb_acc", ai)], writes=[("sb_accb", nb)])

        def f_att(it):
            h, kb, qc, first, hi, bl, oi = it["h"], it["kb"], it["qc"], it["first"], it["hi"], it["bl"], it["oi"]
            po = PO[oi]
            S.op("act", lambda e: e.activation(out=att[bl][:, :], in_=arg_sb[bl][:, :], func=AF.Exp),
                 reads=[("sb_arg", bl)], writes=[("sb_att", bl)])
            off = kb - 4 * qc
            if off >= 0:
                m0 = 384 - off * 128
                S.op("pool", lambda e: e.tensor_tensor(out=att[bl][:, :], in0=att[bl][:, :], in1=mbig[:, m0:m0 + 512], op=ALU.mult),
                     reads=[("sb_att", bl), "sb_mbig"], writes=[("sb_att", bl)])
            S.op("pe", lambda e: e.matmul(C.psum[po][:, :], lhsT=vh[hi][:, kb, :], rhs=att[bl][:, :],
                                          start=(kb == first), stop=(kb == 0)),
                 reads=[("sbv", hi), ("sb_att", bl)], writes=[("ps", po)])
            if kb == 0:
                S.op("dve", lambda e: e.tensor_tensor(out=ost[oi][:, :], in0=C.psum[po][:, :], in1=gs[hi][:, qc * 512:(qc + 1) * 512], op=ALU.mult),
                     reads=[("ps", po), ("sbg", hi)], writes=[("sb_ost", oi)])
                S.dma("sp", lambda e: e.dma_start(out=R.yT[RW + h * 128:RW + (h + 1) * 128, qc * 512:(qc + 1) * 512], in_=ost[oi][:, :]),
                      reads=[("sb_ost", oi)], pwrites=[("dram", R.yT.name)])

        items = []
        n = 0
        for h in range(C.sb_heads):
            hi = h % 2
            for qc in range(4):
                first = 4 * qc + 3
                g = h * 4 + qc
                for kb in range(first, -1, -1):
                    items.append(dict(h=h, hi=hi, qc=qc, kb=kb, first=first, bz=n % 3, be=n % 2, bl=n % 2, ai=g % 2, oi=g % 2,
                                      abi=(n - 1) % 2, nabi=n % 2))
                    n += 1

        def load_head(h):
            hi = h % 2
            S.dma("sp", lambda e: e.dma_start(out=qT[hi][:, :], in_=R.qT[h * 128:(h + 1) * 128, :]),
                  reads=[("dram", R.qT.name)], writes=[("sbq", hi)])
            S.dma("sp", lambda e: e.dma_start(out=kT[hi][:, :], in_=R.ksT[h * 128:(h + 1) * 128, :]),
                  reads=[("dram", R.ksT.name)], writes=[("sbk", hi)])
            S.dma("sp", lambda e: e.dma_start(out=vh[hi][:, :, :], in_=vv[:, :, h * 128:(h + 1) * 128]),
                  reads=[("dram", R.vstok.name)], writes=[("sbv", hi)])
            S.dma("sp", lambda e: e.dma_start(out=gs[hi][:, :], in_=R.gsT[h * 128:(h + 1) * 128, :]),
                  reads=[("dram", R.gsT.name)], writes=[("sbg", hi)])
            S.op("act", lambda e: e.activation(out=gs[hi][:, :], in_=gs[hi][:, :], func=AF.Silu),
                 reads=[("sbg", hi)], writes=[("sbg", hi)])
            S.op("act", lambda e: e.activation(out=kTs[hi][:, :], in_=kT[hi][:, :], func=AF.Copy, scale=scale),
                 reads=[("sbk", hi)], writes=[("sbks", hi)])

        NI = len(items)
        loaded = set()
        for s in range(NI + 3):
            if s < NI:
                hh = items[s]["h"]
                for h2 in (hh, hh + 1):
                    if h2 < C.sb_heads and h2 not in loaded and (h2 == hh or items[s]["qc"] >= 2):
                        load_head(h2)
                        loaded.add(h2)
                f_z(items[s])
            if 0 <= s - 1 < NI:
                f_sp(items[s - 1])
            if 0 <= s - 2 < NI:
                f_later(items[s - 2])
            if 0 <= s - 3 < NI:
                f_att(items[s - 3])
            yield_point = None


def phase_sgu(C):
    S, nc, I, R = C.S, C.nc, C.I, C.R
    with ExitStack() as es:
        def sb(name, shape, dt=F32):
            return es.enter_context(nc.sbuf_tensor(name, list(shape), dt))
        lng = sb("lng", [128, D])
        lnb = sb("lnb", [128, D])
        wsf = sb("wsf", [128, 16, 128])
        wsb = sb("wsb", [128, 16, 128], BF16)
        msk = sb("sg_msk", [128, 128])
        bsb = sb("bsb", [128, 16, 128])
        vb = [sb(f"vb{i}", [128, D]) for i in range(2)]
        vnb = [sb(f"vnb{i}", [128, D], BF16) for i in range(2)]
        stats = sb("sg_stats", [128, 8, 6])
        mv = sb("sg_mv", [128, 2])
        rs = sb("sg_rs", [128, 1])
        epsc = sb("sg_eps", [128, 1])
        ub = [sb(f"ub{i}", [128, 4, 128]) for i in range(8)]
        gb = [sb(f"gb{i}", [128, 4, 128]) for i in range(8)]
        mb = [sb(f"mb{i}", [128, 4, 128]) for i in range(2)]
        yb = [sb(f"yb{i}", [128, 4, 128], BF16) for i in range(2)]

        S.dma("sp", lambda e: e.dma_start(out=lng[:, :], in_=I.o_ln[0:1, :].partition_broadcast(128)), writes=["lng"])
        S.dma("sp", lambda e: e.dma_start(out=lnb[:, :], in_=I.o_ln[1:2, :].partition_broadcast(128)), writes=["lnb"])
        S.dma("sp", lambda e: e.dma_start(out=bsb[:, :, :].rearrange("p g t -> p (g t)"), in_=I.o_bs[0:1, :].partition_broadcast(128)), writes=["bsb"])
        S.dma("sp", lambda e: e.dma_start(out=wsf[:, :, :], in_=I.o_wsT[:, :, :]), writes=["wsf"])
        S.dma("sp", lambda e: e.dma_start(out=msk[:, :], in_=I.consts[:, M2:M2 + 128]), writes=["sg_msk"])
        S.op("pool", lambda e: e.memset(epsc[:, :], LN_EPS), writes=["sg_eps"])
        for g in range(16):
            S.op("dve", lambda e, g=g: e.tensor_tensor(out=wsb[:, g, :], in0=wsf[:, g, :], in1=msk[:, :], op=ALU.mult),
                 reads=["wsf", "sg_msk"], pwrites=["wsb"])

        uv = R.uT.rearrange("(fc p) t -> p fc t", p=128)
        gv = R.ggT.rearrange("(fc p) t -> p fc t", p=128)
        yv = R.y2T.rearrange("(fc p) t -> p fc t", p=128)
        rs_ = C.rs
        for c in range(16):
            tok = slice(c * 128, (c + 1) * 128)
            vi = c % 2
            v_, vn_ = vb[vi], vnb[vi]
            for q in range(4):
                S.dma("sp", lambda e, v_=v_, q=q, tok=tok: e.dma_start(out=v_[:, q * 1024:(q + 1) * 1024], in_=R.vtk[tok, q * 1024:(q + 1) * 1024]),
                      reads=[("dram", R.vtk.name)], pwrites=[("vb", vi)])
            S.op("act", lambda e, v_=v_: e.activation(out=v_[:, :], in_=v_[:, :], func=AF.Gelu),
                 reads=[("vb", vi)], writes=[("vb", vi)])
            for q in range(8):
                S.op("dve", lambda e, v_=v_, q=q: e.bn_stats(out=stats[:, q, :], in_=v_[:, q * 512:(q + 1) * 512]),
                     reads=[("vb", vi)], pwrites=["sg_stats"])
            S.op("dve", lambda e: e.bn_aggr(out=mv[:, :], in_=stats[:, :, :].rearrange("p a b -> p (a b)")),
                 reads=["sg_stats"], writes=["sg_mv"])
            S.op("act", lambda e: e.activation(out=rs[:, :], in_=mv[:, 1:2], func=AF.Sqrt, bias=epsc[:, 0:1], scale=1.0),
                 reads=["sg_mv", "sg_eps"], writes=["sg_rs"])
            S.op("dve", lambda e: e.reciprocal(out=rs[:, :], in_=rs[:, :]), reads=["sg_rs"], writes=["sg_rs"])
            S.op("dve", lambda e, v_=v_: e.tensor_scalar(out=v_[:, :], in0=v_[:, :], scalar1=mv[:, 0:1], scalar2=rs[:, 0:1],
                                                        op0=ALU.subtract, op1=ALU.mult),
                 reads=[("vb", vi), "sg_mv", "sg_rs"], writes=[("vb", vi)])
            S.op("pool", lambda e, v_=v_: e.tensor_tensor(out=v_[:, :], in0=v_[:, :], in1=lng[:, :], op=ALU.mult),
                 reads=[("vb", vi), "lng"], writes=[("vb", vi)])
            S.op("dve", lambda e, v_=v_, vn_=vn_: e.tensor_tensor(out=vn_[:, :], in0=v_[:, :], in1=lnb[:, :], op=ALU.add),
                 reads=[("vb", vi), "lnb"], writes=[("vnb", vi)])
            for f4 in range(8):
                u_, g_ = ub[f4], gb[f4]
                S.dma("sp", lambda e, u_=u_, f4=f4, tok=tok: e.dma_start(out=u_[:, :, :], in_=uv[:, f4 * 4:(f4 + 1) * 4, tok]),
                      reads=[("dram", R.uT.name)], writes=[("sgu", f4)])
                S.dma("sp", lambda e, g_=g_, f4=f4, tok=tok: e.dma_start(out=g_[:, :, :], in_=gv[:, f4 * 4:(f4 + 1) * 4, tok]),
                      reads=[("dram", R.ggT.name)], writes=[("sgg", f4)])
                S.op("act", lambda e, u_=u_: e.activation(out=u_[:, :, :], in_=u_[:, :, :], func=AF.Gelu),
                     reads=[("sgu", f4)], writes=[("sgu", f4)])
            for f4 in range(8):
                g_ = gb[f4]
                S.op("act", lambda e, g_=g_: e.activation(out=g_[:, :, :], in_=g_[:, :, :], func=AF.Silu),
                     reads=[("sgg", f4)], writes=[("sgg", f4)])
            for f4 in range(8):
                u_, g_ = ub[f4], gb[f4]
                bi = f4 % 2
                m_, y_ = mb[bi], yb[bi]
                pi = _rot(C.psum, rs_, "psm")
                ps = C.psum[pi]

                def mm(e, ps=ps, vn_=vn_, f4=f4):
                    ins = None
                    for k in range(4):
                        fc = f4 * 4 + k
                        ins = e.matmul(ps[:, k * 128:(k + 1) * 128], lhsT=vn_[:, fc * 128:(fc + 1) * 128],
                                       rhs=wsb[:, fc // 2, :], start=True, stop=True)
                    return ins
                S.op("pe", mm, reads=[("vnb", vi), "wsb"], writes=[("ps", pi)])
                g0 = f4 * 2
                S.op("dve", lambda e, ps=ps, m_=m_, g0=g0: e.tensor_tensor(
                    out=m_[:, :, :].rearrange("p (a b) t -> p a b t", a=2), in0=ps[:, :].rearrange("p (a b t) -> p a b t", a=2, b=2),
                    in1=bsb[:, g0:g0 + 2, :].unsqueeze(2).broadcast_to([128, 2, 2, 128]), op=ALU.add),
                    reads=[("ps", pi), "bsb"], writes=[("sgm", bi)])
                S.op("dve", lambda e, m_=m_, u_=u_: e.tensor_tensor(out=m_[:, :, :], in0=m_[:, :, :], in1=u_[:, :, :], op=ALU.mult),
                     reads=[("sgm", bi), ("sgu", f4)], writes=[("sgm", bi)])
                S.op("dve", lambda e, m_=m_, g_=g_, y_=y_: e.tensor_tensor(out=y_[:, :, :], in0=m_[:, :, :], in1=g_[:, :, :], op=ALU.mult),
                     reads=[("sgm", bi), ("sgg", f4)], writes=[("sgy", bi)])
                S.dma("sp", lambda e, y_=y_, f4=f4, tok=tok: e.dma_start(out=yv[:, f4 * 4:(f4 + 1) * 4, tok], in_=y_[:, :, :]),
                      reads=[("sgy", bi)], pwrites=[("dram", R.y2T.name)])


def build(phases=("p1", "p2", "p3", "p4", "p5", "p6", "p7"), debug_out=()):
    nc = bass.Bass("TRN2", target_bir_lowering=False)
    C = Ctx()
    C.nc = nc
    C.rs = {}

    def din(name, shape, dt=F32):
        return nc.dram_tensor(name, list(shape), dt, kind="ExternalInput").ap()

    def dscr(name, shape, dt=F32):
        kind = "ExternalOutput" if name in debug_out else "Internal"
        return nc.dram_tensor(name, list(shape), dt, kind=kind).ap()

    I = Ctx()
    C.I = I
    I.xT = din("xT", [D, T])
    I.ng = din("ng", [3, 128, 32])
    I.e_w_in = din("e_w_in", [D, EC])
    I.e_w_out = din("e_w_out", [D, D])
    I.o_w_in = din("o_w_in", [D, OC])
    I.o_w_out = din("o_w_out", [D, D])
    I.e_vec = din("e_vec", [10, 128, 16])
    I.e_mu_v = din("e_mu_v", [16, RW])
    I.e_gn = din("e_gn", [2, 16, RW])
    I.e_wdu = din("e_wdu", [128, RW])
    I.e_aup = din("e_aup", [128, RW])
    I.o_ln = din("o_ln", [2, D])
    I.o_wsT = din("o_wsT", [128, 16, 128])
    I.o_bs = din("o_bs", [1, 16 * 128])
    I.consts = din("consts", [128, NCONST])
    I.outT = nc.dram_tensor("outT", [D, T], F32, kind="ExternalOutput").ap()

    R = Ctx()
    C.R = R
    R.rT = dscr("s_rT", [RW, T])
    R.kT = dscr("s_kT", [RW, T])
    R.vtok = dscr("s_vtok", [T + 1, RW])
    R.loT = dscr("s_loT", [256, T])
    R.gtok = dscr("s_gtok", [T, RW])
    R.qT = dscr("s_qT", [SBW, T], BF16)
    R.ksT = dscr("s_ksT", [SBW, T], BF16)
    R.vstok = dscr("s_vstok", [T, SBW], BF16)
    R.gsT = dscr("s_gsT", [SBW, T])
    R.yT = dscr("s_yT", [D, T], BF16)
    R.x1T = dscr("s_x1T", [D, T])
    R.uT = dscr("s_uT", [D, T])
    R.vtk = dscr("s_vtk", [T, D])
    R.ggT = dscr("s_ggT", [D, T])
    R.y2T = dscr("s_y2T", [D, T], BF16)
    R.x2T = dscr("s_x2T", [D, T])

    with ExitStack() as es:
        S = Sched(nc, es)
        C.S = S

        def sb(name, shape, dt=F32):
            return es.enter_context(nc.sbuf_tensor(name, list(shape), dt))

        C.psum = [es.enter_context(nc.psum_tensor(f"ps{i}", [128, 512], F32)) for i in range(8)]
        C.ones_f = sb("ones_f", [128, 128])
        C.eps_rms = sb("eps_rms", [128, 1])
        C.ngs = sb("ngs", [128, 3, 32])
        S.op("pool", lambda e: e.memset(C.ones_f[:, :], 1.0), writes=["ones_f"])
        S.op("pool", lambda e: e.memset(C.eps_rms[:, :], RMS_EPS), writes=["eps_rms"])
        S.dma("sp", lambda e: e.dma_start(out=C.ngs[:, :, :], in_=I.ng.rearrange("a p c -> p a c")), writes=["ngs"])
        C.rstd = sb("rstd", [128, 512])

        from contextlib import contextmanager

        @contextmanager
        def proj_bufs(tag):
            with ExitStack() as es2:
                def sb2(name, shape, dt=F32):
                    return es2.enter_context(nc.sbuf_tensor(name + tag, list(shape), dt))
                C.hT = sb2("hT", [128, 32, 1024], BF16)
                C.wbuf = [sb2(f"wbuf{i}", [128, 32, 512], BF16) for i in range(2)]
                C.xst = [sb2(f"xst{i}", [128, 4, 512]) for i in range(3)]
                C.sqb = [sb2(f"sqb{i}", [128, 512]) for i in range(2)]
                C.stF = [sb2(f"stF{i}", [128, 1024]) for i in range(3)]
                C.stB = [sb2(f"stB{i}", [128, 1024], BF16) for i in range(3)]
                yield
                S.barrier()

        x1src = I.xT if C.skip_layer0 else R.x1T
        if "p1" in phases:
            with proj_bufs("a"):
                groups = [
                    dict(c0=0, n=RW, mode="F", dst=R.rT, dt=F32),
                    dict(c0=RW, n=RW, mode="F", dst=R.kT, dt=F32),
                    dict(c0=2 * RW, n=RW, mode="T", dst=R.vtok, dt=F32, row_off=1),
                    dict(c0=3 * RW, n=256, mode="F", dst=R.loT, dt=F32),
                    dict(c0=3 * RW + 256, n=RW, mode="T", dst=R.gtok, dt=F32),
                    dict(c0=4 * RW + 256, n=SBW, mode="F", dst=R.qT, dt=BF16),
                    dict(c0=4 * RW + 256 + SBW, n=SBW, mode="F", dst=R.ksT, dt=BF16),
                    dict(c0=4 * RW + 256 + 2 * SBW, n=SBW, mode="T", dst=R.vstok, dt=BF16),
                    dict(c0=4 * RW + 256 + 3 * SBW, n=SBW, mode="F", dst=R.gsT, dt=F32),
                ]
                if C.p1_groups is not None:
                    groups = [groups[i] for i in C.p1_groups]
                for half in range(2):
                    t0 = half * 1024
                    norm_tokens(C, I.xT, C.ngs[:, 0, :], t0, 1024, hT=C.hT, hkey="hT")
                    project(C, C.hT, "hT", I.e_w_in, groups, t0)
        if "p2" in phases:
            phase_rwkv(C)
            S.barrier()
        if "p3" in phases:
            phase_sb(C)
            S.barrier()
        if "p4" in phases or "p5" in phases:
            with proj_bufs("b"):
                if "p4" in phases:
                    for half in range(2):
                        t0 = half * 1024
                        load_actT(C, R.yT, t0)
                        project(C, C.hT, "hT", I.e_w_out,
                                [dict(c0=0, n=D, mode="F", dst=R.x1T, dt=F32, resid=I.xT)], t0)
                if "p5" in phases:
                    groups = [
                        dict(c0=0, n=D, mode="F", dst=R.uT, dt=F32),
                        dict(c0=D, n=D, mode="T", dst=R.vtk, dt=F32),
                        dict(c0=2 * D, n=D, mode="F", dst=R.ggT, dt=F32),
                    ]
                    for half in range(2):
                        t0 = half * 1024
                        norm_tokens(C, x1src, C.ngs[:, 1, :], t0, 1024, hT=C.hT, hkey="hT")
                        project(C, C.hT, "hT", I.o_w_in, groups, t0)
        if "p6" in phases:
            phase_sgu(C)
            S.barrier()
        if "p7" in phases:
            with proj_bufs("c"):
                for half in range(2):
                    t0 = half * 1024
                    load_actT(C, R.y2T, t0)
                    project(C, C.hT, "hT", I.o_w_out,
                            [dict(c0=0, n=D, mode="F", dst=R.x2T, dt=F32, resid=x1src)], t0)
                norm_tokens(C, R.x2T, C.ngs[:, 2, :], 0, T, dstT=I.outT)

        S.final_wait("sp", S.all_tokens())
        S.emit()
    return nc


Ctx.p1_groups = None
Ctx.rk_pairs = 16
Ctx.rk_stop = 0
Ctx.sb_heads = 16
Ctx.skip_layer0 = False


def _consts():
    p = np.arange(128)[:, None]
    f = np.arange(128)[None, :]
    c = np.zeros((128, NCONST), np.float32)
    c[:, M0:M0 + 128] = (p > f)
    c[:, M1:M1 + 128] = (f > p)
    c[:, M2:M2 + 128] = (f >= p)
    c[:, M3:M3 + 128] = (p >= f)
    c[:, IDO:IDO + 128] = (p == f)
    c[:, BDO:BDO + 128] = ((p // 64) == (f // 64))
    c[:, INDO:INDO + 2] = ((p // 64) == np.arange(2)[None, :])
    fb = np.arange(896)[None, :]
    c[:, MBIG:MBIG + 896] = ((fb - p) > 384)
    t = np.arange(2048)[None, :]
    c[:, RMASK:RMASK + 2048] = np.broadcast_to((t % 128) != 0, (128, 2048))
    return c


def make_shared(inp):
    f = lambda a: np.asarray(a, dtype=np.float32)
    sh = {}
    pc = lambda v: np.ascontiguousarray(f(v).reshape(-1, 128).T)
    ng = f(inp["norm_g"])
    sh["ng"] = np.stack([pc(ng[0]), pc(ng[1]), pc(inp["final_norm_g"])])
    sh["e_w_in"] = f(inp["e_w_in"])[0]
    sh["e_w_out"] = f(inp["e_w_out"])[0]
    sh["o_w_in"] = f(inp["o_w_in"])[0]
    sh["o_w_out"] = f(inp["o_w_out"])[0]
    mu = f(inp["e_shift_mu"])[0]
    ev = np.zeros((10, 128, 16), np.float32)
    ev[0] = pc(inp["e_w0"][0]); ev[1] = pc(inp["e_a0"][0]); ev[2] = pc(inp["e_k_k"][0])
    ev[3] = pc(inp["e_k_a"][0]); ev[4] = pc(inp["e_r_k"][0])
    ev[5] = pc(mu[0:2048]); ev[6] = pc(mu[2048:4096])
    ev[7, :, 0] = mu[6144:6272]; ev[8, :, 0] = mu[6272:6400]
    sh["e_vec"] = ev
    rep = lambda v: np.ascontiguousarray(np.tile(f(v).reshape(16, 1, 128), (1, 16, 1)).reshape(16, 2048))
    sh["e_mu_v"] = rep(mu[4096:6144])
    sh["e_gn"] = np.stack([rep(inp["e_gn_g"][0]), rep(inp["e_gn_b"][0])])
    sh["e_wdu"] = f(inp["e_w_decay_up"])[0]
    sh["e_aup"] = f(inp["e_a_up"])[0]
    sh["o_ln"] = np.stack([f(inp["o_ln_g"])[0], f(inp["o_ln_b"])[0]])
    sh["o_wsT"] = np.ascontiguousarray(f(inp["o_w_s"])[0].transpose(2, 0, 1))
    sh["o_bs"] = np.ascontiguousarray(f(inp["o_b_s"])[0].reshape(1, -1))
    sh["consts"] = _consts()
    return sh


_NC_CACHE = {}


def kernel(**inputs):
    x = np.asarray(inputs["x"], dtype=np.float32)
    sh = make_shared(inputs)
    if "nc" not in _NC_CACHE:
        _NC_CACHE["nc"] = build()
    nc = _NC_CACHE["nc"]
    in_maps = []
    for b in range(NB):
        m = dict(sh)
        m["xT"] = np.ascontiguousarray(x[b].T)
        in_maps.append(m)
    res = run_bass_kernel_spmd(nc, in_maps, core_ids=list(range(NB)))
    out = np.stack([np.ascontiguousarray(np.asarray(r["outT"]).T) for r in res.results])
    return out.astype(np.float32)
```

```python
import numpy as np
from contextlib import ExitStack
import concourse.bass as bass
import concourse.mybir as mybir
from concourse.bass_utils import run_bass_kernel_spmd

F32 = mybir.dt.float32
BF16 = mybir.dt.bfloat16
AF = mybir.ActivationFunctionType
ALU = mybir.AluOpType
AX = mybir.AxisListType

D = 4096
T = 2048
NB = 8
RW = 2048
SBW = 2048
EC = 16640
OC = 12288
RMS_EPS = 1e-6
GN_EPS = 64e-5
LN_EPS = 1e-5
M0, M1, M2, M3, IDO, BDO, INDO, MBIG, RMASK = 0, 128, 256, 384, 512, 640, 768, 772, 1668
NCONST = 1668 + 2048
NSLOT = 4


class Sched:
    ENG = ("pe", "act", "dve", "pool", "sp")

    def __init__(self, nc, es, n_dma=12):
        self.nc = nc
        self.semobj = {}
        self.cnt = {}
        for e in self.ENG:
            self.semobj["s_" + e] = es.enter_context(nc.semaphore("s_" + e))
            self.cnt[e] = 0
        self.dq = {}
        for q in ("sp", "pool"):
            names = []
            for i in range(n_dma):
                nm = f"d_{q}{i}"
                self.semobj[nm] = es.enter_context(nc.semaphore(nm))
                names.append(nm)
            self.dq[q] = {"names": names, "val": [0] * n_dma, "next": 0}
        self.ops = {e: [] for e in self.ENG}
        self.wm = {e: {} for e in self.ENG}
        self.lastw = {}
        self.readers = {}

    def _deps(self, reads, writes, pwrites):
        deps = []
        for k in reads:
            deps.extend(self.lastw.get(k, {}).items())
        for k in writes:
            deps.extend(self.lastw.get(k, {}).items())
            deps.extend(self.readers.get(k, {}).items())
        for k in pwrites:
            deps.extend(self.readers.get(k, {}).items())
        return deps

    def _waits(self, eng, deps):
        need = {}
        for (s, v) in deps:
            if eng == "pe" and s == "s_pe":
                continue
            if self.wm[eng].get(s, 0) >= v:
                continue
            if need.get(s, 0) < v:
                need[s] = v
        for s, v in need.items():
            self.wm[eng][s] = v
        return list(need.items())

    def _commit(self, tok, reads, writes, pwrites):
        s, v = tok
        for k in writes:
            self.lastw[k] = {s: v}
            self.readers[k] = {}
        for k in pwrites:
            d = self.lastw.setdefault(k, {})
            d[s] = max(d.get(s, 0), v)
        for k in reads:
            d = self.readers.setdefault(k, {})
            d[s] = max(d.get(s, 0), v)

    def op(self, eng, fn, reads=(), writes=(), pwrites=()):
        reads, writes, pwrites = tuple(reads), tuple(writes), tuple(pwrites)
        waits = self._waits(eng, self._deps(reads, writes, pwrites))
        self.cnt[eng] += 1
        tok = ("s_" + eng, self.cnt[eng])
        self.ops[eng].append((waits, fn, "s_" + eng, 1))
        self._commit(tok, reads, writes, pwrites)
        return tok

    def dma(self, q, fn, reads=(), writes=(), pwrites=()):
        reads, writes, pwrites = tuple(reads), tuple(writes), tuple(pwrites)
        d = self.dq[q]
        i = d["next"]
        d["next"] = (i + 1) % len(d["names"])
        deps = self._deps(reads, writes, pwrites)
        if d["val"][i] > 0:
            deps.append((d["names"][i], d["val"][i]))
        waits = self._waits(q, deps)
        d["val"][i] += 16
        tok = (d["names"][i], d["val"][i])
        self.ops[q].append((waits, fn, d["names"][i], 16))
        self._commit(tok, reads, writes, pwrites)
        return tok

    def barrier(self):
        toks = self.all_tokens()
        for e in self.ENG:
            self.final_wait(e, toks)

    def final_wait(self, eng, toks):
        waits = self._waits(eng, list(toks))
        self.ops[eng].append((waits, None, None, 0))

    def all_tokens(self):
        toks = []
        for e in self.ENG:
            if self.cnt[e]:
                toks.append(("s_" + e, self.cnt[e]))
        for q in self.dq.values():
            for nm, v in zip(q["names"], q["val"]):
                if v:
                    toks.append((nm, v))
        return toks

    def emit(self):
        nc = self.nc
        engmap = {}

        def run(name, eng):
            for (waits, fn, sem, inc) in self.ops[name]:
                for (s, v) in waits:
                    eng.wait_ge(self.semobj[s], v)
                if fn is None:
                    continue
                ins = fn(eng)
                ins.then_inc(self.semobj[sem], inc)

        with nc.Block() as block:
            @block.tensor
            def _(e):
                run("pe", e)

            @block.scalar
            def _(e):
                run("act", e)

            @block.vector
            def _(e):
                run("dve", e)

            @block.gpsimd
            def _(e):
                run("pool", e)

            @block.sync
            def _(e):
                run("sp", e)


class Ctx:
    pass


def _rot(lst, state, name):
    i = state.get(name, 0)
    state[name] = (i + 1) % len(lst)
    return i


def norm_tokens(C, srcT, gcol, t0, ntok, hT=None, hkey=None, dstT=None):
    S = C.S
    nc = C.nc
    src_v = srcT.rearrange("(c p) t -> p c t", p=128)
    dst_v = dstT.rearrange("(c p) t -> p c t", p=128) if dstT is not None else None
    for tt in range(ntok // 512):
        tok = slice(t0 + tt * 512, t0 + (tt + 1) * 512)
        ps = C.psum[_rot(C.psum, C.rs, "psn")]
        pskey = ("ps", C.psum.index(ps))
        for c4 in range(8):
            xi = _rot(C.xst, C.rs, "xst")
            xb = C.xst[xi]
            S.dma("sp", lambda e, xb=xb, c4=c4, tok=tok: e.dma_start(out=xb[:, :, :], in_=src_v[:, c4 * 4:(c4 + 1) * 4, tok]),
                  reads=[("dram", srcT.name)], writes=[("xst", xi)])
            for cc in range(4):
                c = c4 * 4 + cc
                qi = _rot(C.sqb, C.rs, "sqb")
                qb = C.sqb[qi]
                S.op("act", lambda e, qb=qb, xb=xb, cc=cc: e.activation(out=qb[:, :], in_=xb[:, cc, :], func=AF.Square),
                     reads=[("xst", xi)], writes=[("sqb", qi)])
                S.op("pe", lambda e, ps=ps, qb=qb, c=c: e.matmul(ps[:, :], lhsT=C.ones_f[:, :], rhs=qb[:, :], start=(c == 0), stop=(c == 31)),
                     reads=[("sqb", qi), "ones_f"], writes=[pskey])
        S.op("act", lambda e, ps=ps: e.activation(out=C.rstd[:, :], in_=ps[:, :], func=AF.Sqrt, bias=C.eps_rms[:, 0:1], scale=1.0 / D),
             reads=[pskey, "eps_rms"], writes=["rstd"])
        S.op("dve", lambda e: e.reciprocal(out=C.rstd[:, :], in_=C.rstd[:, :]), reads=["rstd"], writes=["rstd"])
        for c4 in range(8):
            xi = _rot(C.xst, C.rs, "xst")
            xb = C.xst[xi]
            S.dma("sp", lambda e, xb=xb, c4=c4, tok=tok: e.dma_start(out=xb[:, :, :], in_=src_v[:, c4 * 4:(c4 + 1) * 4, tok]),
                  reads=[("dram", srcT.name)], writes=[("xst", xi)])
            if hT is not None:
                for cc in range(4):
                    c = c4 * 4 + cc
                    S.op("dve", lambda e, xb=xb, cc=cc, c=c, tt=tt: e.scalar_tensor_tensor(
                        out=hT[:, c, tt * 512:(tt + 1) * 512], in0=xb[:, cc, :], scalar=gcol[:, c:c + 1],
                        in1=C.rstd[:, :], op0=ALU.mult, op1=ALU.mult),
                        reads=[("xst", xi), "rstd", "ngs"], pwrites=[hkey])
            else:
                for cc in range(4):
                    c = c4 * 4 + cc
                    S.op("dve", lambda e, xb=xb, cc=cc, c=c: e.scalar_tensor_tensor(
                        out=xb[:, cc, :], in0=xb[:, cc, :], scalar=gcol[:, c:c + 1],
                        in1=C.rstd[:, :], op0=ALU.mult, op1=ALU.mult),
                        reads=[("xst", xi), "rstd", "ngs"], writes=[("xst", xi)])
                S.dma("sp", lambda e, xb=xb, c4=c4, tok=tok: e.dma_start(out=dst_v[:, c4 * 4:(c4 + 1) * 4, tok], in_=xb[:, :, :]),
                      reads=[("xst", xi)], pwrites=[("dram", dstT.name)])


def project(C, actT, akey, w, groups, t0):
    S = C.S
    wv = w.rearrange("(kc p) n -> p kc n", p=128)
    for g in groups:
        dst = g["dst"]
        dkey = ("dram", dst.name)
        for ct in range(0, g["n"], 512):
            ncol = min(512, g["n"] - ct)
            col0 = g["c0"] + ct
            wi = _rot(C.wbuf, C.rs, "wbuf")
            wb = C.wbuf[wi]
            for hk in range(2):
                S.dma("pool", lambda e, wb=wb, col0=col0, ncol=ncol, hk=hk: e.dma_start(
                    out=wb[:, hk * 16:(hk + 1) * 16, 0:ncol], in_=wv[:, hk * 16:(hk + 1) * 16, col0:col0 + ncol]),
                    reads=[("dram", w.name)], writes=[("wbuf", wi, hk)])
            wkeys = [("wbuf", wi, 0), ("wbuf", wi, 1)]
            if g["mode"] == "F":
                for cc in range(ncol // 128):
                    st_list = C.stF if g["dt"] == F32 else C.stB
                    sname = "stF" if g["dt"] == F32 else "stB"
                    si = _rot(st_list, C.rs, sname)
                    st = st_list[si]
                    skey = (sname, si)
                    row0 = ct + cc * 128
                    if g.get("resid") is not None:
                        res = g["resid"]
                        S.dma("sp", lambda e, st=st, res=res, row0=row0: e.dma_start(
                            out=st[:, :], in_=res[row0:row0 + 128, t0:t0 + 1024]),
                            reads=[("dram", res.name)], writes=[skey])
                    for tt in range(2):
                        pi = _rot(C.psum, C.rs, "psm")
                        ps = C.psum[pi]

                        def mm(e, ps=ps, wb=wb, cc=cc, tt=tt):
                            ins = None
                            for kc in range(32):
                                ins = e.matmul(ps[:, :], lhsT=wb[:, kc, cc * 128:(cc + 1) * 128],
                                               rhs=actT[:, kc, tt * 512:(tt + 1) * 512],
                                               start=(kc == 0), stop=(kc == 31))
                            return ins
                        S.op("pe", mm, reads=wkeys + [akey], writes=[("ps", pi)])
                        if g.get("resid") is not None:
                            S.op("dve", lambda e, ps=ps, st=st, tt=tt: e.tensor_tensor(
                                out=st[:, tt * 512:(tt + 1) * 512], in0=ps[:, :], in1=st[:, tt * 512:(tt + 1) * 512], op=ALU.add),
                                reads=[("ps", pi), skey], pwrites=[skey])
                        else:
                            ev = "act" if _rot([0, 1], C.rs, "evsel") == 0 else "dve"
                            if ev == "act":
                                S.op("act", lambda e, ps=ps, st=st, tt=tt: e.copy(out=st[:, tt * 512:(tt + 1) * 512], in_=ps[:, :]),
                                     reads=[("ps", pi)], pwrites=[skey])
                            else:
                                S.op("dve", lambda e, ps=ps, st=st, tt=tt: e.tensor_copy(out=st[:, tt * 512:(tt + 1) * 512], in_=ps[:, :]),
                                     reads=[("ps", pi)], pwrites=[skey])
                    S.dma("sp", lambda e, st=st, dst=dst, row0=row0: e.dma_start(
                        out=dst[row0:row0 + 128, t0:t0 + 1024], in_=st[:, :]),
                        reads=[skey], pwrites=[dkey])
            else:
                for tb in range(8):
                    st_list = C.stF if g["dt"] == F32 else C.stB
                    sname = "stF" if g["dt"] == F32 else "stB"
                    si = _rot(st_list, C.rs, sname)
                    st = st_list[si]
                    skey = (sname, si)
                    pi = _rot(C.psum, C.rs, "psm")
                    ps = C.psum[pi]

                    def mm(e, ps=ps, wb=wb, tb=tb, ncol=ncol):
                        ins = None
                        for kc in range(32):
                            ins = e.matmul(ps[:, 0:ncol], lhsT=actT[:, kc, tb * 128:(tb + 1) * 128],
                                           rhs=wb[:, kc, 0:ncol], start=(kc == 0), stop=(kc == 31))
                        return ins
                    S.op("pe", mm, reads=wkeys + [akey], writes=[("ps", pi)])
                    ev = "act" if _rot([0, 1], C.rs, "evsel") == 0 else "dve"
                    if ev == "act":
                        S.op("act", lambda e, ps=ps, st=st, ncol=ncol: e.copy(out=st[:, 0:ncol], in_=ps[:, 0:ncol]),
                             reads=[("ps", pi)], writes=[skey])
                    else:
                        S.op("dve", lambda e, ps=ps, st=st, ncol=ncol: e.tensor_copy(out=st[:, 0:ncol], in_=ps[:, 0:ncol]),
                             reads=[("ps", pi)], writes=[skey])
                    ro = g.get("row_off", 0)
                    S.dma("sp", lambda e, st=st, dst=dst, tb=tb, ct=ct, ncol=ncol, ro=ro: e.dma_start(
                        out=dst[ro + t0 + tb * 128:ro + t0 + (tb + 1) * 128, ct:ct + ncol], in_=st[:, 0:ncol]),
                        reads=[skey], pwrites=[dkey])


def load_actT(C, srcT, t0):
    S = C.S
    v = srcT.rearrange("(c p) t -> p c t", p=128)
    for c8 in range(4):
        S.dma("sp", lambda e, c8=c8: e.dma_start(out=C.hT[:, c8 * 8:(c8 + 1) * 8, :], in_=v[:, c8 * 8:(c8 + 1) * 8, t0:t0 + 1024]),
              reads=[("dram", srcT.name)], pwrites=["hT"])


def phase_rwkv(C):
    S, nc, I, R = C.S, C.nc, C.I, C.R
    rs_ = C.rs
    NEG_E = -float(np.exp(-0.5))
    with ExitStack() as es:
        def sb(name, shape, dt=F32):
            return es.enter_context(nc.sbuf_tensor("rk_" + name, list(shape), dt))
        vec = sb("vec", [128, 10, 16])
        wdu = sb("wdu", [128, RW])
        aup = sb("aup", [128, RW])
        bd = sb("bd", [128, 128])
        ind = sb("ind", [128, 2])
        identf = sb("identf", [128, 128])
        identb = sb("identb", [128, 128], BF16)
        rmask = sb("rmask", [128, T])
        mk4 = sb("mk4", [128, 512])
        m04 = sb("m04", [128, 4, 128])
        twlo = sb("twlo", [128, T])
        alo = sb("alo", [128, T])
        gneps = sb("gneps", [128, 1])
        G = [sb(f"G{i}", [128, T]) for i in range(8)]
        bt = sb("bt", [128, T], BF16)
        kt = sb("kt", [128, T], BF16)
        QT = sb("QT", [128, 16, 2, 128], BF16)
        Btok = sb("Btok", [128, 16, 128], BF16)
        Ktok = sb("Ktok", [128, 16, 128], BF16)
        Vb = sb("Vb", [128, 16, 128], BF16)
        ATall = sb("ATall", [128, 32, 512], BF16)
        MTall = sb("MTall", [128, 32, 128], BF16)
        Xg = [[sb(f"Xg{s}{i}", [128, 4, 128], BF16) for i in range(2)] for s in range(NSLOT)]
        XTg = [[sb(f"XTg{s}{i}", [128, 4, 128], BF16) for i in range(2)] for s in range(NSLOT)]
        gl = sb("gl", [128, 16])
        bon = sb("bon", [128, 32])
        st1 = sb("st1", [128, 32])
        st2 = sb("st2", [128, 32])
        muv = sb("muv", [128, 128])
        gng = sb("gng", [128, 128])
        gnb = sb("gnb", [128, 128])
        Hf = sb("Hf", [128, 128])
        Hb = sb("Hb", [128, 128], BF16)
        th = sb("th", [128, 128])
        RHSb = [sb(f"RHSb{i}", [128, 128], BF16) for i in range(2)]
        Ub = [sb(f"Ub{i}", [128, 128], BF16) for i in range(2)]

        def ld(dst, src, key):
            S.dma("sp", lambda e: e.dma_start(out=dst, in_=src), writes=[key])
        ld(vec[:, :, :], I.e_vec.rearrange("a p j -> p a j"), "vec")
        ld(wdu[:, :], I.e_wdu[:, :], "wdu")
        ld(aup[:, :], I.e_aup[:, :], "aup")
        ld(bd[:, :], I.consts[:, BDO:BDO + 128], "bd")
        ld(ind[:, :], I.consts[:, INDO:INDO + 2], "ind")
        ld(identf[:, :], I.consts[:, IDO:IDO + 128], "identf")
        ld(rmask[:, :], I.consts[:, RMASK:RMASK + T], "rmask")
        for q in range(4):
            S.dma("sp", lambda e, q=q: e.dma_start(out=m04[:, q, :], in_=I.consts[:, M0:M0 + 128]), pwrites=["m04"])
            mo = M1 if q % 2 == 0 else M2
            S.dma("sp", lambda e, q=q, mo=mo: e.dma_start(out=mk4[:, q * 128:(q + 1) * 128], in_=I.consts[:, mo:mo + 128]), pwrites=["mk4"])
        S.op("dve", lambda e: e.tensor_copy(out=identb[:, :], in_=identf[:, :]), reads=["identf"], writes=["identb"])
        S.op("pool", lambda e: e.memset(G[7][0:1, :], 0.0), writes=["G7"])
        S.op("pool", lambda e: e.memset(gneps[:, :], GN_EPS), writes=["gneps"])
        S.op("pool", lambda e: e.memset(th[:, :], 0.0), writes=["th"])
        S.op("dve", lambda e: e.tensor_copy(out=C.psum[7][:, 0:128], in_=th[:, :]), reads=["th"], writes=[("ps", 7)])
        S.dma("sp", lambda e: e.dma_start(out=R.vtok[0:1, :], in_=G[7][0:1, :]), reads=["G7"], pwrites=[("dram", R.vtok.name)])

        def lerpF(src, skey, mu_col, tmp, tkey):
            S.op("dve", lambda e: e.tensor_tensor(out=tmp[:, 1:T], in0=src[:, 0:T - 1], in1=src[:, 1:T], op=ALU.subtract),
                 reads=[skey], pwrites=[tkey])
            S.op("dve", lambda e: e.tensor_scalar(out=tmp[:, 0:1], in0=src[:, 0:1], scalar1=-1.0, scalar2=None, op0=ALU.mult),
                 reads=[skey], pwrites=[tkey])
            S.op("dve", lambda e: e.scalar_tensor_tensor(out=src[:, :], in0=tmp[:, :], scalar=mu_col, in1=src[:, :],
                                                         op0=ALU.mult, op1=ALU.add),
                 reads=[skey, tkey, "vec"], writes=[skey])

        ld(twlo[:, :], R.loT[0:128, :], "twlo")
        ld(alo[:, :], R.loT[128:256, :], "alo")
        lerpF(twlo, "twlo", vec[:, 7, 0:1], G[2], "G2")
        lerpF(alo, "alo", vec[:, 8, 0:1], G[2], "G2")
        S.op("act", lambda e: e.activation(out=twlo[:, :], in_=twlo[:, :], func=AF.Tanh), reads=["twlo"], writes=["twlo"])

        def psr():
            i = 2 + _rot(list(range(5)), rs_, "rkps")
            return i, C.psum[i]

        def g3(t):
            return t[:, :].rearrange("p (c f) -> p c f", c=16)

        def g32(t):
            return t[:, :].rearrange("p (c f) -> p c f", c=32)

        def pair(j):
            jc = slice(j * 128, (j + 1) * 128)
            r_, k_, tmp, lw, a_, kkn, c_, ex = G
            S.dma("sp", lambda e: e.dma_start(out=r_[:, :], in_=R.rT[jc, :]), reads=[("dram", R.rT.name)], writes=["G0"])
            S.dma("sp", lambda e: e.dma_start(out=k_[:, :], in_=R.kT[jc, :]), reads=[("dram", R.kT.name)], writes=["G1"])
            lerpF(r_, "G0", vec[:, 5, j:j + 1], tmp, "G2")
            lerpF(k_, "G1", vec[:, 6, j:j + 1], tmp, "G2")
            for tc in range(4):
                ts_ = slice(tc * 512, (tc + 1) * 512)
                pi, ps = psr()
                S.op("pe", lambda e, ps=ps, ts_=ts_: e.matmul(ps[:, :], lhsT=wdu[:, jc], rhs=twlo[:, ts_], start=True, stop=True),
                     reads=["wdu", "twlo"], writes=[("ps", pi)])
                S.op("act", lambda e, ps=ps, ts_=ts_: e.activation(out=lw[:, ts_], in_=ps[:, :], func=AF.Sigmoid, bias=vec[:, 0, j:j + 1], scale=1.0),
                     reads=[("ps", pi), "vec"], pwrites=["G3"])
                pi, ps = psr()
                S.op("pe", lambda e, ps=ps, ts_=ts_: e.matmul(ps[:, :], lhsT=aup[:, jc], rhs=alo[:, ts_], start=True, stop=True),
                     reads=["aup", "alo"], writes=[("ps", pi)])
                S.op("act", lambda e, ps=ps, ts_=ts_: e.activation(out=a_[:, ts_], in_=ps[:, :], func=AF.Sigmoid, bias=vec[:, 1, j:j + 1], scale=1.0),
                     reads=[("ps", pi), "vec"], pwrites=["G4"])
            S.op("act", lambda e: e.activation(out=tmp[:, :], in_=k_[:, :], func=AF.Identity, scale=vec[:, 2, j:j + 1]),
                 reads=["G1", "vec"], writes=["G2"])
            S.op("act", lambda e: e.activation(out=ex[:, :], in_=tmp[:, :], func=AF.Square), reads=["G2"], writes=["G7"])
            for tc in range(4):
                ts_ = slice(tc * 512, (tc + 1) * 512)
                pi, ps = psr()
                S.op("pe", lambda e, ps=ps, ts_=ts_: e.matmul(ps[:, :], lhsT=bd[:, :], rhs=ex[:, ts_], start=True, stop=True),
                     reads=["bd", "G7"], writes=[("ps", pi)])
                S.op("act", lambda e, ps=ps, ts_=ts_: e.activation(out=c_[:, ts_], in_=ps[:, :], func=AF.Sqrt),
                     reads=[("ps", pi)], pwrites=["G6"])
            S.op("dve", lambda e: e.tensor_scalar(out=c_[:, :], in0=c_[:, :], scalar1=1e-12, scalar2=None, op0=ALU.max),
                 reads=["G6"], writes=["G6"])
            S.op("dve", lambda e: e.reciprocal(out=c_[:, :], in_=c_[:, :]), reads=["G6"], writes=["G6"])
            S.op("dve", lambda e: e.tensor_tensor(out=kkn[:, :], in0=tmp[:, :], in1=c_[:, :], op=ALU.mult),
                 reads=["G2", "G6"], writes=["G5"])
            S.op("dve", lambda e: e.tensor_scalar(out=tmp[:, :], in0=a_[:, :], scalar1=-1.0, scalar2=vec[:, 3, j:j + 1], op0=ALU.add, op1=ALU.mult),
                 reads=["G4", "vec"], writes=["G2"])
            S.op("dve", lambda e: e.scalar_tensor_tensor(out=k_[:, :], in0=tmp[:, :], scalar=1.0, in1=k_[:, :], op0=ALU.add, op1=ALU.mult),
                 reads=["G2", "G1"], writes=["G1"])
            S.op("dve", lambda e: e.scalar_tensor_tensor(out=tmp[:, :], in0=r_[:, :], scalar=vec[:, 4, j:j + 1], in1=k_[:, :], op0=ALU.mult, op1=ALU.mult),
                 reads=["G0", "G1", "vec"], writes=["G2"])
            pib, psb = psr()

            def mmb(e, psb=psb):
                ins = None
                for c in range(16):
                    ins = e.matmul(psb[:, 2 * c:2 * c + 2], lhsT=tmp[:, c * 128:(c + 1) * 128], rhs=ind[:, :], start=True, stop=True)
                return ins
            S.op("pe", mmb, reads=["G2", "ind"], writes=[("ps", pib)])
            S.op("act", lambda e, psb=psb: e.copy(out=bon[:, :], in_=psb[:, 0:32]), reads=[("ps", pib)], writes=["bon"])
            S.op("dve", lambda e: e.tensor_tensor(out=a_[:, :], in0=kkn[:, :], in1=a_[:, :], op=ALU.mult), reads=["G5", "G4"], writes=["G4"])
            S.op("dve", lambda e: e.tensor_tensor_scan(out=c_[:, :], data0=rmask[:, :], data1=lw[:, :], initial=0.0, op0=ALU.mult, op1=ALU.add),
                 reads=["rmask", "G3"], writes=["G6"])
            S.op("dve", lambda e: e.tensor_tensor(out=lw[:, :], in0=c_[:, :], in1=lw[:, :], op=ALU.subtract), reads=["G6", "G3"], writes=["G3"])
            S.op("act", lambda e: e.activation(out=ex[:, :], in_=c_[:, :], func=AF.Exp, scale=NEG_E), reads=["G6"], writes=["G7"])
            S.op("dve", lambda e: e.tensor_tensor(out=QT[:, :, 1, :], in0=g3(r_), in1=g3(ex), op=ALU.mult), reads=["G0", "G7"], pwrites=["QT"])
            S.op("dve", lambda e: e.tensor_copy(out=gl[:, :], in_=g3(ex)[:, :, 127]), reads=["G7"], writes=["gl"])
            S.op("act", lambda e: e.activation(out=ex[:, :], in_=lw[:, :], func=AF.Exp, scale=NEG_E), reads=["G3"], writes=["G7"])
            S.op("dve", lambda e: e.scalar_tensor_tensor(out=QT[:, :, 0, :], in0=g3(kkn), scalar=-1.0, in1=g3(ex), op0=ALU.mult, op1=ALU.mult),
                 reads=["G5", "G7"], pwrites=["QT"])
            S.op("act", lambda e: e.activation(out=ex[:, :], in_=c_[:, :], func=AF.Exp, scale=-NEG_E), reads=["G6"], writes=["G7"])
            S.op("dve", lambda e: e.tensor_tensor(out=bt[:, :], in0=a_[:, :], in1=ex[:, :], op=ALU.mult), reads=["G4", "G7"], writes=["bt"])
            S.op("dve", lambda e: e.tensor_tensor(out=kt[:, :], in0=k_[:, :], in1=ex[:, :], op=ALU.mult), reads=["G1", "G7"], writes=["kt"])
            if C.rk_stop == 1:
                return
            for (srcT_, skey, dstk, dkey) in ((bt, "bt", Btok, "Btok"), (kt, "kt", Ktok, "Ktok")):
                for hf in range(2):
                    pi, ps = psr()
                    psv = ps[:, :].bitcast(BF16)

                    def tr(e, psv=psv, srcT_=srcT_, hf=hf):
                        ins = None
                        for cc in range(8):
                            c = hf * 8 + cc
                            ins = e.transpose(out=psv[:, cc * 128:(cc + 1) * 128], in_=srcT_[:, c * 128:(c + 1) * 128], identity=identb[:, :])
                        return ins
                    S.op("pe", tr, reads=[skey, "identb"], writes=[("ps", pi)])
                    S.op("act", lambda e, psv=psv, dstk=dstk, hf=hf: e.copy(
                        out=dstk[:, hf * 8:(hf + 1) * 8, :].rearrange("p c f -> p (c f)"), in_=psv[:, :]),
                        reads=[("ps", pi)], pwrites=[dkey])
            vcur, vprev, sg = G[0], G[1], G[2]
            S.dma("sp", lambda e: e.dma_start(out=g3(vcur), in_=R.vtok[1:T + 1, jc].rearrange("(c p) f -> p c f", p=128)),
                  reads=[("dram", R.vtok.name)], writes=["G0"])
            S.dma("sp", lambda e: e.dma_start(out=g3(vprev), in_=R.vtok[0:T, jc].rearrange("(c p) f -> p c f", p=128)),
                  reads=[("dram", R.vtok.name)], writes=["G1"])
            S.dma("sp", lambda e: e.dma_start(out=g3(sg), in_=R.gtok[:, jc].rearrange("(c p) f -> p c f", p=128)),
                  reads=[("dram", R.gtok.name)], writes=["G2"])
            S.dma("sp", lambda e: e.dma_start(out=muv[:, :], in_=I.e_mu_v[j:j + 1, 0:128].partition_broadcast(128)), writes=["muv"])
            S.dma("sp", lambda e: e.dma_start(out=gng[:, :], in_=I.e_gn[0, j:j + 1, 0:128].partition_broadcast(128)), writes=["gng"])
            S.dma("sp", lambda e: e.dma_start(out=gnb[:, :], in_=I.e_gn[1, j:j + 1, 0:128].partition_broadcast(128)), writes=["gnb"])
            bc16 = lambda t: t[:, :].unsqueeze(1).broadcast_to([128, 16, 128])
            S.op("pool", lambda e: e.tensor_tensor(out=vprev[:, :], in0=vprev[:, :], in1=vcur[:, :], op=ALU.subtract), reads=["G0", "G1"], writes=["G1"])
            S.op("pool", lambda e: e.tensor_tensor(out=g3(vprev), in0=g3(vprev), in1=bc16(muv), op=ALU.mult), reads=["G1", "muv"], writes=["G1"])
            S.op("pool", lambda e: e.tensor_tensor(out=vcur[:, :], in0=vcur[:, :], in1=vprev[:, :], op=ALU.add), reads=["G0", "G1"], writes=["G0"])
            S.op("pool", lambda e: e.tensor_copy(out=Vb[:, :, :], in_=g3(vcur)), reads=["G0"], writes=["Vb"])
            S.op("act", lambda e: e.activation(out=sg[:, :], in_=sg[:, :], func=AF.Silu), reads=["G2"], writes=["G2"])

            if C.rk_stop == 2:
                return
            def group(cg, slot):
                p0 = cg * 4
                X, XT = Xg[slot], XTg[slot]
                psBs = [psr(), psr()]
                for q in range(4):
                    hp = q // 2
                    c = cg * 2 + q % 2
                    rows = slice(64 * hp, 64 * hp + 64)
                    cs_ = slice(c * 128, (c + 1) * 128)
                    pia = hp
                    psA = C.psum[pia]
                    pib, psB = psBs[hp]
                    cl = q % 2

                    def mma(e, psA=psA, rows=rows, cs_=cs_, c=c):
                        qv = QT[rows, c, :, :].rearrange("p a t -> p (a t)")
                        e.matmul(psA[:, 0:256], lhsT=bt[rows, cs_], rhs=qv, start=True, stop=True)
                        return e.matmul(psA[:, 256:512], lhsT=kt[rows, cs_], rhs=qv, start=True, stop=True)
                    S.op("pe", mma, reads=["bt", "kt", "QT"], writes=[("ps", pia)])
                    S.op("dve", lambda e, psA=psA, q=q: e.tensor_tensor(out=ATall[:, p0 + q, :], in0=psA[:, :], in1=mk4[:, :], op=ALU.mult),
                         reads=[("ps", pia), "mk4"], pwrites=[("ATall", cg)])
                    S.op("pe", lambda e, psB=psB, rows=rows, cs_=cs_, c=c, cl=cl: e.matmul(
                        psB[:, cl * 128:(cl + 1) * 128], lhsT=QT[rows, c, 0, :], rhs=bt[rows, cs_], start=True, stop=True),
                        reads=["QT", "bt"], pwrites=[("ps", pib)])
                for hp in range(2):
                    pib, psB = psBs[hp]
                    S.op("dve", lambda e, psB=psB, hp=hp: e.tensor_tensor(
                        out=X[0][:, 2 * hp:2 * hp + 2, :].rearrange("p q f -> p (q f)"), in0=psB[:, 0:256],
                        in1=m04[:, 0:2, :].rearrange("p q f -> p (q f)"), op=ALU.mult),
                        reads=[("ps", pib), "m04"], pwrites=[("Xg", slot, 0)])
                S.op("dve", lambda e: e.tensor_tensor(out=MTall[:, p0:p0 + 4, :], in0=ATall[:, p0:p0 + 4, 0:128],
                                                      in1=identb[:, :].unsqueeze(1).broadcast_to([128, 4, 128]), op=ALU.add),
                     reads=[("ATall", cg), "identb"], writes=[("MTall", cg)])
                yield
                xi = 0
                xtb = None
                xtkey = ("ATall", cg)

                def xt_ap(level_buf, q):
                    if level_buf is None:
                        return ATall[:, p0 + q, 0:128]
                    return XT[level_buf][:, q, :]
                for lvl in range(1, 7):
                    nxi = 1 - xi
                    piX, psX = psr()

                    def mmx(e, psX=psX, xi=xi, xtb=xtb):
                        ins = None
                        for q in range(4):
                            ins = e.matmul(psX[:, q * 128:(q + 1) * 128], lhsT=xt_ap(xtb, q), rhs=X[xi][:, q, :], start=True, stop=True)
                        return ins
                    S.op("pe", mmx, reads=[xtkey, ("Xg", slot, xi)], writes=[("ps", piX)])
                    if lvl < 6:
                        piT, psT = psr()
                        nxt = 0 if xtb is None else 1 - xtb

                        def mmt(e, psT=psT, xi=xi, xtb=xtb):
                            ins = None
                            for q in range(4):
                                ins = e.matmul(psT[:, q * 128:(q + 1) * 128], lhsT=X[xi][:, q, :], rhs=xt_ap(xtb, q), start=True, stop=True)
                            return ins
                        S.op("pe", mmt, reads=[xtkey, ("Xg", slot, xi)], writes=[("ps", piT)])
                    S.op("act", lambda e, psX=psX, nxi=nxi: e.copy(out=X[nxi][:, :, :].rearrange("p q f -> p (q f)"), in_=psX[:, :]),
                         reads=[("ps", piX)], writes=[("Xg", slot, nxi)])
                    if lvl < 6:
                        S.op("act", lambda e, psT=psT, nxt=nxt: e.copy(out=XT[nxt][:, :, :].rearrange("p q f -> p (q f)"), in_=psT[:, :]),
                             reads=[("ps", piT)], writes=[("XTg", slot, nxt)])
                        xtb = nxt
                        xtkey = ("XTg", slot, nxt)
                    xi = nxi
                    piD, psD = psr()

                    def mmd(e, psD=psD, xi=xi):
                        ins = None
                        for q in range(4):
                            ins = e.matmul(psD[:, q * 128:(q + 1) * 128], lhsT=X[xi][:, q, :], rhs=MTall[:, p0 + q, :], start=True, stop=True)
                        return ins
                    S.op("pe", mmd, reads=[("Xg", slot, xi), ("MTall", cg)], writes=[("ps", piD)])
                    S.op("dve", lambda e, psD=psD: e.tensor_tensor(out=MTall[:, p0:p0 + 4, :].rearrange("p q f -> p (q f)"),
                                                                   in0=MTall[:, p0:p0 + 4, :].rearrange("p q f -> p (q f)"), in1=psD[:, :], op=ALU.add),
                         reads=[("MTall", cg), ("ps", piD)], writes=[("MTall", cg)])
                    yield

            if C.rk_stop == 3:
                return
            for gp_ in range(8 // NSLOT):
                gens = [group(NSLOT * gp_ + s_, s_) for s_ in range(NSLOT)]
                for _step in range(7):
                    for g_ in gens:
                        next(g_)
            if C.rk_stop == 4:
                return
            S.op("pool", lambda e: e.memset(Hf[:, :], 0.0), writes=["Hf"])
            S.op("pool", lambda e: e.memset(Hb[:, :], 0.0), writes=["Hb"])
            Yall = G[1]
            psH = C.psum[7]
            for c in range(16):
                cg = c // 2
                bi = c % 2
                ix = [(c // 2) * 4 + hp * 2 + (c % 2) for hp in range(2)]
                piR, psR = psr()

                def mmr(e, psR=psR, c=c, ix=ix):
                    ins = e.matmul(psR[:, 0:128], lhsT=QT[:, c, 0, :], rhs=Hb[:, :], start=True, stop=False)
                    for hp in range(2):
                        vs = slice(64 * hp, 64 * hp + 64)
                        ins = e.matmul(psR[:, vs], lhsT=ATall[:, ix[hp], 256:384], rhs=Vb[:, c, vs], start=False, stop=(hp == 1))
                    return ins
                S.op("pe", mmr, reads=["QT", "Hb", ("ATall", cg), "Vb"], writes=[("ps", piR)])
                S.op("act", lambda e, psR=psR, bi=bi: e.copy(out=RHSb[bi][:, :], in_=psR[:, 0:128]), reads=[("ps", piR)], writes=[("RHSb", bi)])
                piU, psU = psr()

                def mmu(e, psU=psU, c=c, bi=bi, ix=ix):
                    ins = None
                    for hp in range(2):
                        vs = slice(64 * hp, 64 * hp + 64)
                        ins = e.matmul(psU[:, vs], lhsT=MTall[:, ix[hp], :], rhs=RHSb[bi][:, vs], start=True, stop=True)
                    return ins
                S.op("pe", mmu, reads=[("MTall", cg), ("RHSb", bi)], writes=[("ps", piU)])
                S.op("dve", lambda e, psU=psU, bi=bi: e.tensor_copy(out=Ub[bi][:, :], in_=psU[:, 0:128]), reads=[("ps", piU)], writes=[("Ub", bi)])

                def mmh(e, c=c, bi=bi):
                    ins = None
                    for hp in range(2):
                        vs = slice(64 * hp, 64 * hp + 64)
                        e.matmul(psH[vs, vs], lhsT=Btok[:, c, vs], rhs=Ub[bi][:, vs], start=True, stop=False)
                        ins = e.matmul(psH[vs, vs], lhsT=Ktok[:, c, vs], rhs=Vb[:, c, vs], start=False, stop=True)
                    return ins
                S.op("pe", mmh, reads=["Btok", "Ktok", "Vb", ("Ub", bi)], writes=[("ps", 7)])
                piY, psY = psr()

                def mmy(e, psY=psY, c=c, bi=bi, ix=ix):
                    ins = e.matmul(psY[:, 0:128], lhsT=QT[:, c, 1, :], rhs=Hb[:, :], start=True, stop=False)
                    for hp in range(2):
                        vs = slice(64 * hp, 64 * hp + 64)
                        e.matmul(psY[:, vs], lhsT=ATall[:, ix[hp], 128:256], rhs=Ub[bi][:, vs], start=False, stop=False)
                        ins = e.matmul(psY[:, vs], lhsT=ATall[:, ix[hp], 384:512], rhs=Vb[:, c, vs], start=False, stop=(hp == 1))
                    return ins
                S.op("pe", mmy, reads=["QT", "Hb", ("ATall", cg), "Vb", ("Ub", bi)], writes=[("ps", piY)])
                S.op("act", lambda e, psY=psY, c=c: e.copy(out=Yall[:, c * 128:(c + 1) * 128], in_=psY[:, 0:128]),
                     reads=[("ps", piY)], pwrites=["G1"])
                S.op("dve", lambda e: e.tensor_tensor(out=th[:, :], in0=psH[:, 0:128], in1=Hf[:, :], op=ALU.add),
                     reads=[("ps", 7), "Hf"], writes=["th"])
                S.op("dve", lambda e, c=c: e.tensor_scalar(out=Hb[:, :], in0=th[:, :], scalar1=gl[:, c:c + 1], scalar2=None, op0=ALU.mult),
                     reads=["th", "gl"], writes=["Hb"])
                S.op("act", lambda e, c=c: e.activation(out=Hf[:, :], in_=th[:, :], func=AF.Identity, scale=gl[:, c:c + 1]),
                     reads=["th", "gl"], writes=["Hf"])

            if C.rk_stop == 5:
                return
            Y3 = g32(Yall)
            sq3 = g32(G[3])
            bc64 = lambda t: t[:, :].unsqueeze(2).broadcast_to([128, 32, 64])
            S.op("dve", lambda e: e.tensor_reduce(out=st1[:, :], in_=Y3, axis=AX.X, op=ALU.add), reads=["G1", "G1"], writes=["st1"])
            S.op("dve", lambda e: e.tensor_scalar(out=st1[:, :], in0=st1[:, :], scalar1=1.0 / 64, scalar2=None, op0=ALU.mult), reads=["st1"], writes=["st1"])
            S.op("dve", lambda e: e.tensor_tensor(out=Y3, in0=Y3, in1=bc64(st1), op=ALU.subtract), reads=["G1", "st1"], writes=["G1"])
            S.op("act", lambda e: e.activation(out=G[3][:, :], in_=Yall[:, :], func=AF.Square), reads=["G1"], writes=["G3"])
            S.op("dve", lambda e: e.tensor_reduce(out=st2[:, :], in_=sq3, axis=AX.X, op=ALU.add), reads=["G3"], writes=["st2"])
            S.op("act", lambda e: e.activation(out=st2[:, :], in_=st2[:, :], func=AF.Sqrt, bias=gneps[:, 0:1], scale=1.0 / 64),
                 reads=["st2", "gneps"], writes=["st2"])
            S.op("dve", lambda e: e.reciprocal(out=st2[:, :], in_=st2[:, :]), reads=["st2"], writes=["st2"])
            S.op("dve", lambda e: e.tensor_tensor(out=Y3, in0=Y3, in1=bc64(st2), op=ALU.mult), reads=["G1", "st2"], writes=["G1"])
            S.op("dve", lambda e: e.tensor_tensor(out=g3(Yall), in0=g3(Yall), in1=bc16(gng), op=ALU.mult), reads=["G1", "gng"], writes=["G1"])
            S.op("dve", lambda e: e.tensor_tensor(out=g3(Yall), in0=g3(Yall), in1=bc16(gnb), op=ALU.add), reads=["G1", "gnb"], writes=["G1"])
            S.op("dve", lambda e: e.tensor_tensor(out=sq3, in0=g32(G[0]), in1=bc64(bon), op=ALU.mult), reads=["G0", "bon"], writes=["G3"])
            S.op("dve", lambda e: e.tensor_tensor(out=Yall[:, :], in0=Yall[:, :], in1=G[3][:, :], op=ALU.add), reads=["G1", "G3"], writes=["G1"])
            Ytok = kt[:, :].rearrange("p (c f) -> p c f", c=16)
            yTsb = bt
            S.op("dve", lambda e: e.tensor_tensor(out=Ytok, in0=g3(Yall), in1=g3(G[2]), op=ALU.mult), reads=["G1", "G2"], writes=["kt"])
            for hf in range(2):
                pi, ps = psr()
                psv = ps[:, :].bitcast(BF16)

                def tr(e, psv=psv, hf=hf):
                    ins = None
                    for cc in range(8):
                        c = hf * 8 + cc
                        ins = e.transpose(out=psv[:, cc * 128:(cc + 1) * 128], in_=Ytok[:, c, :], identity=identb[:, :])
                    return ins
                S.op("pe", tr, reads=["kt", "identb"], writes=[("ps", pi)])
                S.op("act", lambda e, psv=psv, hf=hf: e.copy(out=yTsb[:, hf * 1024:(hf + 1) * 1024], in_=psv[:, :]),
                     reads=[("ps", pi)], pwrites=["bt"])
            S.dma("sp", lambda e: e.dma_start(out=R.yT[jc, :], in_=yTsb[:, :]), reads=["bt"], pwrites=[("dram", R.yT.name)])

        for j_ in range(C.rk_pairs):
            pair(j_)


def phase_sb(C):
    S, nc, I, R = C.S, C.nc, C.I, C.R
    scale = 1.0 / float(np.sqrt(128.0))
    with ExitStack() as es:
        def sb(name, shape, dt=F32):
            return es.enter_context(nc.sbuf_tensor(name, list(shape), dt))
        qT = [sb(f"sbq{i}", [128, T], BF16) for i in range(2)]
        kT = [sb(f"sbk{i}", [128, T], BF16) for i in range(2)]
        kTs = [sb(f"sbks{i}", [128, T], BF16) for i in range(2)]
        vh = [sb(f"sbv{i}", [128, 16, 128], BF16) for i in range(2)]
        gs = [sb(f"sbg{i}", [128, T]) for i in range(2)]
        mtmp = sb("sb_mtmp", [128, 128])
        Lst = sb("sb_L", [128, 128], BF16)
        onb = sb("sb_ones", [128, 128], BF16)
        mbig = sb("sb_mbig", [128, 896])
        onec = sb("sb_onec", [128, 1])
        e_sb = [sb(f"sb_e{i}", [128, 512]) for i in range(2)]
        sp_sb = [sb(f"sb_sp{i}", [128, 512]) for i in range(3)]
        tmp_sb = [sb(f"sb_tmp{i}", [128, 512]) for i in range(2)]
        arg_sb = [sb(f"sb_arg{i}", [128, 512]) for i in range(2)]
        spb = [sb(f"sb_spb{i}", [128, 512], BF16) for i in range(3)]
        att = [sb(f"sb_att{i}", [128, 512], BF16) for i in range(2)]
        acc = [sb(f"sb_acc{i}", [128, 512]) for i in range(2)]
        accb = [sb(f"sb_accb{i}", [128, 512], BF16) for i in range(2)]
        ost = [sb(f"sb_ost{i}", [128, 512], BF16) for i in range(2)]

        S.dma("sp", lambda e: e.dma_start(out=mtmp[:, :], in_=I.consts[:, M0:M0 + 128]), writes=["sb_mtmp"])
        S.dma("sp", lambda e: e.dma_start(out=mbig[:, :], in_=I.consts[:, MBIG:MBIG + 896]), writes=["sb_mbig"])
        S.op("dve", lambda e: e.tensor_scalar(out=Lst[:, :], in0=mtmp[:, :], scalar1=-1.0, scalar2=None, op0=ALU.mult), reads=["sb_mtmp"], writes=["sb_L"])
        S.op("pool", lambda e: e.memset(onb[:, :], -1.0), writes=["sb_ones"])
        S.op("pool", lambda e: e.memset(onec[:, :], 1.0), writes=["sb_onec"])
        vv = R.vstok.rearrange("(kb p) c -> p kb c", p=128)
        PZ, PL, PO = (0, 1, 4), (2, 3), (6, 7)
        cnt = {"item": 0, "qcg": 0, "accb": 0}

        def f_z(it):
            kb, qc, hi, bz = it["kb"], it["qc"], it["hi"], it["bz"]
            pz = PZ[bz]
            S.op("pe", lambda e: e.matmul(C.psum[pz][:, :], lhsT=kT[hi][:, kb * 128:(kb + 1) * 128],
                                          rhs=qT[hi][:, qc * 512:(qc + 1) * 512], start=True, stop=True),
                 reads=[("sbq", hi), ("sbk", hi)], writes=[("ps", pz)])

        def f_sp(it):
            kb, qc, bz, be = it["kb"], it["qc"], it["bz"], it["be"]
            pz = PZ[bz]
            S.op("act", lambda e: e.activation(out=e_sb[be][:, :], in_=C.psum[pz][:, :], func=AF.Exp, scale=scale),
                 reads=[("ps", pz)], writes=[("sb_e", be)])
            S.op("act", lambda e: e.activation(out=sp_sb[bz][:, :], in_=e_sb[be][:, :], func=AF.Ln, bias=onec[:, 0:1], scale=1.0),
                 reads=[("sb_e", be), "sb_onec"], writes=[("sb_sp", bz)])
            off = kb - 4 * qc
            if off >= 0:
                m0 = 384 - off * 128
                S.op("dve", lambda e: e.tensor_tensor(out=sp_sb[bz][:, :], in0=sp_sb[bz][:, :], in1=mbig[:, m0:m0 + 512], op=ALU.mult),
                     reads=[("sb_sp", bz), "sb_mbig"], writes=[("sb_sp", bz)])
            if it["be"] == 0:
                S.op("act", lambda e: e.copy(out=spb[bz][:, :], in_=sp_sb[bz][:, :]),
                     reads=[("sb_sp", bz)], writes=[("sb_spb", bz)])
            else:
                S.op("dve", lambda e: e.tensor_copy(out=spb[bz][:, :], in_=sp_sb[bz][:, :]),
                     reads=[("sb_sp", bz)], writes=[("sb_spb", bz)])

        def f_later(it):
            kb, qc, first, bz, bl, ai = it["kb"], it["qc"], it["first"], it["bz"], it["bl"], it["ai"]
            pz, pl = PZ[bz], PL[bl]
            hi = it["hi"]
            if kb == first:
                def mm(e):
                    e.matmul(C.psum[pl][:, :], lhsT=Lst[:, :], rhs=spb[bz][:, :], start=True, stop=False)
                    return e.matmul(C.psum[pl][:, :], lhsT=kTs[hi][:, kb * 128:(kb + 1) * 128], rhs=qT[hi][:, qc * 512:(qc + 1) * 512],
                                    start=False, stop=True)
                S.op("pe", mm, reads=["sb_L", ("sb_spb", bz), ("sbks", hi), ("sbq", hi)], writes=[("ps", pl)])
            else:
                abi = it["abi"]

                def mm(e):
                    e.matmul(C.psum[pl][:, :], lhsT=Lst[:, :], rhs=spb[bz][:, :], start=True, stop=False)
                    e.matmul(C.psum[pl][:, :], lhsT=onb[:, :], rhs=accb[abi][:, :], start=False, stop=False)
                    return e.matmul(C.psum[pl][:, :], lhsT=kTs[hi][:, kb * 128:(kb + 1) * 128], rhs=qT[hi][:, qc * 512:(qc + 1) * 512],
                                    start=False, stop=True)
                S.op("pe", mm, reads=["sb_L", "sb_ones", ("sb_spb", bz), ("sb_accb", abi), ("sbks", hi), ("sbq", hi)], writes=[("ps", pl)])
            S.op("dve", lambda e: e.tensor_tensor(out=arg_sb[bl][:, :], in0=C.psum[pl][:, :], in1=sp_sb[bz][:, :], op=ALU.subtract),
                 reads=[("sb_sp", bz), ("ps", pl)], writes=[("sb_arg", bl)])
            if kb > 0:
                if kb == first:
                    S.op("pool", lambda e: e.tensor_copy(out=acc[ai][:, :], in_=sp_sb[bz][:, :]),
                         reads=[("sb_sp", bz)], writes=[("sb_acc", ai)])
                else:
                    S.op("pool", lambda e: e.tensor_tensor(out=acc[ai][:, :], in0=acc[ai][:, :], in1=sp_sb[bz][:, :], op=ALU.add),
                         reads=[("sb_sp", bz), ("sb_acc", ai)], writes=[("sb_acc", ai)])
                nb = it["nabi"]
                S.op("dve", lambda e: e.tensor_copy(out=accb[nb][:, :], in_=acc[ai][:, :]),
                     reads=[("sb_acc", ai)], writes=[("sb_accb", nb)])

        def f_att(it):
            h, kb, qc, first, hi, bl, oi = it["h"], it["kb"], it["qc"], it["first"], it["hi"], it["bl"], it["oi"]
            po = PO[oi]
            S.op("act", lambda e: e.activation(out=att[bl][:, :], in_=arg_sb[bl][:, :], func=AF.Exp),
                 reads=[("sb_arg", bl)], writes=[("sb_att", bl)])
            off = kb - 4 * qc
            if off >= 0:
                m0 = 384 - off * 128
                S.op("pool", lambda e: e.tensor_tensor(out=att[bl][:, :], in0=att[bl][:, :], in1=mbig[:, m0:m0 + 512], op=ALU.mult),
                     reads=[("sb_att", bl), "sb_mbig"], writes=[("sb_att", bl)])
            S.op("pe", lambda e: e.matmul(C.psum[po][:, :], lhsT=vh[hi][:, kb, :], rhs=att[bl][:, :],
                                          start=(kb == first), stop=(kb == 0)),
                 reads=[("sbv", hi), ("sb_att", bl)], writes=[("ps", po)])
            if kb == 0:
                S.op("dve", lambda e: e.tensor_tensor(out=ost[oi][:, :], in0=C.psum[po][:, :], in1=gs[hi][:, qc * 512:(qc + 1) * 512], op=ALU.mult),
                     reads=[("ps", po), ("sbg", hi)], writes=[("sb_ost", oi)])
                S.dma("sp", lambda e: e.dma_start(out=R.yT[RW + h * 128:RW + (h + 1) * 128, qc * 512:(qc + 1) * 512], in_=ost[oi][:, :]),
                      reads=[("sb_ost", oi)], pwrites=[("dram", R.yT.name)])

        items = []
        n = 0
        for h in range(C.sb_heads):
            hi = h % 2
            for qc in range(4):
                first = 4 * qc + 3
                g = h * 4 + qc
                for kb in range(first, -1, -1):
                    items.append(dict(h=h, hi=hi, qc=qc, kb=kb, first=first, bz=n % 3, be=n % 2, bl=n % 2, ai=g % 2, oi=g % 2,
                                      abi=(n - 1) % 2, nabi=n % 2))
                    n += 1

        def load_head(h):
            hi = h % 2
            S.dma("sp", lambda e: e.dma_start(out=qT[hi][:, :], in_=R.qT[h * 128:(h + 1) * 128, :]),
                  reads=[("dram", R.qT.name)], writes=[("sbq", hi)])
            S.dma("sp", lambda e: e.dma_start(out=kT[hi][:, :], in_=R.ksT[h * 128:(h + 1) * 128, :]),
                  reads=[("dram", R.ksT.name)], writes=[("sbk", hi)])
            S.dma("sp", lambda e: e.dma_start(out=vh[hi][:, :, :], in_=vv[:, :, h * 128:(h + 1) * 128]),
                  reads=[("dram", R.vstok.name)], writes=[("sbv", hi)])
            S.dma("sp", lambda e: e.dma_start(out=gs[hi][:, :], in_=R.gsT[h * 128:(h + 1) * 128, :]),
                  reads=[("dram", R.gsT.name)], writes=[("sbg", hi)])
            S.op("act", lambda e: e.activation(out=gs[hi][:, :], in_=gs[hi][:, :], func=AF.Silu),
                 reads=[("sbg", hi)], writes=[("sbg", hi)])
            S.op("act", lambda e: e.activation(out=kTs[hi][:, :], in_=kT[hi][:, :], func=AF.Copy, scale=scale),
                 reads=[("sbk", hi)], writes=[("sbks", hi)])

        NI = len(items)
        loaded = set()
        for s in range(NI + 3):
            if s < NI:
                hh = items[s]["h"]
                for h2 in (hh, hh + 1):
                    if h2 < C.sb_heads and h2 not in loaded and (h2 == hh or items[s]["qc"] >= 2):
                        load_head(h2)
                        loaded.add(h2)
                f_z(items[s])
            if 0 <= s - 1 < NI:
                f_sp(items[s - 1])
            if 0 <= s - 2 < NI:
                f_later(items[s - 2])
            if 0 <= s - 3 < NI:
                f_att(items[s - 3])
            yield_point = None


def phase_sgu(C):
    S, nc, I, R = C.S, C.nc, C.I, C.R
    with ExitStack() as es:
        def sb(name, shape, dt=F32):
            return es.enter_context(nc.sbuf_tensor(name, list(shape), dt))
        lng = sb("lng", [128, D])
        lnb = sb("lnb", [128, D])
        wsf = sb("wsf", [128, 16, 128])
        wsb = sb("wsb", [128, 16, 128], BF16)
        msk = sb("sg_msk", [128, 128])
        bsb = sb("bsb", [128, 16, 128])
        vb = [sb(f"vb{i}", [128, D]) for i in range(2)]
        vnb = [sb(f"vnb{i}", [128, D], BF16) for i in range(2)]
        stats = sb("sg_stats", [128, 8, 6])
        mv = sb("sg_mv", [128, 2])
        rs = sb("sg_rs", [128, 1])
        epsc = sb("sg_eps", [128, 1])
        ub = [sb(f"ub{i}", [128, 4, 128]) for i in range(8)]
        gb = [sb(f"gb{i}", [128, 4, 128]) for i in range(8)]
        mb = [sb(f"mb{i}", [128, 4, 128]) for i in range(2)]
        yb = [sb(f"yb{i}", [128, 4, 128], BF16) for i in range(2)]

        S.dma("sp", lambda e: e.dma_start(out=lng[:, :], in_=I.o_ln[0:1, :].partition_broadcast(128)), writes=["lng"])
        S.dma("sp", lambda e: e.dma_start(out=lnb[:, :], in_=I.o_ln[1:2, :].partition_broadcast(128)), writes=["lnb"])
        S.dma("sp", lambda e: e.dma_start(out=bsb[:, :, :].rearrange("p g t -> p (g t)"), in_=I.o_bs[0:1, :].partition_broadcast(128)), writes=["bsb"])
        S.dma("sp", lambda e: e.dma_start(out=wsf[:, :, :], in_=I.o_wsT[:, :, :]), writes=["wsf"])
        S.dma("sp", lambda e: e.dma_start(out=msk[:, :], in_=I.consts[:, M2:M2 + 128]), writes=["sg_msk"])
        S.op("pool", lambda e: e.memset(epsc[:, :], LN_EPS), writes=["sg_eps"])
        for g in range(16):
            S.op("dve", lambda e, g=g: e.tensor_tensor(out=wsb[:, g, :], in0=wsf[:, g, :], in1=msk[:, :], op=ALU.mult),
                 reads=["wsf", "sg_msk"], pwrites=["wsb"])

        uv = R.uT.rearrange("(fc p) t -> p fc t", p=128)
        gv = R.ggT.rearrange("(fc p) t -> p fc t", p=128)
        yv = R.y2T.rearrange("(fc p) t -> p fc t", p=128)
        rs_ = C.rs
        for c in range(16):
            tok = slice(c * 128, (c + 1) * 128)
            vi = c % 2
            v_, vn_ = vb[vi], vnb[vi]
            for q in range(4):
                S.dma("sp", lambda e, v_=v_, q=q, tok=tok: e.dma_start(out=v_[:, q * 1024:(q + 1) * 1024], in_=R.vtk[tok, q * 1024:(q + 1) * 1024]),
                      reads=[("dram", R.vtk.name)], pwrites=[("vb", vi)])
            S.op("act", lambda e, v_=v_: e.activation(out=v_[:, :], in_=v_[:, :], func=AF.Gelu),
                 reads=[("vb", vi)], writes=[("vb", vi)])
            for q in range(8):
                S.op("dve", lambda e, v_=v_, q=q: e.bn_stats(out=stats[:, q, :], in_=v_[:, q * 512:(q + 1) * 512]),
                     reads=[("vb", vi)], pwrites=["sg_stats"])
            S.op("dve", lambda e: e.bn_aggr(out=mv[:, :], in_=stats[:, :, :].rearrange("p a b -> p (a b)")),
                 reads=["sg_stats"], writes=["sg_mv"])
            S.op("act", lambda e: e.activation(out=rs[:, :], in_=mv[:, 1:2], func=AF.Sqrt, bias=epsc[:, 0:1], scale=1.0),
                 reads=["sg_mv", "sg_eps"], writes=["sg_rs"])
            S.op("dve", lambda e: e.reciprocal(out=rs[:, :], in_=rs[:, :]), reads=["sg_rs"], writes=["sg_rs"])
            S.op("dve", lambda e, v_=v_: e.tensor_scalar(out=v_[:, :], in0=v_[:, :], scalar1=mv[:, 0:1], scalar2=rs[:, 0:1],
                                                        op0=ALU.subtract, op1=ALU.mult),
                 reads=[("vb", vi), "sg_mv", "sg_rs"], writes=[("vb", vi)])
            S.op("pool", lambda e, v_=v_: e.tensor_tensor(out=v_[:, :], in0=v_[:, :], in1=lng[:, :], op=ALU.mult),
                 reads=[("vb", vi), "lng"], writes=[("vb", vi)])
            S.op("dve", lambda e, v_=v_, vn_=vn_: e.tensor_tensor(out=vn_[:, :], in0=v_[:, :], in1=lnb[:, :], op=ALU.add),
                 reads=[("vb", vi), "lnb"], writes=[("vnb", vi)])
            for f4 in range(8):
                u_, g_ = ub[f4], gb[f4]
                S.dma("sp", lambda e, u_=u_, f4=f4, tok=tok: e.dma_start(out=u_[:, :, :], in_=uv[:, f4 * 4:(f4 + 1) * 4, tok]),
                      reads=[("dram", R.uT.name)], writes=[("sgu", f4)])
                S.dma("sp", lambda e, g_=g_, f4=f4, tok=tok: e.dma_start(out=g_[:, :, :], in_=gv[:, f4 * 4:(f4 + 1) * 4, tok]),
                      reads=[("dram", R.ggT.name)], writes=[("sgg", f4)])
                S.op("act", lambda e, u_=u_: e.activation(out=u_[:, :, :], in_=u_[:, :, :], func=AF.Gelu),
                     reads=[("sgu", f4)], writes=[("sgu", f4)])
            for f4 in range(8):
                g_ = gb[f4]
                S.op("act", lambda e, g_=g_: e.activation(out=g_[:, :, :], in_=g_[:, :, :], func=AF.Silu),
                     reads=[("sgg", f4)], writes=[("sgg", f4)])
            for f4 in range(8):
                u_, g_ = ub[f4], gb[f4]
                bi = f4 % 2
                m_, y_ = mb[bi], yb[bi]
                pi = _rot(C.psum, rs_, "psm")
                ps = C.psum[pi]

                def mm(e, ps=ps, vn_=vn_, f4=f4):
                    ins = None
                    for k in range(4):
                        fc = f4 * 4 + k
                        ins = e.matmul(ps[:, k * 128:(k + 1) * 128], lhsT=vn_[:, fc * 128:(fc + 1) * 128],
                                       rhs=wsb[:, fc // 2, :], start=True, stop=True)
                    return ins
                S.op("pe", mm, reads=[("vnb", vi), "wsb"], writes=[("ps", pi)])
                g0 = f4 * 2
                S.op("dve", lambda e, ps=ps, m_=m_, g0=g0: e.tensor_tensor(
                    out=m_[:, :, :].rearrange("p (a b) t -> p a b t", a=2), in0=ps[:, :].rearrange("p (a b t) -> p a b t", a=2, b=2),
                    in1=bsb[:, g0:g0 + 2, :].unsqueeze(2).broadcast_to([128, 2, 2, 128]), op=ALU.add),
                    reads=[("ps", pi), "bsb"], writes=[("sgm", bi)])
                S.op("dve", lambda e, m_=m_, u_=u_: e.tensor_tensor(out=m_[:, :, :], in0=m_[:, :, :], in1=u_[:, :, :], op=ALU.mult),
                     reads=[("sgm", bi), ("sgu", f4)], writes=[("sgm", bi)])
                S.op("dve", lambda e, m_=m_, g_=g_, y_=y_: e.tensor_tensor(out=y_[:, :, :], in0=m_[:, :, :], in1=g_[:, :, :], op=ALU.mult),
                     reads=[("sgm", bi), ("sgg", f4)], writes=[("sgy", bi)])
                S.dma("sp", lambda e, y_=y_, f4=f4, tok=tok: e.dma_start(out=yv[:, f4 * 4:(f4 + 1) * 4, tok], in_=y_[:, :, :]),
                      reads=[("sgy", bi)], pwrites=[("dram", R.y2T.name)])


def build(phases=("p1", "p2", "p3", "p4", "p5", "p6", "p7"), debug_out=()):
    nc = bass.Bass("TRN2", target_bir_lowering=False)
    C = Ctx()
    C.nc = nc
    C.rs = {}

    def din(name, shape, dt=F32):
        return nc.dram_tensor(name, list(shape), dt, kind="ExternalInput").ap()

    def dscr(name, shape, dt=F32):
        kind = "ExternalOutput" if name in debug_out else "Internal"
        return nc.dram_tensor(name, list(shape), dt, kind=kind).ap()

    I = Ctx()
    C.I = I
    I.xT = din("xT", [D, T])
    I.ng = din("ng", [3, 128, 32])
    I.e_w_in = din("e_w_in", [D, EC])
    I.e_w_out = din("e_w_out", [D, D])
    I.o_w_in = din("o_w_in", [D, OC])
    I.o_w_out = din("o_w_out", [D, D])
    I.e_vec = din("e_vec", [10, 128, 16])
    I.e_mu_v = din("e_mu_v", [16, RW])
    I.e_gn = din("e_gn", [2, 16, RW])
    I.e_wdu = din("e_wdu", [128, RW])
    I.e_aup = din("e_aup", [128, RW])
    I.o_ln = din("o_ln", [2, D])
    I.o_wsT = din("o_wsT", [128, 16, 128])
    I.o_bs = din("o_bs", [1, 16 * 128])
    I.consts = din("consts", [128, NCONST])
    I.outT = nc.dram_tensor("outT", [D, T], F32, kind="ExternalOutput").ap()

    R = Ctx()
    C.R = R
    R.rT = dscr("s_rT", [RW, T])
    R.kT = dscr("s_kT", [RW, T])
    R.vtok = dscr("s_vtok", [T + 1, RW])
    R.loT = dscr("s_loT", [256, T])
    R.gtok = dscr("s_gtok", [T, RW])
    R.qT = dscr("s_qT", [SBW, T], BF16)
    R.ksT = dscr("s_ksT", [SBW, T], BF16)
    R.vstok = dscr("s_vstok", [T, SBW], BF16)
    R.gsT = dscr("s_gsT", [SBW, T])
    R.yT = dscr("s_yT", [D, T], BF16)
    R.x1T = dscr("s_x1T", [D, T])
    R.uT = dscr("s_uT", [D, T])
    R.vtk = dscr("s_vtk", [T, D])
    R.ggT = dscr("s_ggT", [D, T])
    R.y2T = dscr("s_y2T", [D, T], BF16)
    R.x2T = dscr("s_x2T", [D, T])

    with ExitStack() as es:
        S = Sched(nc, es)
        C.S = S

        def sb(name, shape, dt=F32):
            return es.enter_context(nc.sbuf_tensor(name, list(shape), dt))

        C.psum = [es.enter_context(nc.psum_tensor(f"ps{i}", [128, 512], F32)) for i in range(8)]
        C.ones_f = sb("ones_f", [128, 128])
        C.eps_rms = sb("eps_rms", [128, 1])
        C.ngs = sb("ngs", [128, 3, 32])
        S.op("pool", lambda e: e.memset(C.ones_f[:, :], 1.0), writes=["ones_f"])
        S.op("pool", lambda e: e.memset(C.eps_rms[:, :], RMS_EPS), writes=["eps_rms"])
        S.dma("sp", lambda e: e.dma_start(out=C.ngs[:, :, :], in_=I.ng.rearrange("a p c -> p a c")), writes=["ngs"])
        C.rstd = sb("rstd", [128, 512])

        from contextlib import contextmanager

        @contextmanager
        def proj_bufs(tag):
            with ExitStack() as es2:
                def sb2(name, shape, dt=F32):
                    return es2.enter_context(nc.sbuf_tensor(name + tag, list(shape), dt))
                C.hT = sb2("hT", [128, 32, 1024], BF16)
                C.wbuf = [sb2(f"wbuf{i}", [128, 32, 512], BF16) for i in range(2)]
                C.xst = [sb2(f"xst{i}", [128, 4, 512]) for i in range(3)]
                C.sqb = [sb2(f"sqb{i}", [128, 512]) for i in range(2)]
                C.stF = [sb2(f"stF{i}", [128, 1024]) for i in range(3)]
                C.stB = [sb2(f"stB{i}", [128, 1024], BF16) for i in range(3)]
                yield
                S.barrier()

        x1src = I.xT if C.skip_layer0 else R.x1T
        if "p1" in phases:
            with proj_bufs("a"):
                groups = [
                    dict(c0=0, n=RW, mode="F", dst=R.rT, dt=F32),
                    dict(c0=RW, n=RW, mode="F", dst=R.kT, dt=F32),
                    dict(c0=2 * RW, n=RW, mode="T", dst=R.vtok, dt=F32, row_off=1),
                    dict(c0=3 * RW, n=256, mode="F", dst=R.loT, dt=F32),
                    dict(c0=3 * RW + 256, n=RW, mode="T", dst=R.gtok, dt=F32),
                    dict(c0=4 * RW + 256, n=SBW, mode="F", dst=R.qT, dt=BF16),
                    dict(c0=4 * RW + 256 + SBW, n=SBW, mode="F", dst=R.ksT, dt=BF16),
                    dict(c0=4 * RW + 256 + 2 * SBW, n=SBW, mode="T", dst=R.vstok, dt=BF16),
                    dict(c0=4 * RW + 256 + 3 * SBW, n=SBW, mode="F", dst=R.gsT, dt=F32),
                ]
                if C.p1_groups is not None:
                    groups = [groups[i] for i in C.p1_groups]
                for half in range(2):
                    t0 = half * 1024
                    norm_tokens(C, I.xT, C.ngs[:, 0, :], t0, 1024, hT=C.hT, hkey="hT")
                    project(C, C.hT, "hT", I.e_w_in, groups, t0)
        if "p2" in phases:
            phase_rwkv(C)
            S.barrier()
        if "p3" in phases:
            phase_sb(C)
            S.barrier()
        if "p4" in phases or "p5" in phases:
            with proj_bufs("b"):
                if "p4" in phases:
                    for half in range(2):
                        t0 = half * 1024
                        load_actT(C, R.yT, t0)
                        project(C, C.hT, "hT", I.e_w_out,
                                [dict(c0=0, n=D, mode="F", dst=R.x1T, dt=F32, resid=I.xT)], t0)
                if "p5" in phases:
                    groups = [
                        dict(c0=0, n=D, mode="F", dst=R.uT, dt=F32),
                        dict(c0=D, n=D, mode="T", dst=R.vtk, dt=F32),
                        dict(c0=2 * D, n=D, mode="F", dst=R.ggT, dt=F32),
                    ]
                    for half in range(2):
                        t0 = half * 1024
                        norm_tokens(C, x1src, C.ngs[:, 1, :], t0, 1024, hT=C.hT, hkey="hT")
                        project(C, C.hT, "hT", I.o_w_in, groups, t0)
        if "p6" in phases:
            phase_sgu(C)
            S.barrier()
        if "p7" in phases:
            with proj_bufs("c"):
                for half in range(2):
                    t0 = half * 1024
                    load_actT(C, R.y2T, t0)
                    project(C, C.hT, "hT", I.o_w_out,
                            [dict(c0=0, n=D, mode="F", dst=R.x2T, dt=F32, resid=x1src)], t0)
                norm_tokens(C, R.x2T, C.ngs[:, 2, :], 0, T, dstT=I.outT)

        S.final_wait("sp", S.all_tokens())
        S.emit()
    return nc


Ctx.p1_groups = None
Ctx.rk_pairs = 16
Ctx.rk_stop = 0
Ctx.sb_heads = 16
Ctx.skip_layer0 = False


def _consts():
    p = np.arange(128)[:, None]
    f = np.arange(128)[None, :]
    c = np.zeros((128, NCONST), np.float32)
    c[:, M0:M0 + 128] = (p > f)
    c[:, M1:M1 + 128] = (f > p)
    c[:, M2:M2 + 128] = (f >= p)
    c[:, M3:M3 + 128] = (p >= f)
    c[:, IDO:IDO + 128] = (p == f)
    c[:, BDO:BDO + 128] = ((p // 64) == (f // 64))
    c[:, INDO:INDO + 2] = ((p // 64) == np.arange(2)[None, :])
    fb = np.arange(896)[None, :]
    c[:, MBIG:MBIG + 896] = ((fb - p) > 384)
    t = np.arange(2048)[None, :]
    c[:, RMASK:RMASK + 2048] = np.broadcast_to((t % 128) != 0, (128, 2048))
    return c


def make_shared(inp):
    f = lambda a: np.asarray(a, dtype=np.float32)
    sh = {}
    pc = lambda v: np.ascontiguousarray(f(v).reshape(-1, 128).T)
    ng = f(inp["norm_g"])
    sh["ng"] = np.stack([pc(ng[0]), pc(ng[1]), pc(inp["final_norm_g"])])
    sh["e_w_in"] = f(inp["e_w_in"])[0]
    sh["e_w_out"] = f(inp["e_w_out"])[0]
    sh["o_w_in"] = f(inp["o_w_in"])[0]
    sh["o_w_out"] = f(inp["o_w_out"])[0]
    mu = f(inp["e_shift_mu"])[0]
    ev = np.zeros((10, 128, 16), np.float32)
    ev[0] = pc(inp["e_w0"][0]); ev[1] = pc(inp["e_a0"][0]); ev[2] = pc(inp["e_k_k"][0])
    ev[3] = pc(inp["e_k_a"][0]); ev[4] = pc(inp["e_r_k"][0])
    ev[5] = pc(mu[0:2048]); ev[6] = pc(mu[2048:4096])
    ev[7, :, 0] = mu[6144:6272]; ev[8, :, 0] = mu[6272:6400]
    sh["e_vec"] = ev
    rep = lambda v: np.ascontiguousarray(np.tile(f(v).reshape(16, 1, 128), (1, 16, 1)).reshape(16, 2048))
    sh["e_mu_v"] = rep(mu[4096:6144])
    sh["e_gn"] = np.stack([rep(inp["e_gn_g"][0]), rep(inp["e_gn_b"][0])])
    sh["e_wdu"] = f(inp["e_w_decay_up"])[0]
    sh["e_aup"] = f(inp["e_a_up"])[0]
    sh["o_ln"] = np.stack([f(inp["o_ln_g"])[0], f(inp["o_ln_b"])[0]])
    sh["o_wsT"] = np.ascontiguousarray(f(inp["o_w_s"])[0].transpose(2, 0, 1))
    sh["o_bs"] = np.ascontiguousarray(f(inp["o_b_s"])[0].reshape(1, -1))
    sh["consts"] = _consts()
    return sh


_NC_CACHE = {}


def kernel(**inputs):
    x = np.asarray(inputs["x"], dtype=np.float32)
    sh = make_shared(inputs)
    if "nc" not in _NC_CACHE:
        _NC_CACHE["nc"] = build()
    nc = _NC_CACHE["nc"]
    in_maps = []
    for b in range(NB):
        m = dict(sh)
        m["xT"] = np.ascontiguousarray(x[b].T)
        in_maps.append(m)
    res = run_bass_kernel_spmd(nc, in_maps, core_ids=list(range(NB)))
    out = np.stack([np.ascontiguousarray(np.asarray(r["outT"]).T) for r in res.results])
    return out.astype(np.float32)
```
